# Optimizing a Trainium2 kernel written in Bass

```python
import jax, jax.numpy as jnp
from jax import lax
import numpy as np

D_MODEL = 1024
BATCH = 8
SEQ = 4096
DEPTH = 2

N_MEM = 256
HEAD_DIM = 64
N_Q_HEADS = 8
N_KV_HEADS = 2
ATTN_WIDTH = N_Q_HEADS * HEAD_DIM
KV_WIDTH = N_KV_HEADS * HEAD_DIM
CONV_CH = D_MODEL - ATTN_WIDTH
MIX_WIDTH = ATTN_WIDTH + CONV_CH
IN_COLS = ATTN_WIDTH + 2 * KV_WIDTH + 2 * CONV_CH
CONV_K = 31
WINDOW = 128
BLOCK = 128
N_X_HEADS = 4
X_HEAD_DIM = D_MODEL // N_X_HEADS
D_FF = (((8 * D_MODEL + 2) // 3 + 255) // 256) * 256
EPS = 1e-6
NEG = -1e30

kernel_name = "hybrid_swa_sink_conformer_memxattn"


def rmsnorm(x, g):
    xf = x.astype(jnp.float32)
    y = xf * lax.rsqrt(jnp.mean(xf * xf, axis=-1, keepdims=True) + EPS)
    return (y * g.astype(jnp.float32)).astype(x.dtype)


def layernorm(x, g, b):
    xf = x.astype(jnp.float32)
    mu = jnp.mean(xf, axis=-1, keepdims=True)
    xc = xf - mu
    y = xc * lax.rsqrt(jnp.mean(xc * xc, axis=-1, keepdims=True) + EPS)
    return (y * g.astype(jnp.float32) + b.astype(jnp.float32)).astype(x.dtype)


def alibi_slopes(n):
    return 2.0 ** (-8.0 * (jnp.arange(n, dtype=jnp.float32) + 1.0) / n)


def sliding_window_attention(q, k, v, sinks):
    B, S, H, hd = q.shape
    G = H // N_KV_HEADS
    nb = S // BLOCK
    qb = q.reshape(B, nb, BLOCK, N_KV_HEADS, G, hd)
    kb = k.reshape(B, nb, BLOCK, N_KV_HEADS, hd)
    vb = v.reshape(B, nb, BLOCK, N_KV_HEADS, hd)
    pad = ((0, 0), (1, 0), (0, 0), (0, 0), (0, 0))
    kk = jnp.concatenate([jnp.pad(kb[:, :-1], pad), kb], axis=2)
    vv = jnp.concatenate([jnp.pad(vb[:, :-1], pad), vb], axis=2)
    scores = jnp.einsum('bnqkgd,bnskd->bnkgqs', qb, kk).astype(jnp.float32) * (hd ** -0.5)
    dist = (jnp.arange(BLOCK)[:, None] + BLOCK - jnp.arange(2 * BLOCK)[None, :])
    valid = (dist >= 0) & (dist < WINDOW)
    exists = (jnp.arange(nb)[:, None, None] > 0) | (jnp.arange(2 * BLOCK)[None, None, :] >= BLOCK)
    valid = valid[None] & exists
    slopes = alibi_slopes(H).reshape(N_KV_HEADS, G)
    bias = -slopes[:, :, None, None] * dist.astype(jnp.float32)
    scores = jnp.where(valid[None, :, None, None], scores + bias, NEG)
    sink = sinks.astype(jnp.float32).reshape(N_KV_HEADS, G)[None, None, :, :, None, None]
    m = jnp.maximum(jnp.max(scores, axis=-1, keepdims=True), sink)
    p = jnp.exp(scores - m)
    probs = p / (jnp.sum(p, axis=-1, keepdims=True) + jnp.exp(sink - m))
    out = jnp.einsum('bnkgqs,bnskd->bnqkgd', probs.astype(v.dtype), vv)
    return out.reshape(B, S, H * hd)


def conformer_conv(c, conv_w, conv_b, ln_g, ln_b):
    val, gate = jnp.split(c, 2, axis=-1)
    g = val * jax.nn.sigmoid(gate)
    y = lax.conv_general_dilated(
        g, conv_w[:, None, :].astype(g.dtype), window_strides=(1,),
        padding=[(CONV_K - 1, 0)], dimension_numbers=('NWC', 'WIO', 'NWC'),
        feature_group_count=CONV_CH) + conv_b
    y = layernorm(y, ln_g, ln_b)
    return jax.nn.silu(y)


def memory_cross_attention(h, mem_n, wq, wkv, qg, kg, wo):
    B, S, _ = h.shape
    q = jnp.einsum('bsd,de->bse', h, wq).reshape(B, S, N_X_HEADS, X_HEAD_DIM)
    kv = jnp.einsum('bmd,de->bme', mem_n, wkv)
    k, v = jnp.split(kv, 2, axis=-1)
    k = k.reshape(B, -1, N_X_HEADS, X_HEAD_DIM)
    v = v.reshape(B, -1, N_X_HEADS, X_HEAD_DIM)
    q = rmsnorm(q, qg)
    k = rmsnorm(k, kg)
    s = jnp.einsum('bshd,bmhd->bhsm', q, k).astype(jnp.float32) * (X_HEAD_DIM ** -0.5)
    p = jax.nn.softmax(s, axis=-1).astype(v.dtype)
    o = jnp.einsum('bhsm,bmhd->bshd', p, v).reshape(B, S, D_MODEL)
    return jnp.einsum('bse,ed->bsd', o, wo)


def swiglu(h, w_gate_up, w_down):
    gu = jnp.einsum('bsd,df->bsf', h, w_gate_up)
    g, u = jnp.split(gu, 2, axis=-1)
    return jnp.einsum('bsf,fd->bsd', jax.nn.silu(g) * u, w_down)


def setup_inputs(seed: int = 0) -> dict:
    key = jax.random.key(seed)
    ks = jax.random.split(key, 24)
    f = jnp.float32
    nrm = lambda k, shape, scale: (jax.random.normal(k, shape, f) * scale).astype(f)
    gain = lambda k, shape: (1.0 + 0.02 * jax.random.normal(k, shape, f)).astype(f)
    L = DEPTH
    return {
        "x": jax.random.normal(ks[0], (BATCH, SEQ, D_MODEL), f),
        "mem": jax.random.normal(ks[1], (BATCH, N_MEM, D_MODEL), f),
        "norm_mix_g": gain(ks[2], (L, D_MODEL)),
        "w_in": nrm(ks[3], (L, D_MODEL, IN_COLS), D_MODEL ** -0.5),
        "q_norm_g": gain(ks[4], (L, HEAD_DIM)),
        "k_norm_g": gain(ks[5], (L, HEAD_DIM)),
        "sinks": nrm(ks[6], (L, N_Q_HEADS), 0.5),
        "conv_w": nrm(ks[7], (L, CONV_K, CONV_CH), CONV_K ** -0.5),
        "conv_b": nrm(ks[8], (L, CONV_CH), 0.01),
        "conv_ln_g": gain(ks[9], (L, CONV_CH)),
        "conv_ln_b": nrm(ks[10], (L, CONV_CH), 0.01),
        "w_out": nrm(ks[11], (L, MIX_WIDTH, D_MODEL), MIX_WIDTH ** -0.5),
        "norm_x_g": gain(ks[12], (L, D_MODEL)),
        "norm_mem_g": gain(ks[13], (L, D_MODEL)),
        "wq_x": nrm(ks[14], (L, D_MODEL, D_MODEL), D_MODEL ** -0.5),
        "wkv_x": nrm(ks[15], (L, D_MODEL, 2 * D_MODEL), D_MODEL ** -0.5),
        "xq_norm_g": gain(ks[16], (L, X_HEAD_DIM)),
        "xk_norm_g": gain(ks[17], (L, X_HEAD_DIM)),
        "wo_x": nrm(ks[18], (L, D_MODEL, D_MODEL), D_MODEL ** -0.5),
        "norm_ffn_g": gain(ks[19], (L, D_MODEL)),
        "w_gate_up": nrm(ks[20], (L, D_MODEL, 2 * D_FF), D_MODEL ** -0.5),
        "w_down": nrm(ks[21], (L, D_FF, D_MODEL), D_FF ** -0.5),
    }


def reference(x, mem, norm_mix_g, w_in, q_norm_g, k_norm_g, sinks, conv_w, conv_b,
              conv_ln_g, conv_ln_b, w_out, norm_x_g, norm_mem_g, wq_x, wkv_x,
              xq_norm_g, xk_norm_g, wo_x, norm_ffn_g, w_gate_up, w_down):
    B, S, _ = x.shape
    for l in range(DEPTH):
        h = rmsnorm(x, norm_mix_g[l])
        u = jnp.einsum('bsd,dp->bsp', h, w_in[l])
        q = u[..., :ATTN_WIDTH].reshape(B, S, N_Q_HEADS, HEAD_DIM)
        k = u[..., ATTN_WIDTH:ATTN_WIDTH + KV_WIDTH].reshape(B, S, N_KV_HEADS, HEAD_DIM)
        v = u[..., ATTN_WIDTH + KV_WIDTH:ATTN_WIDTH + 2 * KV_WIDTH].reshape(B, S, N_KV_HEADS, HEAD_DIM)
        c = u[..., ATTN_WIDTH + 2 * KV_WIDTH:]
        q = rmsnorm(q, q_norm_g[l])
        k = rmsnorm(k, k_norm_g[l])
        attn = sliding_window_attention(q, k, v, sinks[l])
        conv = conformer_conv(c, conv_w[l], conv_b[l], conv_ln_g[l], conv_ln_b[l])
        mixed = jnp.concatenate([attn, conv], axis=-1)
        x = x + jnp.einsum('bse,ed->bsd', mixed, w_out[l])
        hx = rmsnorm(x, norm_x_g[l])
        mem_n = rmsnorm(mem, norm_mem_g[l])
        x = x + memory_cross_attention(hx, mem_n, wq_x[l], wkv_x[l], xq_norm_g[l], xk_norm_g[l], wo_x[l])
        hf = rmsnorm(x, norm_ffn_g[l])
        x = x + swiglu(hf, w_gate_up[l], w_down[l])
    return x
```

```python
import numpy as np
import ml_dtypes
import concourse.bass as bass
import concourse.mybir as mybir
from concourse.bass_utils import run_bass_kernel_spmd

F32 = mybir.dt.float32
BF16 = mybir.dt.bfloat16
AF = mybir.ActivationFunctionType
ALU = mybir.AluOpType
AX = mybir.AxisListType

SEQ = 4096
D = 1024
T = 512
NT = SEQ // T
DEPTH = 2
EPS = 1e-6
NEGM = -30000.0
HF = 1408
NJ = 11
JS = 6
TD = 8


class Tok:
    __slots__ = ("sem", "key", "val", "clock")

    def __init__(self, sem, val, clock):
        self.sem = sem
        self.key = sem.num
        self.val = val
        self.clock = clock


class Res:
    __slots__ = ("name", "w", "r", "pw", "touch")

    def __init__(self, name=""):
        self.name = name
        self.w = []
        self.r = {}
        self.pw = []
        self.touch = 0

    def new_epoch(self):
        self.pw = list(self.r.values()) + list(self.w) + list(self.pw)
        self.w = []
        self.r = {}


class Eng:
    def __init__(self, name, sem):
        self.name = name
        self.sem = sem
        self.cnt = 0
        self.seen = {}
        self.ops = []


class DmaRing:
    def __init__(self, sems):
        self.sems = sems
        self.tot = [0] * len(sems)
        self.last = [None] * len(sems)
        self.i = 0


class Sched:
    def __init__(self, nc, n_dma_sems=12):
        self.nc = nc
        self.E = {}
        for n in ("pe", "act", "dve", "pool"):
            self.E[n] = Eng(n, nc.alloc_semaphore(name=f"sem_{n}"))
        self.E["sp"] = Eng("sp", None)
        self.rings = {}
        for q in ("sp", "pool"):
            self.rings[q] = DmaRing([nc.alloc_semaphore(name=f"dq_{q}_{i}") for i in range(n_dma_sems)])

    def _wait(self, e, tok):
        if e.seen.get(tok.key, 0) >= tok.val:
            return
        e.ops.append(("wait", tok.sem, tok.val))
        seen = e.seen
        for k, v in tok.clock.items():
            if seen.get(k, 0) < v:
                seen[k] = v
        seen[tok.key] = tok.val

    def _dep1(self, en, e, d):
        if en == "pe" and d.sem is e.sem:
            return
        self._wait(e, d)

    def _deps(self, en, e, reads, writes, pwrites):
        for r in reads:
            for d in r.w:
                self._dep1(en, e, d)
        for w in writes:
            for d in w.w:
                self._dep1(en, e, d)
            for d in w.r.values():
                self._dep1(en, e, d)
            for d in w.pw:
                self._dep1(en, e, d)
        for w in pwrites:
            for d in w.pw:
                self._dep1(en, e, d)
            for d in w.r.values():
                self._dep1(en, e, d)

    def _post(self, tok, reads, writes, pwrites):
        self.opidx = getattr(self, "opidx", 0) + 1
        for r in reads:
            r.touch = self.opidx
        for w in writes:
            w.touch = self.opidx
        for w in pwrites:
            w.touch = self.opidx
        for r in reads:
            old = r.r.get(tok.key)
            if old is None or old.val < tok.val:
                r.r[tok.key] = tok
        for w in writes:
            w.w = [tok]
            w.r = {}
            w.pw = []
        for w in pwrites:
            w.w.append(tok)

    def op(self, en, fn, reads=(), writes=(), pwrites=(), signal=True):
        e = self.E[en]
        self._deps(en, e, reads, writes, pwrites)
        if signal:
            e.cnt += 1
            val = e.cnt
        else:
            val = e.cnt + 1
        e.ops.append(("op", fn, signal))
        tok = Tok(e.sem, val, dict(e.seen))
        self._post(tok, reads, writes, pwrites)
        return tok

    def dma(self, q, out_ap, in_ap, reads=(), writes=(), pwrites=()):
        e = self.E[q]
        self._deps(q, e, reads, writes, pwrites)
        ring = self.rings[q]
        i = ring.i
        ring.i = (i + 1) % len(ring.sems)
        if ring.last[i] is not None:
            self._wait(e, ring.last[i])
        ring.tot[i] += 16
        e.ops.append(("dma", out_ap, in_ap, ring.sems[i]))
        tok = Tok(ring.sems[i], ring.tot[i], dict(e.seen))
        ring.last[i] = tok
        self._post(tok, reads, writes, pwrites)
        return tok

    def barrier(self):
        toks = []
        for n, e in self.E.items():
            if e.sem is not None and e.cnt > 0:
                toks.append(Tok(e.sem, e.cnt, {}))
        for q, ring in self.rings.items():
            for t in ring.last:
                if t is not None:
                    toks.append(t)
        for n, e in self.E.items():
            for t in toks:
                if t.sem is e.sem and n == "pe":
                    continue
                self._wait(e, t)

    def replay(self):
        nc = self.nc
        E = self.E

        def run(e, h):
            for o in e.ops:
                k = o[0]
                if k == "wait":
                    h.wait_ge(o[1], o[2])
                elif k == "op":
                    ins = o[1](h)
                    if o[2]:
                        ins.then_inc(e.sem, 1)
                else:
                    h.dma_start(out=o[1], in_=o[2]).then_inc(o[3], 16)

        with nc.Block() as block:
            @block.tensor
            def _(h):
                run(E["pe"], h)

            @block.scalar
            def _(h):
                run(E["act"], h)

            @block.vector
            def _(h):
                run(E["dve"], h)

            @block.gpsimd
            def _(h):
                run(E["pool"], h)

            @block.sync
            def _(h):
                run(E["sp"], h)


V_NMIX, V_NX, V_NFFN, V_NMEM = 0, 8, 16, 24
V_GQ, V_GK = 32, 33
V_CB, V_LNG, V_LNB = 34, 38, 42
V_GXQ, V_GXK = 46, 48
V_SINKP = 50
V_SINKR = 54
V_CW = 62
V_KEEP = 186
V_GQR, V_GKR = 186, 250
V_GXQR, V_GXKR = 314, 570
NV = 826

C_ID = 0
C_1024 = 128
C_BD64 = 256
C_256 = 384
C_512 = 512
C_ONES = 640
C_OPAD = 768
C_BIAS = 960
NCB = C_BIAS + 8 * 256


def _host_consts():
    cb = np.zeros((128, NCB), np.float32)
    cb[:, C_ID:C_ID + 128] = np.eye(128)
    cb[:, C_1024:C_1024 + 128] = 1.0 / 1024
    bd = np.zeros((128, 128), np.float32)
    bd[:64, :64] = 1.0 / 64
    bd[64:, 64:] = 1.0 / 64
    cb[:, C_BD64:C_BD64 + 128] = bd
    cb[:, C_256:C_256 + 128] = 1.0 / 256
    cb[:, C_512:C_512 + 128] = 1.0 / 512
    cb[:, C_ONES:C_ONES + 128] = 1.0
    cb[:, C_OPAD + 64:C_OPAD + 128] = 1.0
    s = np.arange(128)[:, None].astype(np.float32)
    q = np.arange(128)[None, :].astype(np.float32)
    for kb in range(2):
        for g in range(2):
            for par in range(2):
                ti = (g * 2 + par) * 2 + kb
                tab = np.zeros((128, 2, 128), np.float32)
                for j in range(2):
                    h = 4 * g + 2 * j + par
                    slope = 2.0 ** (-(h + 1))
                    if kb == 0:
                        dist = q + 128 - s
                        valid = s > q
                    else:
                        dist = q - s
                        valid = s <= q
                    tab[:, j, :] = np.where(valid, -slope * dist, NEGM)
                cb[:, C_BIAS + ti * 256:C_BIAS + (ti + 1) * 256] = tab.reshape(128, 256)
    return cb.astype(ml_dtypes.bfloat16), np.eye(128, dtype=np.float32)


def _host_vecs(inp):
    out = np.zeros((DEPTH, 128, NV), np.float32)
    p = np.arange(128)
    for l in range(DEPTH):
        v = out[l]

        def cols(vec, n):
            return np.asarray(vec, np.float32).reshape(n, 128).T
        v[:, V_NMIX:V_NMIX + 8] = cols(inp["norm_mix_g"][l], 8)
        v[:, V_NX:V_NX + 8] = cols(inp["norm_x_g"][l], 8)
        v[:, V_NFFN:V_NFFN + 8] = cols(inp["norm_ffn_g"][l], 8)
        v[:, V_NMEM:V_NMEM + 8] = cols(inp["norm_mem_g"][l], 8)
        v[:, V_GQ] = np.asarray(inp["q_norm_g"][l])[p % 64]
        v[:, V_GK] = np.asarray(inp["k_norm_g"][l])[p % 64]
        v[:, V_CB:V_CB + 4] = cols(inp["conv_b"][l], 4)
        v[:, V_LNG:V_LNG + 4] = cols(inp["conv_ln_g"][l], 4)
        v[:, V_LNB:V_LNB + 4] = cols(inp["conv_ln_b"][l], 4)
        v[:, V_GXQ:V_GXQ + 2] = cols(inp["xq_norm_g"][l], 2)
        v[:, V_GXK:V_GXK + 2] = cols(inp["xk_norm_g"][l], 2)
        sk = np.asarray(inp["sinks"][l], np.float32)
        for g in range(2):
            for j in range(2):
                v[:64, V_SINKP + 2 * g + j] = sk[4 * g + 2 * j]
                v[64:, V_SINKP + 2 * g + j] = sk[4 * g + 2 * j + 1]
        v[:, V_SINKR:V_SINKR + 8] = sk[None, :]
        cw = np.asarray(inp["conv_w"][l], np.float32)
        for cc in range(4):
            v[:, V_CW + cc * 31:V_CW + (cc + 1) * 31] = cw[:, cc * 128:(cc + 1) * 128].T
        v[:, V_GQR:V_GQR + 64] = np.asarray(inp["q_norm_g"][l])[None, :]
        v[:, V_GKR:V_GKR + 64] = np.asarray(inp["k_norm_g"][l])[None, :]
        v[:, V_GXQR:V_GXQR + 256] = np.asarray(inp["xq_norm_g"][l])[None, :]
        v[:, V_GXKR:V_GXKR + 256] = np.asarray(inp["xk_norm_g"][l])[None, :]
    return out


def build(stop_after=None, native_silu=True, dump_tile=None):
    nc = bass.Bass("TRN2", target_bir_lowering=False)
    S = Sched(nc)
    dumps = []

    def dump(name, ap, reads):
        shape = list(ap.shape)
        d = nc.dram_tensor("dump_" + name, shape, ap.dtype, kind="ExternalOutput").ap()
        S.dma("sp", d, ap, reads=reads)
        dumps.append(name)

    def din(name, shape, dt=F32):
        return nc.dram_tensor(name, list(shape), dt, kind="ExternalInput").ap()

    x_d = din("x", [SEQ, D])
    mem_d = din("mem", [256, D])
    w_in_d = din("w_in", [DEPTH, D, 1792])
    w_out_d = din("w_out", [DEPTH, D, D])
    wq_d = din("wq_x", [DEPTH, D, D])
    wkv_d = din("wkv_x", [DEPTH, D, 2 * D])
    wo_d = din("wo_x", [DEPTH, D, D])
    wgu_d = din("w_gate_up", [DEPTH, D, 2 * 2816])
    wd_d = din("w_down", [DEPTH, 2816, D])
    vecs_d = din("vecs", [DEPTH, 128, NV])
    cbf_d = din("cbf", [128, NCB], BF16)
    cf_d = din("cf32", [128, 128])
    y_d = nc.dram_tensor("y", [SEQ, D], F32, kind="ExternalOutput").ap()
    xs_d = nc.dram_tensor("xs", [D, SEQ], F32, kind="Internal").ap()
    hs_d = nc.dram_tensor("hs", [D, SEQ], BF16, kind="Internal").ap()
    kvs_d = nc.dram_tensor("kvs", [128, 4096], BF16, kind="Internal").ap()
    xs_v = xs_d.rearrange("(c p) s -> p c s", p=128)
    hs_v = hs_d.rearrange("(c p) s -> p c s", p=128)

    NA = 16 * 1024
    NB = 8 * 1024 + 4 * 31 * 128 + 512
    NB = max(NB, 8 * 2 * HF)
    WA = nc.alloc_sbuf_tensor("WA", [128, NA], BF16)
    WBt = nc.alloc_sbuf_tensor("WBt", [128, NB], BF16)
    RA = [Res(f"WA{i}") for i in range(4)]
    RB = [Res("WB0"), Res("WB1")]
    xbuf = [nc.alloc_sbuf_tensor(f"xb{i}", [128, 8, T], F32) for i in range(2)]
    R_x = [[Res(f"x{i}_{c}") for c in range(8)] for i in range(2)]
    hT = nc.alloc_sbuf_tensor("hT", [128, 8, T], BF16)
    R_h = [Res(f"h{c}") for c in range(8)]
    cbf = nc.alloc_sbuf_tensor("cbf_sb", [128, NCB], BF16)
    identf = nc.alloc_sbuf_tensor("identf", [128, 128], F32)
    R_c = Res("consts")
    pv = nc.alloc_sbuf_tensor("pv", [128, DEPTH, V_KEEP], F32)
    NDV = 16
    dv = nc.alloc_sbuf_tensor("dv", [128, DEPTH, NDV], F32)
    R_v = Res("vecs")
    kvsb = nc.alloc_sbuf_tensor("kvsb", [128, 4096], BF16)
    khx = kvsb[:, 0:2048].rearrange("p (m k) -> p m k", m=8)
    vmem = kvsb[:, 2048:4096].rearrange("p (m d) -> p m d", m=2)
    R_kv = Res("kv")
    stage = [nc.alloc_sbuf_tensor(f"stage{i}", [128, 1024], F32) for i in range(2)]
    R_stage = [Res("stage0"), Res("stage1")]
    TBN = 60 * 1024
    TB = nc.alloc_sbuf_tensor("TB", [128, TBN // 2], BF16)
    PS = [nc.alloc_psum_tensor(f"ps{i}", [128, 512], F32) for i in range(8)]
    R_ps = [Res(f"ps{i}") for i in range(8)]
    R_psh = [[Res(f"psh{i}_{k}") for k in range(2)] for i in range(8)]
    st = {"ps": 0, "tb": 0, "nfull": 8, "ph": 0}

    def set_psum_split(nfull):
        st["nfull"] = nfull
        st["ps"] = 0
        st["ph"] = 0

    def ps():
        i = min(range(8), key=lambda k: R_ps[k].touch)
        R_ps[i].touch = getattr(S, "opidx", 0) + 0.5
        return PS[i], R_ps[i]

    def psh():
        nh = (8 - st["nfull"]) * 2
        k = st["ph"]
        st["ph"] = (k + 1) % nh
        b = st["nfull"] + k // 2
        return PS[b][:, (k % 2) * 256:(k % 2) * 256 + 256], R_psh[b][k % 2]

    def tb_reset():
        st["tb"] = 0

    def tb(shape, dt):
        n = int(np.prod(shape))
        nb = n * (4 if dt == F32 else 2)
        nb = (nb + 63) // 64 * 64
        off = st["tb"]
        assert off + nb <= TBN, (off, nb, TBN)
        st["tb"] = off + nb
        v = TB[:, off // 2:(off + nb) // 2]
        if dt == F32:
            v = v.bitcast(F32)
        v = v[:, 0:n]
        if len(shape) == 2:
            return v.rearrange("p (a b) -> p a b", a=shape[0])
        if len(shape) == 3:
            return v.rearrange("p (a b c) -> p a b c", a=shape[0], b=shape[1])
        return v

    def ACT(out, in_, func, R, W, bias=None, scale=None, PW=()):
        kw = {}
        if bias is not None:
            kw["bias"] = bias
        if scale is not None:
            kw["scale"] = scale
        return S.op("act", lambda h: h.activation(out, in_, func, **kw), reads=R, writes=W, pwrites=PW)

    def TT(en, out, a, b, op, R, W, PW=()):
        return S.op(en, lambda h: h.tensor_tensor(out, a, b, op), reads=R, writes=W, pwrites=PW)

    def STT(out, in0, scalar, in1, op0, op1, R, W, PW=()):
        return S.op("dve", lambda h: h.scalar_tensor_tensor(out, in0, scalar, in1, op0, op1), reads=R, writes=W, pwrites=PW)

    def TS(out, in0, s1, op0, R, W, s2=None, op1=None, PW=()):
        if op1 is None:
            return S.op("dve", lambda h: h.tensor_scalar(out, in0, s1, None, op0), reads=R, writes=W, pwrites=PW)
        return S.op("dve", lambda h: h.tensor_scalar(out, in0, s1, s2, op0, op1), reads=R, writes=W, pwrites=PW)

    def CP(en, out, in_, R, W, PW=()):
        if en == "act":
            return S.op("act", lambda h: h.copy(out, in_), reads=R, writes=W, pwrites=PW)
        return S.op(en, lambda h: h.tensor_copy(out, in_), reads=R, writes=W, pwrites=PW)

    def MM(out, lhsT, rhs, start, stop, R, W, signal):
        return S.op("pe", lambda h: h.matmul(out, lhsT, rhs, start=start, stop=stop), reads=R, writes=W, signal=signal)

    def TR(out, in_, R, W, signal):
        return S.op("pe", lambda h: h.transpose(out, in_, identf[:, :]), reads=list(R) + [R_c], writes=W, signal=signal)

    def cmat(off, n=128):
        return cbf[:, off:off + n]

    def vcol(l, c, n=1):
        return pv[:, l, c:c + n]

    def dcol(l, c, n=1):
        return dv[:, l, c:c + n]

    def rsqrt_from(out, tmp, src, l, R, Rtmp, W):
        ACT(tmp, src, AF.Ln, R=list(R) + [R_v], W=[Rtmp], bias=dcol(l, 9))
        ACT(out, tmp, AF.Exp, R=[Rtmp], W=W, scale=-0.5)

    def compute_barrier():
        toks = []
        for n in ("pe", "act", "dve"):
            e = S.E[n]
            if e.cnt > 0:
                toks.append(Tok(e.sem, e.cnt, {}))
        for n in ("pe", "act", "dve"):
            e = S.E[n]
            for t_ in toks:
                if t_.sem is e.sem and n == "pe":
                    continue
                S._wait(e, t_)

    S.dma("sp", cbf[:, :], cbf_d, writes=[R_c])
    S.dma("sp", identf[:, :], cf_d, pwrites=[R_c])
    tb_reset()
    vfull = tb([DEPTH, NV], F32)
    R_vf = Res("vfull")
    S.dma("sp", vfull, vecs_d.rearrange("l p n -> p l n"), writes=[R_vf])
    CP("dve", pv[:, :, :], vfull[:, :, 0:V_KEEP], R=[R_vf], W=[R_v])
    S.op("dve", lambda h: h.memset(dv[:, :, :], 0.0), writes=[R_v])
    S.op("dve", lambda h: h.memset(dv[:, :, 9:10], EPS), writes=[R_v])
    S.op("dve", lambda h: h.memset(dv[:, :, 10:11], 1.0), writes=[R_v])
    sc1 = tb([256], F32)
    sc2 = tb([8], F32)
    R_s1 = Res("sc1")
    R_s2 = Res("sc2")
    for l in range(DEPTH):
        TS(dcol(l, 0), vcol(l, V_GQ), 0.125, ALU.mult, R=[R_v], W=[R_v])
        TS(dcol(l, 7, 2), vcol(l, V_GXQ, 2), 1.0 / 16, ALU.mult, R=[R_v], W=[R_v])
        TT("dve", sc1[:, 0:64], vfull[:, l, V_GQR:V_GQR + 64], vfull[:, l, V_GKR:V_GKR + 64], ALU.mult, R=[R_vf], W=[R_s1])
        TT("dve", sc1[:, 0:64], sc1[:, 0:64], sc1[:, 0:64], ALU.mult, R=[R_s1], W=[R_s1])
        S.op("dve", lambda h: h.reduce_max(sc2[:, 0:1], sc1[:, 0:64], AX.X), reads=[R_s1], writes=[R_s2])
        ACT(sc2[:, 1:2], sc2[:, 0:1], AF.Ln, R=[R_s2], W=[R_s2])
        ACT(sc2[:, 2:3], sc2[:, 1:2], AF.Exp, R=[R_s2], W=[R_s2], scale=0.5)
        S.op("dve", lambda h, l=l: h.reduce_max(sc2[:, 3:4], vfull[:, l, V_SINKR:V_SINKR + 8], AX.X), reads=[R_vf, R_s2], writes=[R_s2])
        STT(sc2[:, 4:5], sc2[:, 2:3], 8.0, sc2[:, 3:4], ALU.mult, ALU.max, R=[R_s2], W=[R_s2])
        TS(dcol(l, 1), sc2[:, 4:5], -1.0, ALU.mult, R=[R_s2, R_v], W=[R_v])
        ACT(dcol(l, 2, 4), vcol(l, V_SINKP, 4), AF.Exp, R=[R_v], W=[R_v], bias=dcol(l, 1))
        TT("dve", sc1[:, :], vfull[:, l, V_GXQR:V_GXQR + 256], vfull[:, l, V_GXKR:V_GXKR + 256], ALU.mult, R=[R_vf, R_s1], W=[R_s1])
        TT("dve", sc1[:, :], sc1[:, :], sc1[:, :], ALU.mult, R=[R_s1], W=[R_s1])
        S.op("dve", lambda h: h.reduce_max(sc2[:, 5:6], sc1[:, :], AX.X), reads=[R_s1, R_s2], writes=[R_s2])
        ACT(sc2[:, 6:7], sc2[:, 5:6], AF.Ln, R=[R_s2], W=[R_s2])
        ACT(sc2[:, 7:8], sc2[:, 6:7], AF.Exp, R=[R_s2], W=[R_s2], scale=0.5)
        TS(dcol(l, 6), sc2[:, 7:8], -16.0, ALU.mult, R=[R_s2, R_v], W=[R_v])

    wkv = WA[:, 0:8 * 2048].rearrange("p (c n) -> p c n", c=8)
    memsb = xbuf[0][:, 0:4, :].rearrange("p a b -> p (a b)").rearrange("p (m d) -> p m d", m=2)
    R_mem = Res("mem")
    S.dma("sp", memsb, mem_d.rearrange("(m p) d -> p m d", p=128), writes=[R_mem])
    msq = tb([1024], F32)
    R_msq = Res("msq")
    mst = tb([8], F32)
    R_mst = Res("mst")
    mn = xbuf[1][:, 0:4, :].rearrange("p a b -> p (a b)").rearrange("p (m d) -> p m d", m=2)
    R_mn = Res("mn")
    for mb in range(2):
        ACT(msq[:, :], memsb[:, mb, :], AF.Square, R=[R_mem], W=[R_msq])
        S.op("dve", lambda h, mb=mb: h.reduce_sum(mst[:, mb:mb + 1], msq[:, :], AX.X), reads=[R_msq, R_mst], writes=[R_mst])
    TS(mst[:, 2:4], mst[:, 0:2], 1.0 / 1024, ALU.mult, R=[R_mst], W=[R_mst])
    ACT(mst[:, 4:6], mst[:, 2:4], AF.Ln, R=[R_mst, R_v], W=[R_mst], bias=dcol(0, 9))
    ACT(mst[:, 6:8], mst[:, 4:6], AF.Exp, R=[R_mst], W=[R_mst], scale=-0.5)
    for mb in range(2):
        TS(mn[:, mb, :], memsb[:, mb, :], mst[:, 6 + mb:7 + mb], ALU.mult, R=[R_mem, R_mst], W=[R_mn])
    memT = tb([8, 256], BF16)
    R_mT = Res("memT")
    ksq = tb([2, 256], BF16)
    R_ksq = Res("ksq")
    krs = tb([256], F32)
    R_krs = Res("krs")
    klt = tb([256], F32)
    R_klt = Res("klt")
    for l in (1, 0):
        for r_ in RA:
            r_.new_epoch()
        R_mT.new_epoch()
        R_kv.new_epoch()
        for c in range(8):
            S.dma("pool", wkv[:, c, :], wkv_d[l, c * 128:(c + 1) * 128, :], pwrites=RA)
        for c in range(8):
            p, Rp = ps()
            for mb in range(2):
                TR(p[:, mb * 128:(mb + 1) * 128], mn[:, mb, c * 128:(c + 1) * 128], R=[R_mn], W=[Rp], signal=(mb == 1))
            TS(memT[:, c, :], p[:, 0:256], vcol(l, V_NMEM + c), ALU.mult, R=[Rp, R_v], W=[], PW=[R_mT])
        for h_ in range(4):
            pab = []
            R_ksq.new_epoch()
            for a in range(2):
                p, Rp = ps()
                m = 2 * h_ + a
                for c in range(8):
                    MM(p[:, 0:256], wkv[:, c, m * 128:(m + 1) * 128], memT[:, c, :], c == 0, c == 7, R=RA + [R_mT], W=[Rp], signal=(c == 7))
                ACT(ksq[:, a, :], p[:, 0:256], AF.Square, R=[Rp], W=[], PW=[R_ksq])
                pab.append((p, Rp))
            pm, Rpm = ps()
            for a in range(2):
                MM(pm[:, 0:256], cmat(C_256), ksq[:, a, :], a == 0, a == 1, R=[R_ksq, R_c], W=[Rpm], signal=(a == 1))
            rsqrt_from(krs[:, :], klt[:, :], pm[:, 0:256], l, R=[Rpm], Rtmp=R_klt, W=[R_krs])
            for a in range(2):
                p, Rp = pab[a]
                STT(khx[:, 2 * h_ + a, :], p[:, 0:256], vcol(l, V_GXK + a), krs[:, :], ALU.mult, ALU.mult, R=[Rp, R_krs, R_v], W=[], PW=[R_kv])
        for mb in range(2):
            for n2 in range(2):
                p, Rp = ps()
                for c in range(8):
                    MM(p[:, :], memT[:, c, mb * 128:(mb + 1) * 128], wkv[:, c, 1024 + n2 * 512:1024 + (n2 + 1) * 512], c == 0, c == 7, R=RA + [R_mT], W=[Rp], signal=(c == 7))
                CP("act", vmem[:, mb, n2 * 512:(n2 + 1) * 512], p[:, :], R=[Rp], W=[], PW=[R_kv])
        if l == 1:
            S.dma("sp", kvs_d, kvsb[:, :], reads=[R_kv])
    compute_barrier()
    S.barrier()
    for r in (R_kv, R_v, R_c, R_mT):
        r.w = []
        r.r = {}
        r.pw = []

    def wview(region, off, c, n):
        t_ = WA if region == "A" else WBt
        return t_[:, off:off + c * n].rearrange("p (c n) -> p c n", c=c)

    w_in_v = wview("A", 0, 8, 2048)
    w_out_v = wview("B", 0, 8, 1024)
    _dgo = 8 * 1024
    diagw_v = WBt[:, _dgo:_dgo + 4 * 31 * 128].rearrange("p (a j m) -> p a j m", a=4, j=31)
    diagw_f = WBt[:, _dgo:_dgo + 4 * 31 * 128].rearrange("p (a m) -> p a m", a=124)
    wq_v = wview("A", 0, 8, 1024)
    wo_v = wview("A", 8 * 1024, 8, 1024)
    wgu_v = wview("B", 0, 8, 2 * HF)
    wd_v = WA[:, 0:NJ * 1024].rearrange("p (j n) -> p j n", j=NJ)
    W = {"M": (w_in_v, w_out_v, diagw_v), "X": (wq_v, wo_v), "FA": (wgu_v, wd_v), "FB": (wgu_v, wd_v)}

    def load_region(kind, l, region):
        if kind == "M" and region == "A":
            src = w_in_d[l].rearrange("(c p) n -> p c n", p=128)
            for (bi_, c0, c1) in ((0, 0, 512), (1, 512, 768), (3, 1280, 1792), (2, 768, 1280)):
                RA[bi_].new_epoch()
                S.dma("pool", w_in_v[:, :, c0:c1], src[:, :, c0:c1], pwrites=[RA[bi_]])
                if bi_ == 1:
                    for (dst, sc_) in ((1792, 512), (1856, 512), (1920, 576), (1984, 576)):
                        S.op("pool", lambda h, dst=dst, sc_=sc_: h.tensor_copy(w_in_v[:, :, dst:dst + 64], w_in_v[:, :, sc_:sc_ + 64]),
                             reads=[RA[1]], pwrites=[RA[1]])
        elif kind == "M" and region == "B1":
            for r_ in RB:
                r_.new_epoch()
            for c in range(8):
                S.dma("pool", w_out_v[:, c, :], w_out_d[l, c * 128:(c + 1) * 128, :], pwrites=RB)
            S.op("dve", lambda h: h.tensor_tensor(
                diagw_f, cmat(C_ID).unsqueeze(1).to_broadcast([128, 124, 128]),
                pv[:, l, V_CW:V_CW + 124].unsqueeze(2).to_broadcast([128, 124, 128]), ALU.mult),
                reads=[R_c, R_v], pwrites=RB)
        elif kind == "X" and region == "A":
            for r_ in RA:
                r_.new_epoch()
            for c in range(8):
                S.dma("pool", wq_v[:, c, :], wq_d[l, c * 128:(c + 1) * 128, :], pwrites=RA)
            for c in range(8):
                S.dma("pool", wo_v[:, c, :], wo_d[l, c * 128:(c + 1) * 128, :], pwrites=RA)
        elif kind in ("FA", "FB") and region in ("B0", "B1"):
            f0 = (0 if kind == "FA" else 1) * HF
            hb = 0 if region == "B0" else 1
            j0, j1 = (0, JS) if hb == 0 else (JS, NJ)
            RB[hb].new_epoch()
            for c in range(8):
                S.dma("pool", wgu_v[:, c, j0 * 128:j1 * 128], wgu_d[l, c * 128:(c + 1) * 128, f0 + j0 * 128:f0 + j1 * 128], pwrites=[RB[hb]])
                S.dma("pool", wgu_v[:, c, HF + j0 * 128:HF + j1 * 128],
                      wgu_d[l, c * 128:(c + 1) * 128, 2816 + f0 + j0 * 128:2816 + f0 + j1 * 128], pwrites=[RB[hb]])
        elif kind in ("FA", "FB") and region == "A":
            f0 = (0 if kind == "FA" else 1) * HF
            for r_ in RA:
                r_.new_epoch()
            wdv = wd_d[l, f0:f0 + HF, :].rearrange("(j p) n -> p j n", p=128)
            for (j0, j1) in ((0, 4), (4, 8), (8, 11)):
                S.dma("pool", wd_v[:, j0:j1, :], wdv[:, j0:j1, :], pwrites=RA)

    R_xs = [Res(f"xs{t}") for t in range(NT)]
    R_hs = [Res(f"hs{t}") for t in range(NT)]

    def load_x(t, bi):
        S.dma("sp", xbuf[bi][:, :, :], xs_v[:, :, t * T:(t + 1) * T], reads=[R_xs[t]], writes=R_x[bi])

    def store_x(t, bi):
        S.dma("sp", xs_v[:, :, t * T:(t + 1) * T], xbuf[bi][:, :, :], reads=R_x[bi], writes=[R_xs[t]])

    def store_h(t):
        S.dma("sp", hs_v[:, :, t * T:(t + 1) * T], hT[:, :, :], reads=R_h, writes=[R_hs[t]])

    def load_h(t):
        S.dma("sp", hT[:, :, :], hs_v[:, :, t * T:(t + 1) * T], reads=[R_hs[t]], writes=R_h)

    class NextPrep:
        def __init__(self, kind, l):
            self.kind = kind
            self.l = l
            self.gbase = {"M": V_NMIX, "X": V_NX, "FA": V_NFFN, "FB": None}[kind]

        def load(self):
            load_x(0, 0)

        def squares(self, sq, R_sq, eng="act"):
            if self.gbase is not None:
                norm_squares(0, sq, R_sq, eng)

        def finish(self, sq, R_sq, tmpf):
            if self.gbase is not None:
                norm_finish(self.l, 0, self.gbase, sq, R_sq, tmpf)
                if self.kind == "FA":
                    store_h(0)
            else:
                load_h(0)

        def stats(self, sq, R_sq, tmpf, rstd, R_rstd):
            if self.gbase is not None:
                norm_stats(self.l, sq, R_sq, tmpf, rstd, R_rstd)

        def apply(self, rstd, R_rstd):
            if self.gbase is not None:
                norm_apply(self.l, 0, self.gbase, rstd, R_rstd)
                if self.kind == "FA":
                    store_h(0)
            else:
                load_h(0)

    scnt = {"i": 0}

    def tokmajor_dma(t, k):
        si = k % 2
        r0 = t * T + k * 128
        S.dma("sp", stage[si][:, :], x_d[r0:r0 + 128, :], writes=[R_stage[si]])

    def tokmajor_tr(t, bi, k):
        si = k % 2
        for i in range(2):
            p, Rp = ps()
            for mm_ in range(4):
                c = 4 * i + mm_
                TR(p[:, mm_ * 128:(mm_ + 1) * 128], stage[si][:, c * 128:(c + 1) * 128], R=[R_stage[si]], W=[Rp], signal=(mm_ == 3))
            pview = p[:, :].rearrange("p (c n) -> p c n", c=4)
            if k == 0:
                for mm_ in range(4):
                    R_x[bi][4 * i + mm_].new_epoch()
            CP("act" if i else "dve", xbuf[bi][:, 4 * i:4 * i + 4, k * 128:(k + 1) * 128], pview, R=[Rp], W=[],
               PW=[R_x[bi][4 * i + mm_] for mm_ in range(4)])

    def load_x_tokmajor(t, bi):
        for k in range(4):
            tokmajor_dma(t, k)
            tokmajor_tr(t, bi, k)

    def transpose_out(t, bi):
        for k in range(4):
            si = scnt["i"] % 2
            scnt["i"] += 1
            R_stage[si].new_epoch()
            for i in range(2):
                p, Rp = ps()
                for mm_ in range(4):
                    m = 4 * i + mm_
                    TR(p[:, mm_ * 128:(mm_ + 1) * 128], xbuf[bi][:, m, k * 128:(k + 1) * 128], R=[R_x[bi][m]], W=[Rp], signal=(mm_ == 3))
                CP("act" if i else "dve", stage[si][:, i * 512:(i + 1) * 512], p[:, :], R=[Rp], W=[], PW=[R_stage[si]])
            r0 = t * T + k * 128
            S.dma("sp", y_d[r0:r0 + 128, :], stage[si][:, :], reads=[R_stage[si]])

    def norm_squares(bi, sq, R_sq, eng="act"):
        for c in range(8):
            if eng == "act":
                ACT(sq[:, c, :], xbuf[bi][:, c, :], AF.Square, R=[R_x[bi][c]], W=[R_sq[c]])
            else:
                TT("dve", sq[:, c, :], xbuf[bi][:, c, :], xbuf[bi][:, c, :], ALU.mult, R=[R_x[bi][c]], W=[R_sq[c]])

    def norm_finish(l, bi, gbase, sq, R_sq, tmpf):
        rstd, R_rstd = tmpf()
        lnt, R_lnt = tmpf()
        p, Rp = ps()
        for c in range(8):
            MM(p[:, :], cmat(C_1024), sq[:, c, :], c == 0, c == 7, R=[R_sq[c], R_c], W=[Rp], signal=(c == 7))
        rsqrt_from(rstd[:, :], lnt[:, :], p[:, :], l, R=[Rp], Rtmp=R_lnt, W=[R_rstd])
        for c in range(8):
            STT(hT[:, c, :], xbuf[bi][:, c, :], vcol(l, gbase + c), rstd[:, :], ALU.mult, ALU.mult,
                R=[R_x[bi][c], R_rstd, R_v], W=[R_h[c]])

    def norm_stats(l, sq, R_sq, tmpf, rstd, R_rstd):
        lnt, R_lnt = tmpf()
        p, Rp = ps()
        for c in range(8):
            MM(p[:, :], cmat(C_1024), sq[:, c, :], c == 0, c == 7, R=[R_sq[c], R_c], W=[Rp], signal=(c == 7))
        rsqrt_from(rstd[:, :], lnt[:, :], p[:, :], l, R=[Rp], Rtmp=R_lnt, W=[R_rstd])

    def norm_apply(l, bi, gbase, rstd, R_rstd):
        for c in range(8):
            STT(hT[:, c, :], xbuf[bi][:, c, :], vcol(l, gbase + c), rstd[:, :], ALU.mult, ALU.mult,
                R=[R_x[bi][c], R_rstd, R_v], W=[R_h[c]])

    def residual_chunk(bi, m, wmat, RW, nk, rhsT, R_rhs):
        p, Rp = ps()
        for k in range(nk):
            MM(p[:, :], wmat[:, k, m * 128:(m + 1) * 128], rhsT[:, k, :], k == 0, k == nk - 1,
               R=list(RW) + [R_rhs[k]], W=[Rp], signal=(k == nk - 1))
        TT("dve", xbuf[bi][:, m, :], xbuf[bi][:, m, :], p[:, :], ALU.add, R=[Rp, R_x[bi][m]], W=[R_x[bi][m]])

    sqpool = {"tiles": None, "i": 0}

    def ptile_big():
        tl = sqpool["tiles"]
        i = sqpool["i"]
        sqpool["i"] = (i + 1) % len(tl)
        return tl[i]

    def alloc_sqpool(n):
        sqpool["tiles"] = [(tb([T], BF16), Res(f"sqq{i}")) for i in range(n)]
        sqpool["i"] = 0

    def make_rot(tiles, ress):
        c = {"i": 0}

        def f():
            i = c["i"]
            c["i"] = (i + 1) % len(tiles)
            return tiles[i], ress[i]
        return f

    def phase_M(l, next_load, prepped, nextp):
        tb_reset()
        set_psum_split(8)
        w_in, w_out, diagw = W["M"]
        sq = tb([8, T], BF16)
        R_sq = [Res(f"sq{c}") for c in range(8)]
        qhat = tb([4, T], BF16)
        R_q = [Res(f"q{i}") for i in range(4)]
        khat = tb([2, 128 + T], BF16)
        R_k = Res("khat")
        vpad = tb([5, 2, 192], BF16)
        R_vp = Res("vpad")
        gbuf = tb([4, 30 + T], BF16)
        R_g = [Res(f"g{i}") for i in range(4)]
        mixT = tb([8, T], BF16)
        R_mix = [Res(f"mix{i}") for i in range(8)]
        ycv = tb([4, T], F32)
        R_y = [Res(f"y{i}") for i in range(4)]
        NTMP = 5
        tmp = make_rot([tb([T], F32) for _ in range(NTMP)], [Res(f"tmp{i}") for i in range(NTMP)])
        nrstd = tb([T], F32)
        R_nrstd = Res("nrstd")
        NP = 6
        ptile = make_rot([tb([T], BF16) for _ in range(NP)], [Res(f"P{i}") for i in range(NP)])
        alloc_sqpool(3)

        S.op("dve", lambda h: h.memset(vpad.rearrange("p a b c -> p (a b c)"), 0.0), writes=[R_vp])
        for cc in range(4):
            S.op("dve", lambda h, cc=cc: h.memset(gbuf[:, cc, 0:30], 0.0), writes=[R_g[cc]])

        def sigmoid_of(src, Rsrc):
            e, Re = tmp()
            ACT(e[:, :], src, AF.Exp, R=Rsrc, W=[Re], scale=-1.0)
            l_, Rl = tmp()
            ACT(l_[:, :], e[:, :], AF.Ln, R=[Re, R_v], W=[Rl], bias=dcol(l, 10))
            s_, Rs = tmp()
            ACT(s_[:, :], l_[:, :], AF.Exp, R=[Rl], W=[Rs], scale=-1.0)
            return s_, Rs

        def qk_mm(spec):
            col, kind, idx = spec
            p, Rp = ps()
            for c in range(8):
                MM(p[:, :], w_in[:, c, col:col + 128], hT[:, c, :], c == 0, c == 7, R=[RA[0 if kind == "q" else 1], R_h[c]], W=[Rp], signal=(c == 7))
            sqq, Rsqq = ptile_big()
            ACT(sqq[:, :], p[:, :], AF.Square, R=[Rp], W=[Rsqq])
            return (spec, p, Rp, sqq, Rsqq)

        def qk_fin(state):
            (col, kind, idx), p, Rp, sqq, Rsqq = state
            p2, Rp2 = ps()
            MM(p2[:, :], cmat(C_BD64), sqq[:, :], True, True, R=[Rsqq, R_c], W=[Rp2], signal=True)
            lt, Rlt = tmp()
            rs, Rrs = tmp()
            rsqrt_from(rs[:, :], lt[:, :], p2[:, :], l, R=[Rp2], Rtmp=Rlt, W=[Rrs])
            if kind == "q":
                STT(qhat[:, idx, :], p[:, :], dcol(l, 0), rs[:, :], ALU.mult, ALU.mult, R=[Rp, Rrs, R_v], W=[R_q[idx]])
            else:
                STT(khat[:, idx, 128:128 + T], p[:, :], vcol(l, V_GK), rs[:, :], ALU.mult, ALU.mult, R=[Rp, Rrs, R_v], W=[R_k])

        def scores(t, b, g):
            n = 4 * t + b
            kbs = ([0] if n > 0 else []) + [1]
            c0 = kbs[0] * 256
            banks = [ps(), ps()]
            for kb in kbs:
                for par in range(2):
                    p, Rp = banks[par]
                    pview = p[:, kb * 256:(kb + 1) * 256].rearrange("p (a b) -> p a b", a=2)
                    kblk = khat[64 * par:64 * par + 64, g, (b + kb) * 128:(b + kb + 1) * 128]
                    qrhs = qhat[64 * par:64 * par + 64, 2 * g:2 * g + 2, b * 128:(b + 1) * 128]
                    MM(pview, kblk, qrhs, True, False, R=[R_k, R_q[2 * g], R_q[2 * g + 1]], W=[Rp], signal=False)
                    ti = (g * 2 + par) * 2 + kb
                    MM(p[:, kb * 256:(kb + 1) * 256], cmat(C_ID), cbf[:, C_BIAS + ti * 256:C_BIAS + (ti + 1) * 256], False, True,
                       R=[R_c], W=[Rp], signal=(kb == 1))
            plist = []
            for par in range(2):
                p, Rp = banks[par]
                P_, RP_ = ptile()
                ACT(P_[:, c0:512], p[:, c0:512], AF.Exp, R=[Rp, R_v], W=[RP_], bias=dcol(l, 1))
                plist.append((par, P_, RP_))
            return (kbs, plist)

        def pv_den(t, b, g, sc):
            kbs, plist = sc
            pnd, Rpnd = ps()
            pn = pnd[:, 0:256]
            pd = pnd[:, 256:512]
            items = [(kb, par, P_, RP_) for (par, P_, RP_) in plist for kb in kbs]
            nmm = len(items)
            for i, (kb, par, P_, RP_) in enumerate(items):
                lo = 64 if par == 0 else 0
                MM(pn, vpad[:, b + kb, g, lo:lo + 128], P_[:, kb * 256:(kb + 1) * 256], i == 0, i == nmm - 1, R=[R_vp, RP_], W=[Rpnd], signal=False)
            for i, (kb, par, P_, RP_) in enumerate(items):
                lo = 64 if par == 0 else 0
                MM(pd, cbf[:, C_OPAD + lo:C_OPAD + lo + 128], P_[:, kb * 256:(kb + 1) * 256], i == 0, i == nmm - 1, R=[R_c, RP_], W=[Rpnd], signal=(i == nmm - 1))
            ld, Rld = tmp()
            R_first = True
            for j in range(2):
                S.op("act", lambda h, j=j: h.activation(ld[:, j * 128:(j + 1) * 128], pd[:, j * 128:(j + 1) * 128], AF.Ln,
                                                        bias=dcol(l, 2 + 2 * g + j)),
                     reads=[Rpnd, R_v], writes=[Rld] if j == 0 else [], pwrites=[] if j == 0 else [Rld])
            rd, Rrd = tmp()
            ACT(rd[:, 0:256], ld[:, 0:256], AF.Exp, R=[Rld], W=[Rrd], scale=-1.0)
            TT("dve", mixT[:, 2 * g:2 * g + 2, b * 128:(b + 1) * 128], pn.rearrange("p (a b) -> p a b", a=2),
               rd[:, 0:256].rearrange("p (a b) -> p a b", a=2), ALU.mult, R=[Rpnd, Rrd], W=[], PW=[R_mix[2 * g], R_mix[2 * g + 1]])

        def conv_taps_dve(cc):
            TS(ycv[:, cc, :], gbuf[:, cc, 0:T], vcol(l, V_CW + cc * 31), ALU.mult, R=[R_g[cc], R_v], W=[R_y[cc]])
            for j in range(1, TD):
                STT(ycv[:, cc, :], gbuf[:, cc, j:j + T], vcol(l, V_CW + cc * 31 + j), ycv[:, cc, :], ALU.mult, ALU.add,
                    R=[R_g[cc], R_v, R_y[cc]], W=[R_y[cc]])

        def conv_chunk(cc):
            p, Rp = ps()
            for j in range(TD, 31):
                MM(p[:, :], diagw[:, cc, j, :], gbuf[:, cc, j:j + T], j == TD, j == 30, R=RB + [R_g[cc]], W=[Rp], signal=(j == 30))
            STT(ycv[:, cc, :], p[:, :], vcol(l, V_CB + cc), ycv[:, cc, :], ALU.add, ALU.add, R=[Rp, R_v, R_y[cc]], W=[R_y[cc]])
            S.op("dve", lambda h, cc=cc: h.tensor_copy(gbuf[:, cc, 0:30], gbuf[:, cc, T:T + 30]), reads=[], writes=[R_g[cc]])

        ln_state = {}

        def ln_part1():
            for cc in range(4):
                CP("dve", sq[:, cc, :], ycv[:, cc, :], R=[R_y[cc]], W=[R_sq[cc]])
            pm, Rpm = ps()
            for cc in range(4):
                MM(pm[:, :], cmat(C_512), sq[:, cc, :], cc == 0, cc == 3, R=[R_sq[cc], R_c], W=[Rpm], signal=(cc == 3))
            for cc in range(4):
                TT("dve", ycv[:, cc, :], ycv[:, cc, :], pm[:, :], ALU.subtract, R=[Rpm, R_y[cc]], W=[R_y[cc]])
                ACT(sq[:, 4 + cc, :], ycv[:, cc, :], AF.Square, R=[R_y[cc]], W=[R_sq[4 + cc]])

        def ln_part2():
            pvv, Rpvv = ps()
            for cc in range(4):
                MM(pvv[:, :], cmat(C_512), sq[:, 4 + cc, :], cc == 0, cc == 3, R=[R_sq[4 + cc], R_c], W=[Rpvv], signal=(cc == 3))
            lt, Rlt = tmp()
            rs, Rrs = tmp()
            rsqrt_from(rs[:, :], lt[:, :], pvv[:, :], l, R=[Rpvv], Rtmp=Rlt, W=[Rrs])
            for cc in range(4):
                STT(ycv[:, cc, :], ycv[:, cc, :], vcol(l, V_LNG + cc), rs[:, :], ALU.mult, ALU.mult, R=[R_y[cc], Rrs, R_v], W=[R_y[cc]])

        def ln_part3(cc):
            wp, Rwp = tmp()
            ACT(wp[:, :], ycv[:, cc, :], AF.Identity, R=[R_y[cc], R_v], W=[Rwp], bias=vcol(l, V_LNB + cc))
            s_, Rs = sigmoid_of(wp[:, :], [Rwp])
            TT("dve", mixT[:, 4 + cc, :], wp[:, :], s_[:, :], ALU.mult, R=[Rwp, Rs], W=[R_mix[4 + cc]])

        def pre(t):
            bi = t % 2
            if l == 0:
                load_x_tokmajor(t, bi)
            else:
                load_x(t, bi)

        if not prepped:
            pre(0)
            norm_squares(0, sq, R_sq)
            norm_finish(l, 0, V_NMIX, sq, R_sq, tmp)
        for t in range(NT):
            bi = t % 2
            last_tile = (t == NT - 1)
            if not last_tile and l > 0:
                pre(t + 1)
            elif last_tile and nextp is not None:
                nextp.load()
            if not last_tile and l == 0:
                tokmajor_dma(t + 1, 0)
                tokmajor_dma(t + 1, 1)
            specs = [(i * 128, "q", i) for i in range(4)] + [(1792, "k", 0), (1920, "k", 1)]
            prev = None
            for sp_ in specs:
                cur = qk_mm(sp_)
                if prev is not None:
                    qk_fin(prev)
                prev = cur
            p, Rp = ps()
            for k in range(4):
                for c in range(8):
                    MM(p[:, k * 128:(k + 1) * 128], hT[:, c, k * 128:(k + 1) * 128], w_in[:, c, 640:768], c == 0, c == 7,
                       R=[RA[1], R_h[c]], W=[Rp], signal=(k == 3 and c == 7))
            qk_fin(prev)
            S.op("act", lambda h, p=p: h.copy(vpad[:, 1:5, :, 64:128], p[:, :].rearrange("p (k g d) -> p k g d", k=4, g=2)),
                 reads=[Rp], writes=[R_vp])
            if not last_tile and l == 0:
                tokmajor_tr(t + 1, (t + 1) % 2, 0)
                tokmajor_tr(t + 1, (t + 1) % 2, 1)
                tokmajor_dma(t + 1, 2)
                tokmajor_dma(t + 1, 3)
            for cc in range(4):
                pv_, Rpv = ps()
                pg, Rpg = ps()
                for c in range(8):
                    MM(pg[:, :], w_in[:, c, 1280 + cc * 128:1280 + (cc + 1) * 128], hT[:, c, :], c == 0, c == 7, R=[RA[3], R_h[c]], W=[Rpg], signal=(c == 7))
                for c in range(8):
                    MM(pv_[:, :], w_in[:, c, 768 + cc * 128:768 + (cc + 1) * 128], hT[:, c, :], c == 0, c == 7, R=[RA[2], R_h[c]], W=[Rpv], signal=(c == 7))
                s_, Rs = sigmoid_of(pg[:, :], [Rpg])
                TT("dve", gbuf[:, cc, 30:30 + T], pv_[:, :], s_[:, :], ALU.mult, R=[Rpv, Rs], W=[R_g[cc]])
                conv_taps_dve(cc)
            if last_tile and next_load is not None:
                next_load("A")
            if not last_tile and l == 0:
                tokmajor_tr(t + 1, (t + 1) % 2, 2)
                tokmajor_tr(t + 1, (t + 1) % 2, 3)
            for cc in range(4):
                conv_chunk(cc)
            for i_ in range(4):
                R_mix[i_].new_epoch()
            units = [(b, g) for b in range(4) for g in range(2)]
            LOOK = 2
            pend = []
            for ui, (b, g) in enumerate(units):
                pl = scores(t, b, g)
                pend.append((b, g, pl))
                if len(pend) > LOOK:
                    pb, pg_, ppl = pend.pop(0)
                    pv_den(t, pb, pg_, ppl)
                if ui == 0:
                    ln_part1()
                if ui == 2:
                    ln_part2()
                if ui in (2, 3, 4, 5):
                    ln_part3(ui - 2)
                if ui == 5:
                    if not last_tile:
                        norm_squares((t + 1) % 2, sq, R_sq, "dve")
                    elif nextp is not None:
                        nextp.squares(sq, R_sq, "dve")
            while pend:
                pb, pg_, ppl = pend.pop(0)
                pv_den(t, pb, pg_, ppl)
            if not last_tile:
                norm_stats(l, sq, R_sq, tmp, nrstd, R_nrstd)
            elif nextp is not None:
                nextp.stats(sq, R_sq, tmp, nrstd, R_nrstd)
            S.op("dve", lambda h: h.tensor_copy(khat[:, :, 0:128], khat[:, :, T:T + 128]), reads=[], writes=[R_k])
            S.op("dve", lambda h: h.tensor_copy(vpad[:, 0, :, :], vpad[:, 4, :, :]), reads=[], writes=[R_vp])
            if dump_tile is not None and t == dump_tile and l == 0:
                dump("mixT", mixT, R_mix)
            if not last_tile:
                norm_apply(l, (t + 1) % 2, V_NMIX, nrstd, R_nrstd)
            elif nextp is not None:
                nextp.apply(nrstd, R_nrstd)
            for m in range(8):
                residual_chunk(bi, m, w_out, RB, 8, mixT, R_mix)
            store_x(t, bi)
        if next_load is not None:
            next_load("B0")
            next_load("B1")

    def phase_X(l, next_load, prepped, nextp):
        tb_reset()
        set_psum_split(8)
        wq, wo = W["X"]
        sq = tb([8, T], BF16)
        R_sq = [Res(f"xsq{c}") for c in range(8)]
        qx = tb([8, T], BF16)
        R_qx = [Res(f"qx{i}") for i in range(8)]
        oT = tb([8, T], BF16)
        R_o = [Res(f"o{i}") for i in range(8)]
        NTMP = 4
        tmp = make_rot([tb([T], F32) for _ in range(NTMP)], [Res(f"xtmp{i}") for i in range(NTMP)])
        NP = 6
        ptile = make_rot([tb([T], BF16) for _ in range(NP)], [Res(f"xP{i}") for i in range(NP)])
        alloc_sqpool(4)
        if l == 1:
            S.dma("sp", kvsb[:, :], kvs_d, writes=[R_kv])
        if next_load is not None:
            next_load("B0")
            next_load("B1")

        def q_mm(h_):
            pab = []
            sqs = []
            for a in range(2):
                m = 2 * h_ + a
                p, Rp = ps()
                for c in range(8):
                    MM(p[:, :], wq[:, c, m * 128:(m + 1) * 128], hT[:, c, :], c == 0, c == 7, R=RA + [R_h[c]], W=[Rp], signal=(c == 7))
                sqq, Rsqq = ptile_big()
                ACT(sqq[:, :], p[:, :], AF.Square, R=[Rp], W=[Rsqq])
                pab.append((p, Rp))
                sqs.append((sqq, Rsqq))
            return (h_, pab, sqs)

        def q_fin(state):
            h_, pab, sqs = state
            pm, Rpm = ps()
            for a in range(2):
                MM(pm[:, :], cmat(C_256), sqs[a][0][:, :], a == 0, a == 1, R=[sqs[a][1], R_c], W=[Rpm], signal=(a == 1))
            lt, Rlt = tmp()
            rs, Rrs = tmp()
            rsqrt_from(rs[:, :], lt[:, :], pm[:, :], l, R=[Rpm], Rtmp=Rlt, W=[Rrs])
            for a in range(2):
                p, Rp = pab[a]
                STT(qx[:, 2 * h_ + a, :], p[:, :], dcol(l, 7 + a), rs[:, :], ALU.mult, ALU.mult, R=[Rp, Rrs, R_v], W=[R_qx[2 * h_ + a]])

        def xscores(h_):
            Ps = []
            for mb in range(2):
                p, Rp = ps()
                for a in range(2):
                    m = 2 * h_ + a
                    MM(p[:, :], khx[:, m, mb * 128:(mb + 1) * 128], qx[:, m, :], a == 0, a == 1, R=[R_kv, R_qx[m]], W=[Rp], signal=(a == 1))
                P_, RP_ = ptile()
                ACT(P_[:, :], p[:, :], AF.Exp, R=[Rp, R_v], W=[RP_], bias=dcol(l, 6))
                Ps.append((P_, RP_))
            return (h_, Ps)

        def xpv(state):
            h_, Ps = state
            pd, Rpd = ps()
            for mb in range(2):
                MM(pd[:, :], cmat(C_ONES), Ps[mb][0][:, :], mb == 0, mb == 1, R=[R_c, Ps[mb][1]], W=[Rpd], signal=(mb == 1))
            pns = []
            for a in range(2):
                m = 2 * h_ + a
                pn, Rpn = ps()
                for mb in range(2):
                    MM(pn[:, :], vmem[:, mb, m * 128:(m + 1) * 128], Ps[mb][0][:, :], mb == 0, mb == 1, R=[R_kv, Ps[mb][1]], W=[Rpn], signal=(mb == 1))
                pns.append((pn, Rpn))
            ld, Rld = tmp()
            ACT(ld[:, :], pd[:, :], AF.Ln, R=[Rpd], W=[Rld])
            rd, Rrd = tmp()
            ACT(rd[:, :], ld[:, :], AF.Exp, R=[Rld], W=[Rrd], scale=-1.0)
            for a in range(2):
                m = 2 * h_ + a
                TT("dve", oT[:, m, :], pns[a][0][:, :], rd[:, :], ALU.mult, R=[pns[a][1], Rrd], W=[R_o[m]])

        if not prepped:
            load_x(0, 0)
            norm_squares(0, sq, R_sq)
            norm_finish(l, 0, V_NX, sq, R_sq, tmp)
        for t in range(NT):
            bi = t % 2
            last_tile = (t == NT - 1)
            if not last_tile:
                load_x(t + 1, (t + 1) % 2)
            elif nextp is not None:
                nextp.load()
            prev = None
            for h_ in range(4):
                cur = q_mm(h_)
                if prev is not None:
                    q_fin(prev)
                prev = cur
            q_fin(prev)
            if not last_tile:
                norm_squares((t + 1) % 2, sq, R_sq, "dve")
            elif nextp is not None:
                nextp.squares(sq, R_sq, "dve")
            prevs = None
            for h_ in range(4):
                cur = xscores(h_)
                if prevs is not None:
                    xpv(prevs)
                prevs = cur
            xpv(prevs)
            for m in range(8):
                residual_chunk(bi, m, wo, RA, 8, oT, R_o)
                if m == 1 and not last_tile:
                    norm_finish(l, (t + 1) % 2, V_NX, sq, R_sq, tmp)
                elif m == 1 and nextp is not None:
                    nextp.finish(sq, R_sq, tmp)
            store_x(t, bi)
        if next_load is not None:
            next_load("A")

    def phase_F(l, half, last, next_load, prepped, nextp):
        tb_reset()
        set_psum_split(8)
        wgu, wd = W["FA" if half == 0 else "FB"]
        sq = tb([8, T], BF16)
        R_sq = [Res(f"fsq{c}") for c in range(8)]
        actT = tb([NJ, T], BF16)
        R_a = [Res(f"a{i}") for i in range(NJ)]
        NTMP = 6
        tmp = make_rot([tb([T], F32) for _ in range(NTMP)], [Res(f"ftmp{i}") for i in range(NTMP)])

        def gu_mm(j):
            pg, Rpg = ps()
            pu, Rpu = ps()
            for c in range(8):
                MM(pg[:, :], wgu[:, c, j * 128:(j + 1) * 128], hT[:, c, :], c == 0, c == 7, R=[RB[0 if j < JS else 1], R_h[c]], W=[Rpg], signal=(c == 7))
            for c in range(8):
                MM(pu[:, :], wgu[:, c, HF + j * 128:HF + (j + 1) * 128], hT[:, c, :], c == 0, c == 7, R=[RB[0 if j < JS else 1], R_h[c]], W=[Rpu], signal=(c == 7))
            return (j, pg, Rpg, pu, Rpu)

        def gu_fin(state):
            j, pg, Rpg, pu, Rpu = state
            s_, Rs = tmp()
            ACT(s_[:, :], pg[:, :], AF.Silu, R=[Rpg], W=[Rs])
            TT("dve", actT[:, j, :], s_[:, :], pu[:, :], ALU.mult, R=[Rs, Rpu], W=[R_a[j]])

        def h_ready(t):
            bi = t % 2
            if half == 0:
                norm_finish(l, bi, V_NFFN, sq, R_sq, tmp)
                store_h(t)
            else:
                load_h(t)

        if not prepped:
            load_x(0, 0)
            if half == 0:
                norm_squares(0, sq, R_sq)
            h_ready(0)
        for t in range(NT):
            bi = t % 2
            last_tile = (t == NT - 1)
            if not last_tile:
                load_x(t + 1, (t + 1) % 2)
            elif nextp is not None:
                nextp.load()
            prev = None
            for j in range(NJ):
                cur = gu_mm(j)
                if prev is not None:
                    gu_fin(prev)
                prev = cur
                if j == JS - 1 and last_tile and next_load is not None:
                    next_load("B0")
            gu_fin(prev)
            if last_tile and next_load is not None:
                next_load("B1")
            if not last_tile and half == 0:
                norm_squares((t + 1) % 2, sq, R_sq)
            if not last_tile and half == 1:
                h_ready(t + 1)
            if last_tile and nextp is not None:
                nextp.squares(sq, R_sq)
            for m in range(8):
                residual_chunk(bi, m, wd, RA, NJ, actT, R_a)
                if m == 0 and not last_tile and half == 0:
                    h_ready(t + 1)
                elif m == 0 and last_tile and nextp is not None:
                    nextp.finish(sq, R_sq, tmp)
            if last:
                transpose_out(t, bi)
            else:
                store_x(t, bi)
        if next_load is not None:
            next_load("A")

    phases = []
    for l in range(DEPTH):
        phases += [("M", l), ("X", l), ("FA", l), ("FB", l)]
    if stop_after is not None:
        phases = phases[:stop_after]
    load_region(phases[0][0], phases[0][1], "A")
    load_region(phases[0][0], phases[0][1], "B0")
    load_region(phases[0][0], phases[0][1], "B1")
    for i, (kind, l) in enumerate(phases):
        nxt = None
        if i + 1 < len(phases):
            nk, nl = phases[i + 1]
            nxt = (lambda region, nk=nk, nl=nl: load_region(nk, nl, region))
        nextp = NextPrep(phases[i + 1][0], phases[i + 1][1]) if i + 1 < len(phases) else None
        prepped = (i > 0)
        if kind == "M":
            phase_M(l, nxt, prepped, nextp)
        elif kind == "X":
            phase_X(l, nxt, prepped, nextp)
        elif kind == "FA":
            phase_F(l, 0, False, nxt, prepped, nextp)
        else:
            phase_F(l, 1, (l == DEPTH - 1 and stop_after is None), nxt, prepped, nextp)
        compute_barrier()
    if stop_after is not None:
        dbg = nc.dram_tensor("dbg", [D, SEQ], F32, kind="ExternalOutput").ap()
        for t in range(NT):
            S.dma("sp", xbuf[0][:, :, :], xs_v[:, :, t * T:(t + 1) * T], reads=[R_xs[t]], writes=R_x[0])
            S.dma("sp", dbg.rearrange("(c p) s -> p c s", p=128)[:, :, t * T:(t + 1) * T], xbuf[0][:, :, :], reads=R_x[0])
    S.barrier()
    S.replay()
    nc._dump_names = dumps
    return nc


_CACHE = {}


def _get_nc(stop_after=None, dump_tile=None):
    key = (stop_after, dump_tile)
    if key not in _CACHE:
        _CACHE[key] = build(stop_after=stop_after, dump_tile=dump_tile)
    return _CACHE[key]


def kernel(x, mem, norm_mix_g, w_in, q_norm_g, k_norm_g, sinks, conv_w, conv_b, conv_ln_g, conv_ln_b, w_out,
           norm_x_g, norm_mem_g, wq_x, wkv_x, xq_norm_g, xk_norm_g, wo_x, norm_ffn_g, w_gate_up, w_down,
           _stop_after=None, _dump_tile=None):
    inp = dict(norm_mix_g=norm_mix_g, q_norm_g=q_norm_g, k_norm_g=k_norm_g, sinks=sinks, conv_w=conv_w, conv_b=conv_b,
               conv_ln_g=conv_ln_g, conv_ln_b=conv_ln_b, norm_x_g=norm_x_g, norm_mem_g=norm_mem_g,
               xq_norm_g=xq_norm_g, xk_norm_g=xk_norm_g, norm_ffn_g=norm_ffn_g)
    inp = {k: np.asarray(v, np.float32) for k, v in inp.items()}
    vecs = _host_vecs(inp)
    cb, cf = _host_consts()
    f = lambda a: np.ascontiguousarray(np.asarray(a, np.float32))
    shared = {"w_in": f(w_in), "w_out": f(w_out), "wq_x": f(wq_x), "wkv_x": f(wkv_x), "wo_x": f(wo_x),
              "w_gate_up": f(w_gate_up), "w_down": f(w_down), "vecs": vecs, "cbf": cb, "cf32": cf}
    x = np.asarray(x, np.float32)
    mem = np.asarray(mem, np.float32)
    B = x.shape[0]
    in_maps = []
    for b in range(B):
        m = dict(shared)
        m["x"] = np.ascontiguousarray(x[b])
        m["mem"] = np.ascontiguousarray(mem[b])
        in_maps.append(m)
    nc = _get_nc(_stop_after, _dump_tile)
    res = run_bass_kernel_spmd(nc, in_maps, core_ids=list(range(B)))
    if _dump_tile is not None:
        kernel.last_dumps = {n: np.asarray(res.results[0]["dump_" + n]) for n in nc._dump_names}
    if _stop_after is not None:
        return np.stack([np.asarray(r["dbg"]).T for r in res.results], axis=0)
    return np.stack([np.asarray(r["y"]) for r in res.results], axis=0).astype(np.float32)
```

```python
import numpy as np
import ml_dtypes
import concourse.bass as bass
import concourse.mybir as mybir
from concourse.bass_utils import run_bass_kernel_spmd

F32 = mybir.dt.float32
BF16 = mybir.dt.bfloat16
AF = mybir.ActivationFunctionType
ALU = mybir.AluOpType
AX = mybir.AxisListType

SEQ = 4096
D = 1024
T = 512
NT = SEQ // T
DEPTH = 2
EPS = 1e-6
NEGM = -30000.0
HF = 1408
NJ = 11
JS = 6
TD = 12


class Tok:
    __slots__ = ("sem", "key", "val", "clock")

    def __init__(self, sem, val, clock):
        self.sem = sem
        self.key = sem.num
        self.val = val
        self.clock = clock


class Res:
    __slots__ = ("name", "w", "r", "pw", "touch")

    def __init__(self, name=""):
        self.name = name
        self.w = []
        self.r = {}
        self.pw = []
        self.touch = 0

    def new_epoch(self):
        self.pw = list(self.r.values()) + list(self.w) + list(self.pw)
        self.w = []
        self.r = {}


class Eng:
    def __init__(self, name, sem):
        self.name = name
        self.sem = sem
        self.cnt = 0
        self.seen = {}
        self.ops = []


class DmaRing:
    def __init__(self, sems):
        self.sems = sems
        self.tot = [0] * len(sems)
        self.last = [None] * len(sems)
        self.i = 0


class Sched:
    def __init__(self, nc, n_dma_sems=12):
        self.nc = nc
        self.E = {}
        for n in ("pe", "act", "dve", "pool"):
            self.E[n] = Eng(n, nc.alloc_semaphore(name=f"sem_{n}"))
        self.E["sp"] = Eng("sp", None)
        self.rings = {}
        for q in ("sp", "pool"):
            self.rings[q] = DmaRing([nc.alloc_semaphore(name=f"dq_{q}_{i}") for i in range(n_dma_sems)])

    def _wait(self, e, tok):
        if e.seen.get(tok.key, 0) >= tok.val:
            return
        e.ops.append(("wait", tok.sem, tok.val))
        seen = e.seen
        for k, v in tok.clock.items():
            if seen.get(k, 0) < v:
                seen[k] = v
        seen[tok.key] = tok.val

    def _dep1(self, en, e, d):
        if en == "pe" and d.sem is e.sem:
            return
        self._wait(e, d)

    def _deps(self, en, e, reads, writes, pwrites):
        for r in reads:
            for d in r.w:
                self._dep1(en, e, d)
        for w in writes:
            for d in w.w:
                self._dep1(en, e, d)
            for d in w.r.values():
                self._dep1(en, e, d)
            for d in w.pw:
                self._dep1(en, e, d)
        for w in pwrites:
            for d in w.pw:
                self._dep1(en, e, d)
            for d in w.r.values():
                self._dep1(en, e, d)

    def _post(self, tok, reads, writes, pwrites):
        self.opidx = getattr(self, "opidx", 0) + 1
        for r in reads:
            r.touch = self.opidx
        for w in writes:
            w.touch = self.opidx
        for w in pwrites:
            w.touch = self.opidx
        for r in reads:
            old = r.r.get(tok.key)
            if old is None or old.val < tok.val:
                r.r[tok.key] = tok
        for w in writes:
            w.w = [tok]
            w.r = {}
            w.pw = []
        for w in pwrites:
            w.w.append(tok)

    def op(self, en, fn, reads=(), writes=(), pwrites=(), signal=True):
        e = self.E[en]
        self._deps(en, e, reads, writes, pwrites)
        if signal:
            e.cnt += 1
            val = e.cnt
        else:
            val = e.cnt + 1
        e.ops.append(("op", fn, signal))
        tok = Tok(e.sem, val, dict(e.seen))
        self._post(tok, reads, writes, pwrites)
        return tok

    def dma(self, q, out_ap, in_ap, reads=(), writes=(), pwrites=()):
        e = self.E[q]
        self._deps(q, e, reads, writes, pwrites)
        ring = self.rings[q]
        i = ring.i
        ring.i = (i + 1) % len(ring.sems)
        if ring.last[i] is not None:
            self._wait(e, ring.last[i])
        ring.tot[i] += 16
        e.ops.append(("dma", out_ap, in_ap, ring.sems[i]))
        tok = Tok(ring.sems[i], ring.tot[i], dict(e.seen))
        ring.last[i] = tok
        self._post(tok, reads, writes, pwrites)
        return tok

    def barrier(self):
        toks = []
        for n, e in self.E.items():
            if e.sem is not None and e.cnt > 0:
                toks.append(Tok(e.sem, e.cnt, {}))
        for q, ring in self.rings.items():
            for t in ring.last:
                if t is not None:
                    toks.append(t)
        for n, e in self.E.items():
            for t in toks:
                if t.sem is e.sem and n == "pe":
                    continue
                self._wait(e, t)

    def replay(self):
        nc = self.nc
        E = self.E

        def run(e, h):
            for o in e.ops:
                k = o[0]
                if k == "wait":
                    h.wait_ge(o[1], o[2])
                elif k == "op":
                    ins = o[1](h)
                    if o[2]:
                        ins.then_inc(e.sem, 1)
                else:
                    h.dma_start(out=o[1], in_=o[2]).then_inc(o[3], 16)

        with nc.Block() as block:
            @block.tensor
            def _(h):
                run(E["pe"], h)

            @block.scalar
            def _(h):
                run(E["act"], h)

            @block.vector
            def _(h):
                run(E["dve"], h)

            @block.gpsimd
            def _(h):
                run(E["pool"], h)

            @block.sync
            def _(h):
                run(E["sp"], h)


V_NMIX, V_NX, V_NFFN, V_NMEM = 0, 8, 16, 24
V_GQ, V_GK = 32, 33
V_CB, V_LNG, V_LNB = 34, 38, 42
V_GXQ, V_GXK = 46, 48
V_SINKP = 50
V_SINKR = 54
V_CW = 62
V_KEEP = 186
V_GQR, V_GKR = 186, 250
V_GXQR, V_GXKR = 314, 570
NV = 826

C_ID = 0
C_1024 = 128
C_BD64 = 256
C_256 = 384
C_512 = 512
C_ONES = 640
C_OPAD = 768
C_BIAS = 960
NCB = C_BIAS + 8 * 256


def _host_consts():
    cb = np.zeros((128, NCB), np.float32)
    cb[:, C_ID:C_ID + 128] = np.eye(128)
    cb[:, C_1024:C_1024 + 128] = 1.0 / 1024
    bd = np.zeros((128, 128), np.float32)
    bd[:64, :64] = 1.0 / 64
    bd[64:, 64:] = 1.0 / 64
    cb[:, C_BD64:C_BD64 + 128] = bd
    cb[:, C_256:C_256 + 128] = 1.0 / 256
    cb[:, C_512:C_512 + 128] = 1.0 / 512
    cb[:, C_ONES:C_ONES + 128] = 1.0
    cb[:, C_OPAD + 64:C_OPAD + 128] = 1.0
    s = np.arange(128)[:, None].astype(np.float32)
    q = np.arange(128)[None, :].astype(np.float32)
    for kb in range(2):
        for g in range(2):
            for par in range(2):
                ti = (g * 2 + par) * 2 + kb
                tab = np.zeros((128, 2, 128), np.float32)
                for j in range(2):
                    h = 4 * g + 2 * j + par
                    slope = 2.0 ** (-(h + 1))
                    if kb == 0:
                        dist = q + 128 - s
                        valid = s > q
                    else:
                        dist = q - s
                        valid = s <= q
                    tab[:, j, :] = np.where(valid, -slope * dist, NEGM)
                cb[:, C_BIAS + ti * 256:C_BIAS + (ti + 1) * 256] = tab.reshape(128, 256)
    return cb.astype(ml_dtypes.bfloat16), np.eye(128, dtype=np.float32)


def _host_vecs(inp):
    out = np.zeros((DEPTH, 128, NV), np.float32)
    p = np.arange(128)
    for l in range(DEPTH):
        v = out[l]

        def cols(vec, n):
            return np.asarray(vec, np.float32).reshape(n, 128).T
        v[:, V_NMIX:V_NMIX + 8] = cols(inp["norm_mix_g"][l], 8)
        v[:, V_NX:V_NX + 8] = cols(inp["norm_x_g"][l], 8)
        v[:, V_NFFN:V_NFFN + 8] = cols(inp["norm_ffn_g"][l], 8)
        v[:, V_NMEM:V_NMEM + 8] = cols(inp["norm_mem_g"][l], 8)
        v[:, V_GQ] = np.asarray(inp["q_norm_g"][l])[p % 64]
        v[:, V_GK] = np.asarray(inp["k_norm_g"][l])[p % 64]
        v[:, V_CB:V_CB + 4] = cols(inp["conv_b"][l], 4)
        v[:, V_LNG:V_LNG + 4] = cols(inp["conv_ln_g"][l], 4)
        v[:, V_LNB:V_LNB + 4] = cols(inp["conv_ln_b"][l], 4)
        v[:, V_GXQ:V_GXQ + 2] = cols(inp["xq_norm_g"][l], 2)
        v[:, V_GXK:V_GXK + 2] = cols(inp["xk_norm_g"][l], 2)
        sk = np.asarray(inp["sinks"][l], np.float32)
        for g in range(2):
            for j in range(2):
                v[:64, V_SINKP + 2 * g + j] = sk[4 * g + 2 * j]
                v[64:, V_SINKP + 2 * g + j] = sk[4 * g + 2 * j + 1]
        v[:, V_SINKR:V_SINKR + 8] = sk[None, :]
        cw = np.asarray(inp["conv_w"][l], np.float32)
        for cc in range(4):
            v[:, V_CW + cc * 31:V_CW + (cc + 1) * 31] = cw[:, cc * 128:(cc + 1) * 128].T
        v[:, V_GQR:V_GQR + 64] = np.asarray(inp["q_norm_g"][l])[None, :]
        v[:, V_GKR:V_GKR + 64] = np.asarray(inp["k_norm_g"][l])[None, :]
        v[:, V_GXQR:V_GXQR + 256] = np.asarray(inp["xq_norm_g"][l])[None, :]
        v[:, V_GXKR:V_GXKR + 256] = np.asarray(inp["xk_norm_g"][l])[None, :]
    return out


def build(stop_after=None, native_silu=True, dump_tile=None):
    nc = bass.Bass("TRN2", target_bir_lowering=False)
    S = Sched(nc)
    dumps = []

    def dump(name, ap, reads):
        shape = list(ap.shape)
        d = nc.dram_tensor("dump_" + name, shape, ap.dtype, kind="ExternalOutput").ap()
        S.dma("sp", d, ap, reads=reads)
        dumps.append(name)

    def din(name, shape, dt=F32):
        return nc.dram_tensor(name, list(shape), dt, kind="ExternalInput").ap()

    x_d = din("x", [SEQ, D])
    mem_d = din("mem", [256, D])
    w_in_d = din("w_in", [DEPTH, D, 1792])
    w_out_d = din("w_out", [DEPTH, D, D])
    wq_d = din("wq_x", [DEPTH, D, D])
    wkv_d = din("wkv_x", [DEPTH, D, 2 * D])
    wo_d = din("wo_x", [DEPTH, D, D])
    wgu_d = din("w_gate_up", [DEPTH, D, 2 * 2816])
    wd_d = din("w_down", [DEPTH, 2816, D])
    vecs_d = din("vecs", [DEPTH, 128, NV])
    cbf_d = din("cbf", [128, NCB], BF16)
    cf_d = din("cf32", [128, 128])
    y_d = nc.dram_tensor("y", [SEQ, D], F32, kind="ExternalOutput").ap()
    xs_d = nc.dram_tensor("xs", [D, SEQ], F32, kind="Internal").ap()
    hs_d = nc.dram_tensor("hs", [D, SEQ], BF16, kind="Internal").ap()
    kvs_d = nc.dram_tensor("kvs", [128, 4096], BF16, kind="Internal").ap()
    xs_v = xs_d.rearrange("(c p) s -> p c s", p=128)
    hs_v = hs_d.rearrange("(c p) s -> p c s", p=128)

    NA = 16 * 1024
    NB = 8 * 1024 + 4 * 31 * 128 + 512
    NB = max(NB, 8 * 2 * HF)
    WA = nc.alloc_sbuf_tensor("WA", [128, NA], BF16)
    WBt = nc.alloc_sbuf_tensor("WBt", [128, NB], BF16)
    RA = [Res(f"WA{i}") for i in range(4)]
    RB = [Res("WB0"), Res("WB1")]
    xbuf = [nc.alloc_sbuf_tensor(f"xb{i}", [128, 8, T], F32) for i in range(2)]
    R_x = [[Res(f"x{i}_{c}") for c in range(8)] for i in range(2)]
    hT = nc.alloc_sbuf_tensor("hT", [128, 8, T], BF16)
    R_h = [Res(f"h{c}") for c in range(8)]
    cbf = nc.alloc_sbuf_tensor("cbf_sb", [128, NCB], BF16)
    identf = nc.alloc_sbuf_tensor("identf", [128, 128], F32)
    R_c = Res("consts")
    pv = nc.alloc_sbuf_tensor("pv", [128, DEPTH, V_KEEP], F32)
    NDV = 16
    dv = nc.alloc_sbuf_tensor("dv", [128, DEPTH, NDV], F32)
    R_v = Res("vecs")
    kvsb = nc.alloc_sbuf_tensor("kvsb", [128, 4096], BF16)
    khx = kvsb[:, 0:2048].rearrange("p (m k) -> p m k", m=8)
    vmem = kvsb[:, 2048:4096].rearrange("p (m d) -> p m d", m=2)
    R_kv = Res("kv")
    stage = [nc.alloc_sbuf_tensor(f"stage{i}", [128, 1024], F32) for i in range(2)]
    R_stage = [Res("stage0"), Res("stage1")]
    TBN = 60 * 1024
    TB = nc.alloc_sbuf_tensor("TB", [128, TBN // 2], BF16)
    PS = [nc.alloc_psum_tensor(f"ps{i}", [128, 512], F32) for i in range(8)]
    R_ps = [Res(f"ps{i}") for i in range(8)]
    R_psh = [[Res(f"psh{i}_{k}") for k in range(2)] for i in range(8)]
    st = {"ps": 0, "tb": 0, "nfull": 8, "ph": 0}

    def set_psum_split(nfull):
        st["nfull"] = nfull
        st["ps"] = 0
        st["ph"] = 0

    def ps():
        i = min(range(8), key=lambda k: R_ps[k].touch)
        R_ps[i].touch = getattr(S, "opidx", 0) + 0.5
        return PS[i], R_ps[i]

    def psh():
        nh = (8 - st["nfull"]) * 2
        k = st["ph"]
        st["ph"] = (k + 1) % nh
        b = st["nfull"] + k // 2
        return PS[b][:, (k % 2) * 256:(k % 2) * 256 + 256], R_psh[b][k % 2]

    def tb_reset():
        st["tb"] = 0

    def tb(shape, dt):
        n = int(np.prod(shape))
        nb = n * (4 if dt == F32 else 2)
        nb = (nb + 63) // 64 * 64
        off = st["tb"]
        assert off + nb <= TBN, (off, nb, TBN)
        st["tb"] = off + nb
        v = TB[:, off // 2:(off + nb) // 2]
        if dt == F32:
            v = v.bitcast(F32)
        v = v[:, 0:n]
        if len(shape) == 2:
            return v.rearrange("p (a b) -> p a b", a=shape[0])
        if len(shape) == 3:
            return v.rearrange("p (a b c) -> p a b c", a=shape[0], b=shape[1])
        return v

    def ACT(out, in_, func, R, W, bias=None, scale=None, PW=()):
        kw = {}
        if bias is not None:
            kw["bias"] = bias
        if scale is not None:
            kw["scale"] = scale
        return S.op("act", lambda h: h.activation(out, in_, func, **kw), reads=R, writes=W, pwrites=PW)

    def TT(en, out, a, b, op, R, W, PW=()):
        return S.op(en, lambda h: h.tensor_tensor(out, a, b, op), reads=R, writes=W, pwrites=PW)

    def STT(out, in0, scalar, in1, op0, op1, R, W, PW=()):
        return S.op("dve", lambda h: h.scalar_tensor_tensor(out, in0, scalar, in1, op0, op1), reads=R, writes=W, pwrites=PW)

    def TS(out, in0, s1, op0, R, W, s2=None, op1=None, PW=()):
        if op1 is None:
            return S.op("dve", lambda h: h.tensor_scalar(out, in0, s1, None, op0), reads=R, writes=W, pwrites=PW)
        return S.op("dve", lambda h: h.tensor_scalar(out, in0, s1, s2, op0, op1), reads=R, writes=W, pwrites=PW)

    def CP(en, out, in_, R, W, PW=()):
        if en == "act":
            return S.op("act", lambda h: h.copy(out, in_), reads=R, writes=W, pwrites=PW)
        return S.op(en, lambda h: h.tensor_copy(out, in_), reads=R, writes=W, pwrites=PW)

    def MM(out, lhsT, rhs, start, stop, R, W, signal):
        return S.op("pe", lambda h: h.matmul(out, lhsT, rhs, start=start, stop=stop), reads=R, writes=W, signal=signal)

    def TR(out, in_, R, W, signal):
        return S.op("pe", lambda h: h.transpose(out, in_, identf[:, :]), reads=list(R) + [R_c], writes=W, signal=signal)

    def cmat(off, n=128):
        return cbf[:, off:off + n]

    def vcol(l, c, n=1):
        return pv[:, l, c:c + n]

    def dcol(l, c, n=1):
        return dv[:, l, c:c + n]

    def rsqrt_from(out, tmp, src, l, R, Rtmp, W):
        ACT(tmp, src, AF.Ln, R=list(R) + [R_v], W=[Rtmp], bias=dcol(l, 9))
        ACT(out, tmp, AF.Exp, R=[Rtmp], W=W, scale=-0.5)

    def compute_barrier():
        toks = []
        for n in ("pe", "act", "dve"):
            e = S.E[n]
            if e.cnt > 0:
                toks.append(Tok(e.sem, e.cnt, {}))
        for n in ("pe", "act", "dve"):
            e = S.E[n]
            for t_ in toks:
                if t_.sem is e.sem and n == "pe":
                    continue
                S._wait(e, t_)

    S.dma("sp", cbf[:, :], cbf_d, writes=[R_c])
    S.dma("sp", identf[:, :], cf_d, pwrites=[R_c])
    tb_reset()
    vfull = tb([DEPTH, NV], F32)
    R_vf = Res("vfull")
    S.dma("sp", vfull, vecs_d.rearrange("l p n -> p l n"), writes=[R_vf])
    CP("dve", pv[:, :, :], vfull[:, :, 0:V_KEEP], R=[R_vf], W=[R_v])
    S.op("dve", lambda h: h.memset(dv[:, :, :], 0.0), writes=[R_v])
    S.op("dve", lambda h: h.memset(dv[:, :, 9:10], EPS), writes=[R_v])
    S.op("dve", lambda h: h.memset(dv[:, :, 10:11], 1.0), writes=[R_v])
    sc1 = tb([256], F32)
    sc2 = tb([8], F32)
    R_s1 = Res("sc1")
    R_s2 = Res("sc2")
    for l in range(DEPTH):
        TS(dcol(l, 0), vcol(l, V_GQ), 0.125, ALU.mult, R=[R_v], W=[R_v])
        TS(dcol(l, 7, 2), vcol(l, V_GXQ, 2), 1.0 / 16, ALU.mult, R=[R_v], W=[R_v])
        TT("dve", sc1[:, 0:64], vfull[:, l, V_GQR:V_GQR + 64], vfull[:, l, V_GKR:V_GKR + 64], ALU.mult, R=[R_vf], W=[R_s1])
        TT("dve", sc1[:, 0:64], sc1[:, 0:64], sc1[:, 0:64], ALU.mult, R=[R_s1], W=[R_s1])
        S.op("dve", lambda h: h.reduce_max(sc2[:, 0:1], sc1[:, 0:64], AX.X), reads=[R_s1], writes=[R_s2])
        ACT(sc2[:, 1:2], sc2[:, 0:1], AF.Ln, R=[R_s2], W=[R_s2])
        ACT(sc2[:, 2:3], sc2[:, 1:2], AF.Exp, R=[R_s2], W=[R_s2], scale=0.5)
        S.op("dve", lambda h, l=l: h.reduce_max(sc2[:, 3:4], vfull[:, l, V_SINKR:V_SINKR + 8], AX.X), reads=[R_vf, R_s2], writes=[R_s2])
        STT(sc2[:, 4:5], sc2[:, 2:3], 8.0, sc2[:, 3:4], ALU.mult, ALU.max, R=[R_s2], W=[R_s2])
        TS(dcol(l, 1), sc2[:, 4:5], -1.0, ALU.mult, R=[R_s2, R_v], W=[R_v])
        ACT(dcol(l, 2, 4), vcol(l, V_SINKP, 4), AF.Exp, R=[R_v], W=[R_v], bias=dcol(l, 1))
        TT("dve", sc1[:, :], vfull[:, l, V_GXQR:V_GXQR + 256], vfull[:, l, V_GXKR:V_GXKR + 256], ALU.mult, R=[R_vf, R_s1], W=[R_s1])
        TT("dve", sc1[:, :], sc1[:, :], sc1[:, :], ALU.mult, R=[R_s1], W=[R_s1])
        S.op("dve", lambda h: h.reduce_max(sc2[:, 5:6], sc1[:, :], AX.X), reads=[R_s1, R_s2], writes=[R_s2])
        ACT(sc2[:, 6:7], sc2[:, 5:6], AF.Ln, R=[R_s2], W=[R_s2])
        ACT(sc2[:, 7:8], sc2[:, 6:7], AF.Exp, R=[R_s2], W=[R_s2], scale=0.5)
        TS(dcol(l, 6), sc2[:, 7:8], -16.0, ALU.mult, R=[R_s2, R_v], W=[R_v])

    wkv = WA[:, 0:8 * 2048].rearrange("p (c n) -> p c n", c=8)
    memsb = xbuf[0][:, 0:4, :].rearrange("p a b -> p (a b)").rearrange("p (m d) -> p m d", m=2)
    R_mem = Res("mem")
    S.dma("sp", memsb, mem_d.rearrange("(m p) d -> p m d", p=128), writes=[R_mem])
    msq = tb([1024], F32)
    R_msq = Res("msq")
    mst = tb([8], F32)
    R_mst = Res("mst")
    mn = xbuf[1][:, 0:4, :].rearrange("p a b -> p (a b)").rearrange("p (m d) -> p m d", m=2)
    R_mn = Res("mn")
    for mb in range(2):
        ACT(msq[:, :], memsb[:, mb, :], AF.Square, R=[R_mem], W=[R_msq])
        S.op("dve", lambda h, mb=mb: h.reduce_sum(mst[:, mb:mb + 1], msq[:, :], AX.X), reads=[R_msq, R_mst], writes=[R_mst])
    TS(mst[:, 2:4], mst[:, 0:2], 1.0 / 1024, ALU.mult, R=[R_mst], W=[R_mst])
    ACT(mst[:, 4:6], mst[:, 2:4], AF.Ln, R=[R_mst, R_v], W=[R_mst], bias=dcol(0, 9))
    ACT(mst[:, 6:8], mst[:, 4:6], AF.Exp, R=[R_mst], W=[R_mst], scale=-0.5)
    for mb in range(2):
        TS(mn[:, mb, :], memsb[:, mb, :], mst[:, 6 + mb:7 + mb], ALU.mult, R=[R_mem, R_mst], W=[R_mn])
    memT = tb([8, 256], BF16)
    R_mT = Res("memT")
    ksq = tb([2, 256], BF16)
    R_ksq = Res("ksq")
    krs = tb([256], F32)
    R_krs = Res("krs")
    klt = tb([256], F32)
    R_klt = Res("klt")
    for l in (1, 0):
        for r_ in RA:
            r_.new_epoch()
        R_mT.new_epoch()
        R_kv.new_epoch()
        for c in range(8):
            S.dma("pool", wkv[:, c, :], wkv_d[l, c * 128:(c + 1) * 128, :], pwrites=RA)
        for c in range(8):
            p, Rp = ps()
            for mb in range(2):
                TR(p[:, mb * 128:(mb + 1) * 128], mn[:, mb, c * 128:(c + 1) * 128], R=[R_mn], W=[Rp], signal=(mb == 1))
            TS(memT[:, c, :], p[:, 0:256], vcol(l, V_NMEM + c), ALU.mult, R=[Rp, R_v], W=[], PW=[R_mT])
        for h_ in range(4):
            pab = []
            R_ksq.new_epoch()
            for a in range(2):
                p, Rp = ps()
                m = 2 * h_ + a
                for c in range(8):
                    MM(p[:, 0:256], wkv[:, c, m * 128:(m + 1) * 128], memT[:, c, :], c == 0, c == 7, R=RA + [R_mT], W=[Rp], signal=(c == 7))
                ACT(ksq[:, a, :], p[:, 0:256], AF.Square, R=[Rp], W=[], PW=[R_ksq])
                pab.append((p, Rp))
            pm, Rpm = ps()
            for a in range(2):
                MM(pm[:, 0:256], cmat(C_256), ksq[:, a, :], a == 0, a == 1, R=[R_ksq, R_c], W=[Rpm], signal=(a == 1))
            rsqrt_from(krs[:, :], klt[:, :], pm[:, 0:256], l, R=[Rpm], Rtmp=R_klt, W=[R_krs])
            for a in range(2):
                p, Rp = pab[a]
                STT(khx[:, 2 * h_ + a, :], p[:, 0:256], vcol(l, V_GXK + a), krs[:, :], ALU.mult, ALU.mult, R=[Rp, R_krs, R_v], W=[], PW=[R_kv])
        for mb in range(2):
            for n2 in range(2):
                p, Rp = ps()
                for c in range(8):
                    MM(p[:, :], memT[:, c, mb * 128:(mb + 1) * 128], wkv[:, c, 1024 + n2 * 512:1024 + (n2 + 1) * 512], c == 0, c == 7, R=RA + [R_mT], W=[Rp], signal=(c == 7))
                CP("act", vmem[:, mb, n2 * 512:(n2 + 1) * 512], p[:, :], R=[Rp], W=[], PW=[R_kv])
        if l == 1:
            S.dma("sp", kvs_d, kvsb[:, :], reads=[R_kv])
    compute_barrier()
    S.barrier()
    for r in (R_kv, R_v, R_c, R_mT):
        r.w = []
        r.r = {}
        r.pw = []

    def wview(region, off, c, n):
        t_ = WA if region == "A" else WBt
        return t_[:, off:off + c * n].rearrange("p (c n) -> p c n", c=c)

    w_in_v = wview("A", 0, 8, 2048)
    w_out_v = wview("B", 0, 8, 1024)
    _dgo = 8 * 1024
    diagw_v = WBt[:, _dgo:_dgo + 4 * 31 * 128].rearrange("p (a j m) -> p a j m", a=4, j=31)
    diagw_f = WBt[:, _dgo:_dgo + 4 * 31 * 128].rearrange("p (a m) -> p a m", a=124)
    wq_v = wview("A", 0, 8, 1024)
    wo_v = wview("A", 8 * 1024, 8, 1024)
    wgu_v = wview("B", 0, 8, 2 * HF)
    wd_v = WA[:, 0:NJ * 1024].rearrange("p (j n) -> p j n", j=NJ)
    W = {"M": (w_in_v, w_out_v, diagw_v), "X": (wq_v, wo_v), "FA": (wgu_v, wd_v), "FB": (wgu_v, wd_v)}

    def load_region(kind, l, region):
        if kind == "M" and region == "A":
            src = w_in_d[l].rearrange("(c p) n -> p c n", p=128)
            for (bi_, c0, c1) in ((0, 0, 512), (1, 512, 768), (3, 1280, 1792), (2, 768, 1280)):
                RA[bi_].new_epoch()
                S.dma("pool", w_in_v[:, :, c0:c1], src[:, :, c0:c1], pwrites=[RA[bi_]])
                if bi_ == 1:
                    for (dst, sc_) in ((1792, 512), (1856, 512), (1920, 576), (1984, 576)):
                        S.op("pool", lambda h, dst=dst, sc_=sc_: h.tensor_copy(w_in_v[:, :, dst:dst + 64], w_in_v[:, :, sc_:sc_ + 64]),
                             reads=[RA[1]], pwrites=[RA[1]])
        elif kind == "M" and region == "B1":
            for r_ in RB:
                r_.new_epoch()
            for c in range(8):
                S.dma("pool", w_out_v[:, c, :], w_out_d[l, c * 128:(c + 1) * 128, :], pwrites=RB)
            S.op("dve", lambda h: h.tensor_tensor(
                diagw_f, cmat(C_ID).unsqueeze(1).to_broadcast([128, 124, 128]),
                pv[:, l, V_CW:V_CW + 124].unsqueeze(2).to_broadcast([128, 124, 128]), ALU.mult),
                reads=[R_c, R_v], pwrites=RB)
        elif kind == "X" and region == "A":
            for r_ in RA:
                r_.new_epoch()
            for c in range(8):
                S.dma("pool", wq_v[:, c, :], wq_d[l, c * 128:(c + 1) * 128, :], pwrites=RA)
            for c in range(8):
                S.dma("pool", wo_v[:, c, :], wo_d[l, c * 128:(c + 1) * 128, :], pwrites=RA)
        elif kind in ("FA", "FB") and region in ("B0", "B1"):
            f0 = (0 if kind == "FA" else 1) * HF
            hb = 0 if region == "B0" else 1
            j0, j1 = (0, JS) if hb == 0 else (JS, NJ)
            RB[hb].new_epoch()
            for c in range(8):
                S.dma("pool", wgu_v[:, c, j0 * 128:j1 * 128], wgu_d[l, c * 128:(c + 1) * 128, f0 + j0 * 128:f0 + j1 * 128], pwrites=[RB[hb]])
                S.dma("pool", wgu_v[:, c, HF + j0 * 128:HF + j1 * 128],
                      wgu_d[l, c * 128:(c + 1) * 128, 2816 + f0 + j0 * 128:2816 + f0 + j1 * 128], pwrites=[RB[hb]])
        elif kind in ("FA", "FB") and region == "A":
            f0 = (0 if kind == "FA" else 1) * HF
            for r_ in RA:
                r_.new_epoch()
            wdv = wd_d[l, f0:f0 + HF, :].rearrange("(j p) n -> p j n", p=128)
            for (j0, j1) in ((0, 4), (4, 8), (8, 11)):
                S.dma("pool", wd_v[:, j0:j1, :], wdv[:, j0:j1, :], pwrites=RA)

    R_xs = [Res(f"xs{t}") for t in range(NT)]
    R_hs = [Res(f"hs{t}") for t in range(NT)]

    def load_x(t, bi):
        S.dma("sp", xbuf[bi][:, :, :], xs_v[:, :, t * T:(t + 1) * T], reads=[R_xs[t]], writes=R_x[bi])

    def store_x(t, bi):
        S.dma("sp", xs_v[:, :, t * T:(t + 1) * T], xbuf[bi][:, :, :], reads=R_x[bi], writes=[R_xs[t]])

    def store_h(t):
        S.dma("sp", hs_v[:, :, t * T:(t + 1) * T], hT[:, :, :], reads=R_h, writes=[R_hs[t]])

    def load_h(t):
        S.dma("sp", hT[:, :, :], hs_v[:, :, t * T:(t + 1) * T], reads=[R_hs[t]], writes=R_h)

    class NextPrep:
        def __init__(self, kind, l):
            self.kind = kind
            self.l = l
            self.gbase = {"M": V_NMIX, "X": V_NX, "FA": V_NFFN, "FB": None}[kind]

        def load(self):
            load_x(0, 0)

        def squares(self, sq, R_sq, eng="act"):
            if self.gbase is not None:
                norm_squares(0, sq, R_sq, eng)

        def finish(self, sq, R_sq, tmpf):
            if self.gbase is not None:
                norm_finish(self.l, 0, self.gbase, sq, R_sq, tmpf)
                if self.kind == "FA":
                    store_h(0)
            else:
                load_h(0)

        def stats(self, sq, R_sq, tmpf, rstd, R_rstd):
            if self.gbase is not None:
                norm_stats(self.l, sq, R_sq, tmpf, rstd, R_rstd)

        def apply(self, rstd, R_rstd):
            if self.gbase is not None:
                norm_apply(self.l, 0, self.gbase, rstd, R_rstd)
                if self.kind == "FA":
                    store_h(0)
            else:
                load_h(0)

    scnt = {"i": 0}

    def tokmajor_dma(t, k):
        si = k % 2
        r0 = t * T + k * 128
        S.dma("sp", stage[si][:, :], x_d[r0:r0 + 128, :], writes=[R_stage[si]])

    def tokmajor_tr(t, bi, k):
        si = k % 2
        for i in range(2):
            p, Rp = ps()
            for mm_ in range(4):
                c = 4 * i + mm_
                TR(p[:, mm_ * 128:(mm_ + 1) * 128], stage[si][:, c * 128:(c + 1) * 128], R=[R_stage[si]], W=[Rp], signal=(mm_ == 3))
            pview = p[:, :].rearrange("p (c n) -> p c n", c=4)
            if k == 0:
                for mm_ in range(4):
                    R_x[bi][4 * i + mm_].new_epoch()
            CP("act" if i else "dve", xbuf[bi][:, 4 * i:4 * i + 4, k * 128:(k + 1) * 128], pview, R=[Rp], W=[],
               PW=[R_x[bi][4 * i + mm_] for mm_ in range(4)])

    def load_x_tokmajor(t, bi):
        for k in range(4):
            tokmajor_dma(t, k)
            tokmajor_tr(t, bi, k)

    def transpose_out(t, bi):
        for k in range(4):
            si = scnt["i"] % 2
            scnt["i"] += 1
            R_stage[si].new_epoch()
            for i in range(2):
                p, Rp = ps()
                for mm_ in range(4):
                    m = 4 * i + mm_
                    TR(p[:, mm_ * 128:(mm_ + 1) * 128], xbuf[bi][:, m, k * 128:(k + 1) * 128], R=[R_x[bi][m]], W=[Rp], signal=(mm_ == 3))
                CP("act" if i else "dve", stage[si][:, i * 512:(i + 1) * 512], p[:, :], R=[Rp], W=[], PW=[R_stage[si]])
            r0 = t * T + k * 128
            S.dma("sp", y_d[r0:r0 + 128, :], stage[si][:, :], reads=[R_stage[si]])

    def norm_squares(bi, sq, R_sq, eng="act"):
        for c in range(8):
            if eng == "act":
                ACT(sq[:, c, :], xbuf[bi][:, c, :], AF.Square, R=[R_x[bi][c]], W=[R_sq[c]])
            else:
                TT("dve", sq[:, c, :], xbuf[bi][:, c, :], xbuf[bi][:, c, :], ALU.mult, R=[R_x[bi][c]], W=[R_sq[c]])

    def norm_finish(l, bi, gbase, sq, R_sq, tmpf):
        rstd, R_rstd = tmpf()
        lnt, R_lnt = tmpf()
        p, Rp = ps()
        for c in range(8):
            MM(p[:, :], cmat(C_1024), sq[:, c, :], c == 0, c == 7, R=[R_sq[c], R_c], W=[Rp], signal=(c == 7))
        rsqrt_from(rstd[:, :], lnt[:, :], p[:, :], l, R=[Rp], Rtmp=R_lnt, W=[R_rstd])
        for c in range(8):
            STT(hT[:, c, :], xbuf[bi][:, c, :], vcol(l, gbase + c), rstd[:, :], ALU.mult, ALU.mult,
                R=[R_x[bi][c], R_rstd, R_v], W=[R_h[c]])

    def norm_stats(l, sq, R_sq, tmpf, rstd, R_rstd):
        lnt, R_lnt = tmpf()
        p, Rp = ps()
        for c in range(8):
            MM(p[:, :], cmat(C_1024), sq[:, c, :], c == 0, c == 7, R=[R_sq[c], R_c], W=[Rp], signal=(c == 7))
        rsqrt_from(rstd[:, :], lnt[:, :], p[:, :], l, R=[Rp], Rtmp=R_lnt, W=[R_rstd])

    def norm_apply(l, bi, gbase, rstd, R_rstd):
        for c in range(8):
            STT(hT[:, c, :], xbuf[bi][:, c, :], vcol(l, gbase + c), rstd[:, :], ALU.mult, ALU.mult,
                R=[R_x[bi][c], R_rstd, R_v], W=[R_h[c]])

    def residual_chunk(bi, m, wmat, RW, nk, rhsT, R_rhs):
        p, Rp = ps()
        for k in range(nk):
            MM(p[:, :], wmat[:, k, m * 128:(m + 1) * 128], rhsT[:, k, :], k == 0, k == nk - 1,
               R=list(RW) + [R_rhs[k]], W=[Rp], signal=(k == nk - 1))
        TT("dve", xbuf[bi][:, m, :], xbuf[bi][:, m, :], p[:, :], ALU.add, R=[Rp, R_x[bi][m]], W=[R_x[bi][m]])

    sqpool = {"tiles": None, "i": 0}

    def ptile_big():
        tl = sqpool["tiles"]
        i = sqpool["i"]
        sqpool["i"] = (i + 1) % len(tl)
        return tl[i]

    def alloc_sqpool(n):
        sqpool["tiles"] = [(tb([T], BF16), Res(f"sqq{i}")) for i in range(n)]
        sqpool["i"] = 0

    def make_rot(tiles, ress):
        c = {"i": 0}

        def f():
            i = c["i"]
            c["i"] = (i + 1) % len(tiles)
            return tiles[i], ress[i]
        return f

    def phase_M(l, next_load, prepped, nextp):
        tb_reset()
        set_psum_split(8)
        w_in, w_out, diagw = W["M"]
        sq = tb([8, T], BF16)
        R_sq = [Res(f"sq{c}") for c in range(8)]
        qhat = tb([4, T], BF16)
        R_q = [Res(f"q{i}") for i in range(4)]
        khat = tb([2, 128 + T], BF16)
        R_k = Res("khat")
        vpad = tb([5, 2, 192], BF16)
        R_vp = Res("vpad")
        gbuf = tb([4, 30 + T], BF16)
        R_g = [Res(f"g{i}") for i in range(4)]
        mixT = tb([8, T], BF16)
        R_mix = [Res(f"mix{i}") for i in range(8)]
        ycv = tb([4, T], F32)
        R_y = [Res(f"y{i}") for i in range(4)]
        NTMP = 5
        tmp = make_rot([tb([T], F32) for _ in range(NTMP)], [Res(f"tmp{i}") for i in range(NTMP)])
        nrstd = tb([T], F32)
        R_nrstd = Res("nrstd")
        NP = 6
        ptile = make_rot([tb([T], BF16) for _ in range(NP)], [Res(f"P{i}") for i in range(NP)])
        alloc_sqpool(3)

        S.op("dve", lambda h: h.memset(vpad.rearrange("p a b c -> p (a b c)"), 0.0), writes=[R_vp])
        for cc in range(4):
            S.op("dve", lambda h, cc=cc: h.memset(gbuf[:, cc, 0:30], 0.0), writes=[R_g[cc]])

        def sigmoid_of(src, Rsrc):
            e, Re = tmp()
            ACT(e[:, :], src, AF.Exp, R=Rsrc, W=[Re], scale=-1.0)
            l_, Rl = tmp()
            ACT(l_[:, :], e[:, :], AF.Ln, R=[Re, R_v], W=[Rl], bias=dcol(l, 10))
            s_, Rs = tmp()
            ACT(s_[:, :], l_[:, :], AF.Exp, R=[Rl], W=[Rs], scale=-1.0)
            return s_, Rs

        def qk_mm(spec):
            col, kind, idx = spec
            p, Rp = ps()
            for c in range(8):
                MM(p[:, :], w_in[:, c, col:col + 128], hT[:, c, :], c == 0, c == 7, R=[RA[0 if kind == "q" else 1], R_h[c]], W=[Rp], signal=(c == 7))
            sqq, Rsqq = ptile_big()
            ACT(sqq[:, :], p[:, :], AF.Square, R=[Rp], W=[Rsqq])
            return (spec, p, Rp, sqq, Rsqq)

        def qk_fin(state):
            (col, kind, idx), p, Rp, sqq, Rsqq = state
            p2, Rp2 = ps()
            MM(p2[:, :], cmat(C_BD64), sqq[:, :], True, True, R=[Rsqq, R_c], W=[Rp2], signal=True)
            lt, Rlt = tmp()
            rs, Rrs = tmp()
            rsqrt_from(rs[:, :], lt[:, :], p2[:, :], l, R=[Rp2], Rtmp=Rlt, W=[Rrs])
            if kind == "q":
                STT(qhat[:, idx, :], p[:, :], dcol(l, 0), rs[:, :], ALU.mult, ALU.mult, R=[Rp, Rrs, R_v], W=[R_q[idx]])
            else:
                STT(khat[:, idx, 128:128 + T], p[:, :], vcol(l, V_GK), rs[:, :], ALU.mult, ALU.mult, R=[Rp, Rrs, R_v], W=[R_k])

        def scores(t, b, g):
            n = 4 * t + b
            kbs = ([0] if n > 0 else []) + [1]
            c0 = kbs[0] * 256
            banks = [ps(), ps()]
            for kb in kbs:
                for par in range(2):
                    p, Rp = banks[par]
                    pview = p[:, kb * 256:(kb + 1) * 256].rearrange("p (a b) -> p a b", a=2)
                    kblk = khat[64 * par:64 * par + 64, g, (b + kb) * 128:(b + kb + 1) * 128]
                    qrhs = qhat[64 * par:64 * par + 64, 2 * g:2 * g + 2, b * 128:(b + 1) * 128]
                    MM(pview, kblk, qrhs, True, False, R=[R_k, R_q[2 * g], R_q[2 * g + 1]], W=[Rp], signal=False)
                    ti = (g * 2 + par) * 2 + kb
                    MM(p[:, kb * 256:(kb + 1) * 256], cmat(C_ID), cbf[:, C_BIAS + ti * 256:C_BIAS + (ti + 1) * 256], False, True,
                       R=[R_c], W=[Rp], signal=(kb == 1))
            plist = []
            for par in range(2):
                p, Rp = banks[par]
                P_, RP_ = ptile()
                ACT(P_[:, c0:512], p[:, c0:512], AF.Exp, R=[Rp, R_v], W=[RP_], bias=dcol(l, 1))
                plist.append((par, P_, RP_))
            return (kbs, plist)

        def pv_den(t, b, g, sc):
            kbs, plist = sc
            pnd, Rpnd = ps()
            pn = pnd[:, 0:256]
            pd = pnd[:, 256:512]
            items = [(kb, par, P_, RP_) for (par, P_, RP_) in plist for kb in kbs]
            nmm = len(items)
            for i, (kb, par, P_, RP_) in enumerate(items):
                lo = 64 if par == 0 else 0
                MM(pn, vpad[:, b + kb, g, lo:lo + 128], P_[:, kb * 256:(kb + 1) * 256], i == 0, i == nmm - 1, R=[R_vp, RP_], W=[Rpnd], signal=False)
            for i, (kb, par, P_, RP_) in enumerate(items):
                lo = 64 if par == 0 else 0
                MM(pd, cbf[:, C_OPAD + lo:C_OPAD + lo + 128], P_[:, kb * 256:(kb + 1) * 256], i == 0, i == nmm - 1, R=[R_c, RP_], W=[Rpnd], signal=(i == nmm - 1))
            ld, Rld = tmp()
            R_first = True
            for j in range(2):
                S.op("act", lambda h, j=j: h.activation(ld[:, j * 128:(j + 1) * 128], pd[:, j * 128:(j + 1) * 128], AF.Ln,
                                                        bias=dcol(l, 2 + 2 * g + j)),
                     reads=[Rpnd, R_v], writes=[Rld] if j == 0 else [], pwrites=[] if j == 0 else [Rld])
            rd, Rrd = tmp()
            ACT(rd[:, 0:256], ld[:, 0:256], AF.Exp, R=[Rld], W=[Rrd], scale=-1.0)
            TT("dve", mixT[:, 2 * g:2 * g + 2, b * 128:(b + 1) * 128], pn.rearrange("p (a b) -> p a b", a=2),
               rd[:, 0:256].rearrange("p (a b) -> p a b", a=2), ALU.mult, R=[Rpnd, Rrd], W=[], PW=[R_mix[2 * g], R_mix[2 * g + 1]])

        def conv_taps_dve(cc):
            TS(ycv[:, cc, :], gbuf[:, cc, 0:T], vcol(l, V_CW + cc * 31), ALU.mult, R=[R_g[cc], R_v], W=[R_y[cc]])
            for j in range(1, TD):
                STT(ycv[:, cc, :], gbuf[:, cc, j:j + T], vcol(l, V_CW + cc * 31 + j), ycv[:, cc, :], ALU.mult, ALU.add,
                    R=[R_g[cc], R_v, R_y[cc]], W=[R_y[cc]])

        def conv_chunk(cc):
            p, Rp = ps()
            for j in range(TD, 31):
                MM(p[:, :], diagw[:, cc, j, :], gbuf[:, cc, j:j + T], j == TD, j == 30, R=RB + [R_g[cc]], W=[Rp], signal=(j == 30))
            STT(ycv[:, cc, :], p[:, :], vcol(l, V_CB + cc), ycv[:, cc, :], ALU.add, ALU.add, R=[Rp, R_v, R_y[cc]], W=[R_y[cc]])
            S.op("dve", lambda h, cc=cc: h.tensor_copy(gbuf[:, cc, 0:30], gbuf[:, cc, T:T + 30]), reads=[], writes=[R_g[cc]])

        ln_state = {}

        def ln_part1():
            for cc in range(4):
                CP("dve", sq[:, cc, :], ycv[:, cc, :], R=[R_y[cc]], W=[R_sq[cc]])
            pm, Rpm = ps()
            for cc in range(4):
                MM(pm[:, :], cmat(C_512), sq[:, cc, :], cc == 0, cc == 3, R=[R_sq[cc], R_c], W=[Rpm], signal=(cc == 3))
            for cc in range(4):
                TT("dve", ycv[:, cc, :], ycv[:, cc, :], pm[:, :], ALU.subtract, R=[Rpm, R_y[cc]], W=[R_y[cc]])
                ACT(sq[:, 4 + cc, :], ycv[:, cc, :], AF.Square, R=[R_y[cc]], W=[R_sq[4 + cc]])

        def ln_part2():
            pvv, Rpvv = ps()
            for cc in range(4):
                MM(pvv[:, :], cmat(C_512), sq[:, 4 + cc, :], cc == 0, cc == 3, R=[R_sq[4 + cc], R_c], W=[Rpvv], signal=(cc == 3))
            lt, Rlt = tmp()
            rs, Rrs = tmp()
            rsqrt_from(rs[:, :], lt[:, :], pvv[:, :], l, R=[Rpvv], Rtmp=Rlt, W=[Rrs])
            for cc in range(4):
                STT(ycv[:, cc, :], ycv[:, cc, :], vcol(l, V_LNG + cc), rs[:, :], ALU.mult, ALU.mult, R=[R_y[cc], Rrs, R_v], W=[R_y[cc]])

        def ln_part3(cc):
            wp, Rwp = tmp()
            ACT(wp[:, :], ycv[:, cc, :], AF.Identity, R=[R_y[cc], R_v], W=[Rwp], bias=vcol(l, V_LNB + cc))
            s_, Rs = sigmoid_of(wp[:, :], [Rwp])
            TT("dve", mixT[:, 4 + cc, :], wp[:, :], s_[:, :], ALU.mult, R=[Rwp, Rs], W=[R_mix[4 + cc]])

        def pre(t):
            bi = t % 2
            if l == 0:
                load_x_tokmajor(t, bi)
            else:
                load_x(t, bi)

        if not prepped:
            pre(0)
            norm_squares(0, sq, R_sq)
            norm_finish(l, 0, V_NMIX, sq, R_sq, tmp)
        for t in range(NT):
            bi = t % 2
            last_tile = (t == NT - 1)
            if not last_tile and l > 0:
                pre(t + 1)
            elif last_tile and nextp is not None:
                nextp.load()
            if not last_tile and l == 0:
                tokmajor_dma(t + 1, 0)
                tokmajor_dma(t + 1, 1)
            specs = [(i * 128, "q", i) for i in range(4)] + [(1792, "k", 0), (1920, "k", 1)]
            prev = None
            for sp_ in specs:
                cur = qk_mm(sp_)
                if prev is not None:
                    qk_fin(prev)
                prev = cur
            p, Rp = ps()
            for k in range(4):
                for c in range(8):
                    MM(p[:, k * 128:(k + 1) * 128], hT[:, c, k * 128:(k + 1) * 128], w_in[:, c, 640:768], c == 0, c == 7,
                       R=[RA[1], R_h[c]], W=[Rp], signal=(k == 3 and c == 7))
            qk_fin(prev)
            S.op("act", lambda h, p=p: h.copy(vpad[:, 1:5, :, 64:128], p[:, :].rearrange("p (k g d) -> p k g d", k=4, g=2)),
                 reads=[Rp], writes=[R_vp])
            if not last_tile and l == 0:
                tokmajor_tr(t + 1, (t + 1) % 2, 0)
                tokmajor_tr(t + 1, (t + 1) % 2, 1)
                tokmajor_dma(t + 1, 2)
                tokmajor_dma(t + 1, 3)
            for cc in range(4):
                pv_, Rpv = ps()
                pg, Rpg = ps()
                for c in range(8):
                    MM(pg[:, :], w_in[:, c, 1280 + cc * 128:1280 + (cc + 1) * 128], hT[:, c, :], c == 0, c == 7, R=[RA[3], R_h[c]], W=[Rpg], signal=(c == 7))
                for c in range(8):
                    MM(pv_[:, :], w_in[:, c, 768 + cc * 128:768 + (cc + 1) * 128], hT[:, c, :], c == 0, c == 7, R=[RA[2], R_h[c]], W=[Rpv], signal=(c == 7))
                s_, Rs = sigmoid_of(pg[:, :], [Rpg])
                TT("dve", gbuf[:, cc, 30:30 + T], pv_[:, :], s_[:, :], ALU.mult, R=[Rpv, Rs], W=[R_g[cc]])
                conv_taps_dve(cc)
            if last_tile and next_load is not None:
                next_load("A")
            if not last_tile and l == 0:
                tokmajor_tr(t + 1, (t + 1) % 2, 2)
                tokmajor_tr(t + 1, (t + 1) % 2, 3)
            for cc in range(4):
                conv_chunk(cc)
            for i_ in range(4):
                R_mix[i_].new_epoch()
            units = [(b, g) for b in range(4) for g in range(2)]
            LOOK = 2
            pend = []
            for ui, (b, g) in enumerate(units):
                pl = scores(t, b, g)
                pend.append((b, g, pl))
                if len(pend) > LOOK:
                    pb, pg_, ppl = pend.pop(0)
                    pv_den(t, pb, pg_, ppl)
                if ui == 0:
                    ln_part1()
                if ui == 2:
                    ln_part2()
                if ui in (2, 3, 4, 5):
                    ln_part3(ui - 2)
                if ui == 5:
                    if not last_tile:
                        norm_squares((t + 1) % 2, sq, R_sq, "dve")
                    elif nextp is not None:
                        nextp.squares(sq, R_sq, "dve")
            while pend:
                pb, pg_, ppl = pend.pop(0)
                pv_den(t, pb, pg_, ppl)
            if not last_tile:
                norm_stats(l, sq, R_sq, tmp, nrstd, R_nrstd)
            elif nextp is not None:
                nextp.stats(sq, R_sq, tmp, nrstd, R_nrstd)
            S.op("dve", lambda h: h.tensor_copy(khat[:, :, 0:128], khat[:, :, T:T + 128]), reads=[], writes=[R_k])
            S.op("dve", lambda h: h.tensor_copy(vpad[:, 0, :, :], vpad[:, 4, :, :]), reads=[], writes=[R_vp])
            if dump_tile is not None and t == dump_tile and l == 0:
                dump("mixT", mixT, R_mix)
            if not last_tile:
                norm_apply(l, (t + 1) % 2, V_NMIX, nrstd, R_nrstd)
            elif nextp is not None:
                nextp.apply(nrstd, R_nrstd)
            for m in range(8):
                residual_chunk(bi, m, w_out, RB, 8, mixT, R_mix)
            store_x(t, bi)
        if next_load is not None:
            next_load("B0")
            next_load("B1")

    def phase_X(l, next_load, prepped, nextp):
        tb_reset()
        set_psum_split(8)
        wq, wo = W["X"]
        sq = tb([8, T], BF16)
        R_sq = [Res(f"xsq{c}") for c in range(8)]
        qx = tb([8, T], BF16)
        R_qx = [Res(f"qx{i}") for i in range(8)]
        oT = tb([8, T], BF16)
        R_o = [Res(f"o{i}") for i in range(8)]
        NTMP = 4
        tmp = make_rot([tb([T], F32) for _ in range(NTMP)], [Res(f"xtmp{i}") for i in range(NTMP)])
        NP = 6
        ptile = make_rot([tb([T], BF16) for _ in range(NP)], [Res(f"xP{i}") for i in range(NP)])
        alloc_sqpool(4)
        if l == 1:
            S.dma("sp", kvsb[:, :], kvs_d, writes=[R_kv])
        if next_load is not None:
            next_load("B0")
            next_load("B1")

        def q_mm(h_):
            pab = []
            sqs = []
            for a in range(2):
                m = 2 * h_ + a
                p, Rp = ps()
                for c in range(8):
                    MM(p[:, :], wq[:, c, m * 128:(m + 1) * 128], hT[:, c, :], c == 0, c == 7, R=RA + [R_h[c]], W=[Rp], signal=(c == 7))
                sqq, Rsqq = ptile_big()
                ACT(sqq[:, :], p[:, :], AF.Square, R=[Rp], W=[Rsqq])
                pab.append((p, Rp))
                sqs.append((sqq, Rsqq))
            return (h_, pab, sqs)

        def q_fin(state):
            h_, pab, sqs = state
            pm, Rpm = ps()
            for a in range(2):
                MM(pm[:, :], cmat(C_256), sqs[a][0][:, :], a == 0, a == 1, R=[sqs[a][1], R_c], W=[Rpm], signal=(a == 1))
            lt, Rlt = tmp()
            rs, Rrs = tmp()
            rsqrt_from(rs[:, :], lt[:, :], pm[:, :], l, R=[Rpm], Rtmp=Rlt, W=[Rrs])
            for a in range(2):
                p, Rp = pab[a]
                STT(qx[:, 2 * h_ + a, :], p[:, :], dcol(l, 7 + a), rs[:, :], ALU.mult, ALU.mult, R=[Rp, Rrs, R_v], W=[R_qx[2 * h_ + a]])

        def xscores(h_):
            Ps = []
            for mb in range(2):
                p, Rp = ps()
                for a in range(2):
                    m = 2 * h_ + a
                    MM(p[:, :], khx[:, m, mb * 128:(mb + 1) * 128], qx[:, m, :], a == 0, a == 1, R=[R_kv, R_qx[m]], W=[Rp], signal=(a == 1))
                P_, RP_ = ptile()
                ACT(P_[:, :], p[:, :], AF.Exp, R=[Rp, R_v], W=[RP_], bias=dcol(l, 6))
                Ps.append((P_, RP_))
            return (h_, Ps)

        def xpv(state):
            h_, Ps = state
            pd, Rpd = ps()
            for mb in range(2):
                MM(pd[:, :], cmat(C_ONES), Ps[mb][0][:, :], mb == 0, mb == 1, R=[R_c, Ps[mb][1]], W=[Rpd], signal=(mb == 1))
            pns = []
            for a in range(2):
                m = 2 * h_ + a
                pn, Rpn = ps()
                for mb in range(2):
                    MM(pn[:, :], vmem[:, mb, m * 128:(m + 1) * 128], Ps[mb][0][:, :], mb == 0, mb == 1, R=[R_kv, Ps[mb][1]], W=[Rpn], signal=(mb == 1))
                pns.append((pn, Rpn))
            ld, Rld = tmp()
            ACT(ld[:, :], pd[:, :], AF.Ln, R=[Rpd], W=[Rld])
            rd, Rrd = tmp()
            ACT(rd[:, :], ld[:, :], AF.Exp, R=[Rld], W=[Rrd], scale=-1.0)
            for a in range(2):
                m = 2 * h_ + a
                TT("dve", oT[:, m, :], pns[a][0][:, :], rd[:, :], ALU.mult, R=[pns[a][1], Rrd], W=[R_o[m]])

        if not prepped:
            load_x(0, 0)
            norm_squares(0, sq, R_sq)
            norm_finish(l, 0, V_NX, sq, R_sq, tmp)
        for t in range(NT):
            bi = t % 2
            last_tile = (t == NT - 1)
            if not last_tile:
                load_x(t + 1, (t + 1) % 2)
            elif nextp is not None:
                nextp.load()
            prev = None
            for h_ in range(4):
                cur = q_mm(h_)
                if prev is not None:
                    q_fin(prev)
                prev = cur
            q_fin(prev)
            if not last_tile:
                norm_squares((t + 1) % 2, sq, R_sq, "dve")
            elif nextp is not None:
                nextp.squares(sq, R_sq, "dve")
            prevs = None
            for h_ in range(4):
                cur = xscores(h_)
                if prevs is not None:
                    xpv(prevs)
                prevs = cur
            xpv(prevs)
            for m in range(8):
                residual_chunk(bi, m, wo, RA, 8, oT, R_o)
                if m == 1 and not last_tile:
                    norm_finish(l, (t + 1) % 2, V_NX, sq, R_sq, tmp)
                elif m == 1 and nextp is not None:
                    nextp.finish(sq, R_sq, tmp)
            store_x(t, bi)
        if next_load is not None:
            next_load("A")

    def phase_F(l, half, last, next_load, prepped, nextp):
        tb_reset()
        set_psum_split(8)
        wgu, wd = W["FA" if half == 0 else "FB"]
        sq = tb([8, T], BF16)
        R_sq = [Res(f"fsq{c}") for c in range(8)]
        actT = tb([NJ, T], BF16)
        R_a = [Res(f"a{i}") for i in range(NJ)]
        NTMP = 6
        tmp = make_rot([tb([T], F32) for _ in range(NTMP)], [Res(f"ftmp{i}") for i in range(NTMP)])

        def gu_mm(j):
            pg, Rpg = ps()
            pu, Rpu = ps()
            for c in range(8):
                MM(pg[:, :], wgu[:, c, j * 128:(j + 1) * 128], hT[:, c, :], c == 0, c == 7, R=[RB[0 if j < JS else 1], R_h[c]], W=[Rpg], signal=(c == 7))
            for c in range(8):
                MM(pu[:, :], wgu[:, c, HF + j * 128:HF + (j + 1) * 128], hT[:, c, :], c == 0, c == 7, R=[RB[0 if j < JS else 1], R_h[c]], W=[Rpu], signal=(c == 7))
            return (j, pg, Rpg, pu, Rpu)

        def gu_fin(state):
            j, pg, Rpg, pu, Rpu = state
            s_, Rs = tmp()
            ACT(s_[:, :], pg[:, :], AF.Silu, R=[Rpg], W=[Rs])
            TT("dve", actT[:, j, :], s_[:, :], pu[:, :], ALU.mult, R=[Rs, Rpu], W=[R_a[j]])

        def h_ready(t):
            bi = t % 2
            if half == 0:
                norm_finish(l, bi, V_NFFN, sq, R_sq, tmp)
                store_h(t)
            else:
                load_h(t)

        if not prepped:
            load_x(0, 0)
            if half == 0:
                norm_squares(0, sq, R_sq)
            h_ready(0)
        for t in range(NT):
            bi = t % 2
            last_tile = (t == NT - 1)
            if not last_tile:
                load_x(t + 1, (t + 1) % 2)
            elif nextp is not None:
                nextp.load()
            prev = None
            for j in range(NJ):
                cur = gu_mm(j)
                if prev is not None:
                    gu_fin(prev)
                prev = cur
                if j == JS - 1 and last_tile and next_load is not None:
                    next_load("B0")
            gu_fin(prev)
            if last_tile and next_load is not None:
                next_load("B1")
            if not last_tile and half == 0:
                norm_squares((t + 1) % 2, sq, R_sq)
            if not last_tile and half == 1:
                h_ready(t + 1)
            if last_tile and nextp is not None:
                nextp.squares(sq, R_sq)
            for m in range(8):
                residual_chunk(bi, m, wd, RA, NJ, actT, R_a)
                if m == 0 and not last_tile and half == 0:
                    h_ready(t + 1)
                elif m == 0 and last_tile and nextp is not None:
                    nextp.finish(sq, R_sq, tmp)
            if last:
                transpose_out(t, bi)
            else:
                store_x(t, bi)
        if next_load is not None:
            next_load("A")

    phases = []
    for l in range(DEPTH):
        phases += [("M", l), ("X", l), ("FA", l), ("FB", l)]
    if stop_after is not None:
        phases = phases[:stop_after]
    load_region(phases[0][0], phases[0][1], "A")
    load_region(phases[0][0], phases[0][1], "B0")
    load_region(phases[0][0], phases[0][1], "B1")
    for i, (kind, l) in enumerate(phases):
        nxt = None
        if i + 1 < len(phases):
            nk, nl = phases[i + 1]
            nxt = (lambda region, nk=nk, nl=nl: load_region(nk, nl, region))
        nextp = NextPrep(phases[i + 1][0], phases[i + 1][1]) if i + 1 < len(phases) else None
        prepped = (i > 0)
        if kind == "M":
            phase_M(l, nxt, prepped, nextp)
        elif kind == "X":
            phase_X(l, nxt, prepped, nextp)
        elif kind == "FA":
            phase_F(l, 0, False, nxt, prepped, nextp)
        else:
            phase_F(l, 1, (l == DEPTH - 1 and stop_after is None), nxt, prepped, nextp)
        compute_barrier()
    if stop_after is not None:
        dbg = nc.dram_tensor("dbg", [D, SEQ], F32, kind="ExternalOutput").ap()
        for t in range(NT):
            S.dma("sp", xbuf[0][:, :, :], xs_v[:, :, t * T:(t + 1) * T], reads=[R_xs[t]], writes=R_x[0])
            S.dma("sp", dbg.rearrange("(c p) s -> p c s", p=128)[:, :, t * T:(t + 1) * T], xbuf[0][:, :, :], reads=R_x[0])
    S.barrier()
    S.replay()
    nc._dump_names = dumps
    return nc


_CACHE = {}


def _get_nc(stop_after=None, dump_tile=None):
    key = (stop_after, dump_tile)
    if key not in _CACHE:
        _CACHE[key] = build(stop_after=stop_after, dump_tile=dump_tile)
    return _CACHE[key]


def kernel(x, mem, norm_mix_g, w_in, q_norm_g, k_norm_g, sinks, conv_w, conv_b, conv_ln_g, conv_ln_b, w_out,
           norm_x_g, norm_mem_g, wq_x, wkv_x, xq_norm_g, xk_norm_g, wo_x, norm_ffn_g, w_gate_up, w_down,
           _stop_after=None, _dump_tile=None):
    inp = dict(norm_mix_g=norm_mix_g, q_norm_g=q_norm_g, k_norm_g=k_norm_g, sinks=sinks, conv_w=conv_w, conv_b=conv_b,
               conv_ln_g=conv_ln_g, conv_ln_b=conv_ln_b, norm_x_g=norm_x_g, norm_mem_g=norm_mem_g,
               xq_norm_g=xq_norm_g, xk_norm_g=xk_norm_g, norm_ffn_g=norm_ffn_g)
    inp = {k: np.asarray(v, np.float32) for k, v in inp.items()}
    vecs = _host_vecs(inp)
    cb, cf = _host_consts()
    f = lambda a: np.ascontiguousarray(np.asarray(a, np.float32))
    shared = {"w_in": f(w_in), "w_out": f(w_out), "wq_x": f(wq_x), "wkv_x": f(wkv_x), "wo_x": f(wo_x),
              "w_gate_up": f(w_gate_up), "w_down": f(w_down), "vecs": vecs, "cbf": cb, "cf32": cf}
    x = np.asarray(x, np.float32)
    mem = np.asarray(mem, np.float32)
    B = x.shape[0]
    in_maps = []
    for b in range(B):
        m = dict(shared)
        m["x"] = np.ascontiguousarray(x[b])
        m["mem"] = np.ascontiguousarray(mem[b])
        in_maps.append(m)
    nc = _get_nc(_stop_after, _dump_tile)
    res = run_bass_kernel_spmd(nc, in_maps, core_ids=list(range(B)))
    if _dump_tile is not None:
        kernel.last_dumps = {n: np.asarray(res.results[0]["dump_" + n]) for n in nc._dump_names}
    if _stop_after is not None:
        return np.stack([np.asarray(r["dbg"]).T for r in res.results], axis=0)
    return np.stack([np.asarray(r["y"]) for r in res.results], axis=0).astype(np.float32)
```

```python
import numpy as np
import ml_dtypes
import concourse.bass as bass
import concourse.mybir as mybir
from concourse.bass_utils import run_bass_kernel_spmd

F32 = mybir.dt.float32
BF16 = mybir.dt.bfloat16
AF = mybir.ActivationFunctionType
ALU = mybir.AluOpType
AX = mybir.AxisListType

SEQ = 4096
D = 1024
T = 512
NT = SEQ // T
DEPTH = 2
EPS = 1e-6
NEGM = -30000.0
HF = 1408
NJ = 11
JS = 6
TD = 8


class Tok:
    __slots__ = ("sem", "key", "val", "clock")

    def __init__(self, sem, val, clock):
        self.sem = sem
        self.key = sem.num
        self.val = val
        self.clock = clock


class Res:
    __slots__ = ("name", "w", "r", "pw", "touch")

    def __init__(self, name=""):
        self.name = name
        self.w = []
        self.r = {}
        self.pw = []
        self.touch = 0

    def new_epoch(self):
        self.pw = list(self.r.values()) + list(self.w) + list(self.pw)
        self.w = []
        self.r = {}


class Eng:
    def __init__(self, name, sem):
        self.name = name
        self.sem = sem
        self.cnt = 0
        self.seen = {}
        self.ops = []


class DmaRing:
    def __init__(self, sems):
        self.sems = sems
        self.tot = [0] * len(sems)
        self.last = [None] * len(sems)
        self.i = 0


class Sched:
    def __init__(self, nc, n_dma_sems=12):
        self.nc = nc
        self.E = {}
        for n in ("pe", "act", "dve", "pool"):
            self.E[n] = Eng(n, nc.alloc_semaphore(name=f"sem_{n}"))
        self.E["sp"] = Eng("sp", None)
        self.rings = {}
        for q in ("sp", "pool"):
            self.rings[q] = DmaRing([nc.alloc_semaphore(name=f"dq_{q}_{i}") for i in range(n_dma_sems)])

    def _wait(self, e, tok):
        if e.seen.get(tok.key, 0) >= tok.val:
            return
        e.ops.append(("wait", tok.sem, tok.val))
        seen = e.seen
        for k, v in tok.clock.items():
            if seen.get(k, 0) < v:
                seen[k] = v
        seen[tok.key] = tok.val

    def _dep1(self, en, e, d):
        if en == "pe" and d.sem is e.sem:
            return
        self._wait(e, d)

    def _deps(self, en, e, reads, writes, pwrites):
        for r in reads:
            for d in r.w:
                self._dep1(en, e, d)
        for w in writes:
            for d in w.w:
                self._dep1(en, e, d)
            for d in w.r.values():
                self._dep1(en, e, d)
            for d in w.pw:
                self._dep1(en, e, d)
        for w in pwrites:
            for d in w.pw:
                self._dep1(en, e, d)
            for d in w.r.values():
                self._dep1(en, e, d)

    def _post(self, tok, reads, writes, pwrites):
        self.opidx = getattr(self, "opidx", 0) + 1
        for r in reads:
            r.touch = self.opidx
        for w in writes:
            w.touch = self.opidx
        for w in pwrites:
            w.touch = self.opidx
        for r in reads:
            old = r.r.get(tok.key)
            if old is None or old.val < tok.val:
                r.r[tok.key] = tok
        for w in writes:
            w.w = [tok]
            w.r = {}
            w.pw = []
        for w in pwrites:
            w.w.append(tok)

    def op(self, en, fn, reads=(), writes=(), pwrites=(), signal=True):
        e = self.E[en]
        self._deps(en, e, reads, writes, pwrites)
        if signal:
            e.cnt += 1
            val = e.cnt
        else:
            val = e.cnt + 1
        e.ops.append(("op", fn, signal))
        tok = Tok(e.sem, val, dict(e.seen))
        self._post(tok, reads, writes, pwrites)
        return tok

    def dma(self, q, out_ap, in_ap, reads=(), writes=(), pwrites=()):
        e = self.E[q]
        self._deps(q, e, reads, writes, pwrites)
        ring = self.rings[q]
        i = ring.i
        ring.i = (i + 1) % len(ring.sems)
        if ring.last[i] is not None:
            self._wait(e, ring.last[i])
        ring.tot[i] += 16
        e.ops.append(("dma", out_ap, in_ap, ring.sems[i]))
        tok = Tok(ring.sems[i], ring.tot[i], dict(e.seen))
        ring.last[i] = tok
        self._post(tok, reads, writes, pwrites)
        return tok

    def barrier(self):
        toks = []
        for n, e in self.E.items():
            if e.sem is not None and e.cnt > 0:
                toks.append(Tok(e.sem, e.cnt, {}))
        for q, ring in self.rings.items():
            for t in ring.last:
                if t is not None:
                    toks.append(t)
        for n, e in self.E.items():
            for t in toks:
                if t.sem is e.sem and n == "pe":
                    continue
                self._wait(e, t)

    def replay(self):
        nc = self.nc
        E = self.E

        def run(e, h):
            for o in e.ops:
                k = o[0]
                if k == "wait":
                    h.wait_ge(o[1], o[2])
                elif k == "op":
                    ins = o[1](h)
                    if o[2]:
                        ins.then_inc(e.sem, 1)
                else:
                    h.dma_start(out=o[1], in_=o[2]).then_inc(o[3], 16)

        with nc.Block() as block:
            @block.tensor
            def _(h):
                run(E["pe"], h)

            @block.scalar
            def _(h):
                run(E["act"], h)

            @block.vector
            def _(h):
                run(E["dve"], h)

            @block.gpsimd
            def _(h):
                run(E["pool"], h)

            @block.sync
            def _(h):
                run(E["sp"], h)


V_NMIX, V_NX, V_NFFN, V_NMEM = 0, 8, 16, 24
V_GQ, V_GK = 32, 33
V_CB, V_LNG, V_LNB = 34, 38, 42
V_GXQ, V_GXK = 46, 48
V_SINKP = 50
V_SINKR = 54
V_CW = 62
V_KEEP = 186
V_GQR, V_GKR = 186, 250
V_GXQR, V_GXKR = 314, 570
NV = 826

C_ID = 0
C_1024 = 128
C_BD64 = 256
C_256 = 384
C_512 = 512
C_ONES = 640
C_OPAD = 768
C_BIAS = 960
NCB = C_BIAS + 8 * 256


def _host_consts():
    cb = np.zeros((128, NCB), np.float32)
    cb[:, C_ID:C_ID + 128] = np.eye(128)
    cb[:, C_1024:C_1024 + 128] = 1.0 / 1024
    bd = np.zeros((128, 128), np.float32)
    bd[:64, :64] = 1.0 / 64
    bd[64:, 64:] = 1.0 / 64
    cb[:, C_BD64:C_BD64 + 128] = bd
    cb[:, C_256:C_256 + 128] = 1.0 / 256
    cb[:, C_512:C_512 + 128] = 1.0 / 512
    cb[:, C_ONES:C_ONES + 128] = 1.0
    cb[:, C_OPAD + 64:C_OPAD + 128] = 1.0
    s = np.arange(128)[:, None].astype(np.float32)
    q = np.arange(128)[None, :].astype(np.float32)
    for kb in range(2):
        for g in range(2):
            for par in range(2):
                ti = (g * 2 + par) * 2 + kb
                tab = np.zeros((128, 2, 128), np.float32)
                for j in range(2):
                    h = 4 * g + 2 * j + par
                    slope = 2.0 ** (-(h + 1))
                    if kb == 0:
                        dist = q + 128 - s
                        valid = s > q
                    else:
                        dist = q - s
                        valid = s <= q
                    tab[:, j, :] = np.where(valid, -slope * dist, NEGM)
                cb[:, C_BIAS + ti * 256:C_BIAS + (ti + 1) * 256] = tab.reshape(128, 256)
    return cb.astype(ml_dtypes.bfloat16), np.eye(128, dtype=np.float32)


def _host_vecs(inp):
    out = np.zeros((DEPTH, 128, NV), np.float32)
    p = np.arange(128)
    for l in range(DEPTH):
        v = out[l]

        def cols(vec, n):
            return np.asarray(vec, np.float32).reshape(n, 128).T
        v[:, V_NMIX:V_NMIX + 8] = cols(inp["norm_mix_g"][l], 8)
        v[:, V_NX:V_NX + 8] = cols(inp["norm_x_g"][l], 8)
        v[:, V_NFFN:V_NFFN + 8] = cols(inp["norm_ffn_g"][l], 8)
        v[:, V_NMEM:V_NMEM + 8] = cols(inp["norm_mem_g"][l], 8)
        v[:, V_GQ] = np.asarray(inp["q_norm_g"][l])[p % 64]
        v[:, V_GK] = np.asarray(inp["k_norm_g"][l])[p % 64]
        v[:, V_CB:V_CB + 4] = cols(inp["conv_b"][l], 4)
        v[:, V_LNG:V_LNG + 4] = cols(inp["conv_ln_g"][l], 4)
        v[:, V_LNB:V_LNB + 4] = cols(inp["conv_ln_b"][l], 4)
        v[:, V_GXQ:V_GXQ + 2] = cols(inp["xq_norm_g"][l], 2)
        v[:, V_GXK:V_GXK + 2] = cols(inp["xk_norm_g"][l], 2)
        sk = np.asarray(inp["sinks"][l], np.float32)
        for g in range(2):
            for j in range(2):
                v[:64, V_SINKP + 2 * g + j] = sk[4 * g + 2 * j]
                v[64:, V_SINKP + 2 * g + j] = sk[4 * g + 2 * j + 1]
        v[:, V_SINKR:V_SINKR + 8] = sk[None, :]
        cw = np.asarray(inp["conv_w"][l], np.float32)
        for cc in range(4):
            v[:, V_CW + cc * 31:V_CW + (cc + 1) * 31] = cw[:, cc * 128:(cc + 1) * 128].T
        v[:, V_GQR:V_GQR + 64] = np.asarray(inp["q_norm_g"][l])[None, :]
        v[:, V_GKR:V_GKR + 64] = np.asarray(inp["k_norm_g"][l])[None, :]
        v[:, V_GXQR:V_GXQR + 256] = np.asarray(inp["xq_norm_g"][l])[None, :]
        v[:, V_GXKR:V_GXKR + 256] = np.asarray(inp["xk_norm_g"][l])[None, :]
    return out


def build(stop_after=None, native_silu=True, dump_tile=None):
    nc = bass.Bass("TRN2", target_bir_lowering=False)
    S = Sched(nc)
    dumps = []

    def dump(name, ap, reads):
        shape = list(ap.shape)
        d = nc.dram_tensor("dump_" + name, shape, ap.dtype, kind="ExternalOutput").ap()
        S.dma("sp", d, ap, reads=reads)
        dumps.append(name)

    def din(name, shape, dt=F32):
        return nc.dram_tensor(name, list(shape), dt, kind="ExternalInput").ap()

    x_d = din("x", [SEQ, D])
    mem_d = din("mem", [256, D])
    w_in_d = din("w_in", [DEPTH, D, 1792])
    w_out_d = din("w_out", [DEPTH, D, D])
    wq_d = din("wq_x", [DEPTH, D, D])
    wkv_d = din("wkv_x", [DEPTH, D, 2 * D])
    wo_d = din("wo_x", [DEPTH, D, D])
    wgu_d = din("w_gate_up", [DEPTH, D, 2 * 2816])
    wd_d = din("w_down", [DEPTH, 2816, D])
    vecs_d = din("vecs", [DEPTH, 128, NV])
    cbf_d = din("cbf", [128, NCB], BF16)
    cf_d = din("cf32", [128, 128])
    y_d = nc.dram_tensor("y", [SEQ, D], F32, kind="ExternalOutput").ap()
    xs_d = nc.dram_tensor("xs", [D, SEQ], F32, kind="Internal").ap()
    hs_d = nc.dram_tensor("hs", [D, SEQ], BF16, kind="Internal").ap()
    kvs_d = nc.dram_tensor("kvs", [128, 4096], BF16, kind="Internal").ap()
    xs_v = xs_d.rearrange("(c p) s -> p c s", p=128)
    hs_v = hs_d.rearrange("(c p) s -> p c s", p=128)

    NA = 16 * 1024
    NB = 8 * 1024 + 4 * 31 * 128 + 512
    NB = max(NB, 8 * 2 * HF)
    WA = nc.alloc_sbuf_tensor("WA", [128, NA], BF16)
    WBt = nc.alloc_sbuf_tensor("WBt", [128, NB], BF16)
    RA = [Res(f"WA{i}") for i in range(4)]
    RB = [Res("WB0"), Res("WB1")]
    xbuf = [nc.alloc_sbuf_tensor(f"xb{i}", [128, 8, T], F32) for i in range(2)]
    R_x = [[Res(f"x{i}_{c}") for c in range(8)] for i in range(2)]
    hT = nc.alloc_sbuf_tensor("hT", [128, 8, T], BF16)
    R_h = [Res(f"h{c}") for c in range(8)]
    cbf = nc.alloc_sbuf_tensor("cbf_sb", [128, NCB], BF16)
    identf = nc.alloc_sbuf_tensor("identf", [128, 128], F32)
    R_c = Res("consts")
    pv = nc.alloc_sbuf_tensor("pv", [128, DEPTH, V_KEEP], F32)
    NDV = 16
    dv = nc.alloc_sbuf_tensor("dv", [128, DEPTH, NDV], F32)
    R_v = Res("vecs")
    kvsb = nc.alloc_sbuf_tensor("kvsb", [128, 4096], BF16)
    khx = kvsb[:, 0:2048].rearrange("p (m k) -> p m k", m=8)
    vmem = kvsb[:, 2048:4096].rearrange("p (m d) -> p m d", m=2)
    R_kv = Res("kv")
    stage = [nc.alloc_sbuf_tensor(f"stage{i}", [128, 1024], F32) for i in range(2)]
    R_stage = [Res("stage0"), Res("stage1")]
    TBN = 60 * 1024
    TB = nc.alloc_sbuf_tensor("TB", [128, TBN // 2], BF16)
    PS = [nc.alloc_psum_tensor(f"ps{i}", [128, 512], F32) for i in range(8)]
    R_ps = [Res(f"ps{i}") for i in range(8)]
    R_psh = [[Res(f"psh{i}_{k}") for k in range(2)] for i in range(8)]
    st = {"ps": 0, "tb": 0, "nfull": 8, "ph": 0}

    def set_psum_split(nfull):
        st["nfull"] = nfull
        st["ps"] = 0
        st["ph"] = 0

    def ps():
        i = min(range(8), key=lambda k: R_ps[k].touch)
        R_ps[i].touch = getattr(S, "opidx", 0) + 0.5
        return PS[i], R_ps[i]

    def psh():
        nh = (8 - st["nfull"]) * 2
        k = st["ph"]
        st["ph"] = (k + 1) % nh
        b = st["nfull"] + k // 2
        return PS[b][:, (k % 2) * 256:(k % 2) * 256 + 256], R_psh[b][k % 2]

    def tb_reset():
        st["tb"] = 0

    def tb(shape, dt):
        n = int(np.prod(shape))
        nb = n * (4 if dt == F32 else 2)
        nb = (nb + 63) // 64 * 64
        off = st["tb"]
        assert off + nb <= TBN, (off, nb, TBN)
        st["tb"] = off + nb
        v = TB[:, off // 2:(off + nb) // 2]
        if dt == F32:
            v = v.bitcast(F32)
        v = v[:, 0:n]
        if len(shape) == 2:
            return v.rearrange("p (a b) -> p a b", a=shape[0])
        if len(shape) == 3:
            return v.rearrange("p (a b c) -> p a b c", a=shape[0], b=shape[1])
        return v

    def ACT(out, in_, func, R, W, bias=None, scale=None, PW=()):
        kw = {}
        if bias is not None:
            kw["bias"] = bias
        if scale is not None:
            kw["scale"] = scale
        return S.op("act", lambda h: h.activation(out, in_, func, **kw), reads=R, writes=W, pwrites=PW)

    def TT(en, out, a, b, op, R, W, PW=()):
        return S.op(en, lambda h: h.tensor_tensor(out, a, b, op), reads=R, writes=W, pwrites=PW)

    def STT(out, in0, scalar, in1, op0, op1, R, W, PW=()):
        return S.op("dve", lambda h: h.scalar_tensor_tensor(out, in0, scalar, in1, op0, op1), reads=R, writes=W, pwrites=PW)

    def TS(out, in0, s1, op0, R, W, s2=None, op1=None, PW=()):
        if op1 is None:
            return S.op("dve", lambda h: h.tensor_scalar(out, in0, s1, None, op0), reads=R, writes=W, pwrites=PW)
        return S.op("dve", lambda h: h.tensor_scalar(out, in0, s1, s2, op0, op1), reads=R, writes=W, pwrites=PW)

    def CP(en, out, in_, R, W, PW=()):
        if en == "act":
            return S.op("act", lambda h: h.copy(out, in_), reads=R, writes=W, pwrites=PW)
        return S.op(en, lambda h: h.tensor_copy(out, in_), reads=R, writes=W, pwrites=PW)

    def MM(out, lhsT, rhs, start, stop, R, W, signal):
        return S.op("pe", lambda h: h.matmul(out, lhsT, rhs, start=start, stop=stop), reads=R, writes=W, signal=signal)

    def TR(out, in_, R, W, signal):
        return S.op("pe", lambda h: h.transpose(out, in_, identf[:, :]), reads=list(R) + [R_c], writes=W, signal=signal)

    def cmat(off, n=128):
        return cbf[:, off:off + n]

    def vcol(l, c, n=1):
        return pv[:, l, c:c + n]

    def dcol(l, c, n=1):
        return dv[:, l, c:c + n]

    def rsqrt_from(out, tmp, src, l, R, Rtmp, W):
        ACT(tmp, src, AF.Ln, R=list(R) + [R_v], W=[Rtmp], bias=dcol(l, 9))
        ACT(out, tmp, AF.Exp, R=[Rtmp], W=W, scale=-0.5)

    def compute_barrier():
        toks = []
        for n in ("pe", "act", "dve"):
            e = S.E[n]
            if e.cnt > 0:
                toks.append(Tok(e.sem, e.cnt, {}))
        for n in ("pe", "act", "dve"):
            e = S.E[n]
            for t_ in toks:
                if t_.sem is e.sem and n == "pe":
                    continue
                S._wait(e, t_)

    S.dma("sp", cbf[:, :], cbf_d, writes=[R_c])
    S.dma("sp", identf[:, :], cf_d, pwrites=[R_c])
    tb_reset()
    vfull = tb([DEPTH, NV], F32)
    R_vf = Res("vfull")
    S.dma("sp", vfull, vecs_d.rearrange("l p n -> p l n"), writes=[R_vf])
    CP("dve", pv[:, :, :], vfull[:, :, 0:V_KEEP], R=[R_vf], W=[R_v])
    S.op("dve", lambda h: h.memset(dv[:, :, :], 0.0), writes=[R_v])
    S.op("dve", lambda h: h.memset(dv[:, :, 9:10], EPS), writes=[R_v])
    S.op("dve", lambda h: h.memset(dv[:, :, 10:11], 1.0), writes=[R_v])
    sc1 = tb([256], F32)
    sc2 = tb([8], F32)
    R_s1 = Res("sc1")
    R_s2 = Res("sc2")
    for l in range(DEPTH):
        TS(dcol(l, 0), vcol(l, V_GQ), 0.125, ALU.mult, R=[R_v], W=[R_v])
        TS(dcol(l, 7, 2), vcol(l, V_GXQ, 2), 1.0 / 16, ALU.mult, R=[R_v], W=[R_v])
        TT("dve", sc1[:, 0:64], vfull[:, l, V_GQR:V_GQR + 64], vfull[:, l, V_GKR:V_GKR + 64], ALU.mult, R=[R_vf], W=[R_s1])
        TT("dve", sc1[:, 0:64], sc1[:, 0:64], sc1[:, 0:64], ALU.mult, R=[R_s1], W=[R_s1])
        S.op("dve", lambda h: h.reduce_max(sc2[:, 0:1], sc1[:, 0:64], AX.X), reads=[R_s1], writes=[R_s2])
        ACT(sc2[:, 1:2], sc2[:, 0:1], AF.Ln, R=[R_s2], W=[R_s2])
        ACT(sc2[:, 2:3], sc2[:, 1:2], AF.Exp, R=[R_s2], W=[R_s2], scale=0.5)
        S.op("dve", lambda h, l=l: h.reduce_max(sc2[:, 3:4], vfull[:, l, V_SINKR:V_SINKR + 8], AX.X), reads=[R_vf, R_s2], writes=[R_s2])
        STT(sc2[:, 4:5], sc2[:, 2:3], 8.0, sc2[:, 3:4], ALU.mult, ALU.max, R=[R_s2], W=[R_s2])
        TS(dcol(l, 1), sc2[:, 4:5], -1.0, ALU.mult, R=[R_s2, R_v], W=[R_v])
        ACT(dcol(l, 2, 4), vcol(l, V_SINKP, 4), AF.Exp, R=[R_v], W=[R_v], bias=dcol(l, 1))
        TT("dve", sc1[:, :], vfull[:, l, V_GXQR:V_GXQR + 256], vfull[:, l, V_GXKR:V_GXKR + 256], ALU.mult, R=[R_vf, R_s1], W=[R_s1])
        TT("dve", sc1[:, :], sc1[:, :], sc1[:, :], ALU.mult, R=[R_s1], W=[R_s1])
        S.op("dve", lambda h: h.reduce_max(sc2[:, 5:6], sc1[:, :], AX.X), reads=[R_s1, R_s2], writes=[R_s2])
        ACT(sc2[:, 6:7], sc2[:, 5:6], AF.Ln, R=[R_s2], W=[R_s2])
        ACT(sc2[:, 7:8], sc2[:, 6:7], AF.Exp, R=[R_s2], W=[R_s2], scale=0.5)
        TS(dcol(l, 6), sc2[:, 7:8], -16.0, ALU.mult, R=[R_s2, R_v], W=[R_v])

    wkv = WA[:, 0:8 * 2048].rearrange("p (c n) -> p c n", c=8)
    memsb = xbuf[0][:, 0:4, :].rearrange("p a b -> p (a b)").rearrange("p (m d) -> p m d", m=2)
    R_mem = Res("mem")
    S.dma("sp", memsb, mem_d.rearrange("(m p) d -> p m d", p=128), writes=[R_mem])
    msq = tb([1024], F32)
    R_msq = Res("msq")
    mst = tb([8], F32)
    R_mst = Res("mst")
    mn = xbuf[1][:, 0:4, :].rearrange("p a b -> p (a b)").rearrange("p (m d) -> p m d", m=2)
    R_mn = Res("mn")
    for mb in range(2):
        ACT(msq[:, :], memsb[:, mb, :], AF.Square, R=[R_mem], W=[R_msq])
        S.op("dve", lambda h, mb=mb: h.reduce_sum(mst[:, mb:mb + 1], msq[:, :], AX.X), reads=[R_msq, R_mst], writes=[R_mst])
    TS(mst[:, 2:4], mst[:, 0:2], 1.0 / 1024, ALU.mult, R=[R_mst], W=[R_mst])
    ACT(mst[:, 4:6], mst[:, 2:4], AF.Ln, R=[R_mst, R_v], W=[R_mst], bias=dcol(0, 9))
    ACT(mst[:, 6:8], mst[:, 4:6], AF.Exp, R=[R_mst], W=[R_mst], scale=-0.5)
    for mb in range(2):
        TS(mn[:, mb, :], memsb[:, mb, :], mst[:, 6 + mb:7 + mb], ALU.mult, R=[R_mem, R_mst], W=[R_mn])
    memT = tb([8, 256], BF16)
    R_mT = Res("memT")
    ksq = tb([2, 256], BF16)
    R_ksq = Res("ksq")
    krs = tb([256], F32)
    R_krs = Res("krs")
    klt = tb([256], F32)
    R_klt = Res("klt")
    for l in (1, 0):
        for r_ in RA:
            r_.new_epoch()
        R_mT.new_epoch()
        R_kv.new_epoch()
        for c in range(8):
            S.dma("pool", wkv[:, c, :], wkv_d[l, c * 128:(c + 1) * 128, :], pwrites=RA)
        for c in range(8):
            p, Rp = ps()
            for mb in range(2):
                TR(p[:, mb * 128:(mb + 1) * 128], mn[:, mb, c * 128:(c + 1) * 128], R=[R_mn], W=[Rp], signal=(mb == 1))
            TS(memT[:, c, :], p[:, 0:256], vcol(l, V_NMEM + c), ALU.mult, R=[Rp, R_v], W=[], PW=[R_mT])
        for h_ in range(4):
            pab = []
            R_ksq.new_epoch()
            for a in range(2):
                p, Rp = ps()
                m = 2 * h_ + a
                for c in range(8):
                    MM(p[:, 0:256], wkv[:, c, m * 128:(m + 1) * 128], memT[:, c, :], c == 0, c == 7, R=RA + [R_mT], W=[Rp], signal=(c == 7))
                ACT(ksq[:, a, :], p[:, 0:256], AF.Square, R=[Rp], W=[], PW=[R_ksq])
                pab.append((p, Rp))
            pm, Rpm = ps()
            for a in range(2):
                MM(pm[:, 0:256], cmat(C_256), ksq[:, a, :], a == 0, a == 1, R=[R_ksq, R_c], W=[Rpm], signal=(a == 1))
            rsqrt_from(krs[:, :], klt[:, :], pm[:, 0:256], l, R=[Rpm], Rtmp=R_klt, W=[R_krs])
            for a in range(2):
                p, Rp = pab[a]
                STT(khx[:, 2 * h_ + a, :], p[:, 0:256], vcol(l, V_GXK + a), krs[:, :], ALU.mult, ALU.mult, R=[Rp, R_krs, R_v], W=[], PW=[R_kv])
        for mb in range(2):
            for n2 in range(2):
                p, Rp = ps()
                for c in range(8):
                    MM(p[:, :], memT[:, c, mb * 128:(mb + 1) * 128], wkv[:, c, 1024 + n2 * 512:1024 + (n2 + 1) * 512], c == 0, c == 7, R=RA + [R_mT], W=[Rp], signal=(c == 7))
                CP("act", vmem[:, mb, n2 * 512:(n2 + 1) * 512], p[:, :], R=[Rp], W=[], PW=[R_kv])
        if l == 1:
            S.dma("sp", kvs_d, kvsb[:, :], reads=[R_kv])
    compute_barrier()
    S.barrier()
    for r in (R_kv, R_v, R_c, R_mT):
        r.w = []
        r.r = {}
        r.pw = []

    def wview(region, off, c, n):
        t_ = WA if region == "A" else WBt
        return t_[:, off:off + c * n].rearrange("p (c n) -> p c n", c=c)

    w_in_v = wview("A", 0, 8, 2048)
    w_out_v = wview("B", 0, 8, 1024)
    _dgo = 8 * 1024
    diagw_v = WBt[:, _dgo:_dgo + 4 * 31 * 128].rearrange("p (a j m) -> p a j m", a=4, j=31)
    diagw_f = WBt[:, _dgo:_dgo + 4 * 31 * 128].rearrange("p (a m) -> p a m", a=124)
    wq_v = wview("A", 0, 8, 1024)
    wo_v = wview("A", 8 * 1024, 8, 1024)
    wgu_v = wview("B", 0, 8, 2 * HF)
    wd_v = WA[:, 0:NJ * 1024].rearrange("p (j n) -> p j n", j=NJ)
    W = {"M": (w_in_v, w_out_v, diagw_v), "X": (wq_v, wo_v), "FA": (wgu_v, wd_v), "FB": (wgu_v, wd_v)}

    def load_region(kind, l, region):
        if kind == "M" and region == "A":
            src = w_in_d[l].rearrange("(c p) n -> p c n", p=128)
            for (bi_, c0, c1) in ((0, 0, 512), (1, 512, 768), (3, 1280, 1792), (2, 768, 1280)):
                RA[bi_].new_epoch()
                S.dma("pool", w_in_v[:, :, c0:c1], src[:, :, c0:c1], pwrites=[RA[bi_]])
                if bi_ == 1:
                    for (dst, sc_) in ((1792, 512), (1856, 512), (1920, 576), (1984, 576)):
                        S.op("pool", lambda h, dst=dst, sc_=sc_: h.tensor_copy(w_in_v[:, :, dst:dst + 64], w_in_v[:, :, sc_:sc_ + 64]),
                             reads=[RA[1]], pwrites=[RA[1]])
        elif kind == "M" and region == "B1":
            for r_ in RB:
                r_.new_epoch()
            for c in range(8):
                S.dma("pool", w_out_v[:, c, :], w_out_d[l, c * 128:(c + 1) * 128, :], pwrites=RB)
            S.op("dve", lambda h: h.tensor_tensor(
                diagw_f, cmat(C_ID).unsqueeze(1).to_broadcast([128, 124, 128]),
                pv[:, l, V_CW:V_CW + 124].unsqueeze(2).to_broadcast([128, 124, 128]), ALU.mult),
                reads=[R_c, R_v], pwrites=RB)
        elif kind == "X" and region == "A":
            for r_ in RA:
                r_.new_epoch()
            for c in range(8):
                S.dma("pool", wq_v[:, c, :], wq_d[l, c * 128:(c + 1) * 128, :], pwrites=RA)
            for c in range(8):
                S.dma("pool", wo_v[:, c, :], wo_d[l, c * 128:(c + 1) * 128, :], pwrites=RA)
        elif kind in ("FA", "FB") and region in ("B0", "B1"):
            f0 = (0 if kind == "FA" else 1) * HF
            hb = 0 if region == "B0" else 1
            j0, j1 = (0, JS) if hb == 0 else (JS, NJ)
            RB[hb].new_epoch()
            for c in range(8):
                S.dma("pool", wgu_v[:, c, j0 * 128:j1 * 128], wgu_d[l, c * 128:(c + 1) * 128, f0 + j0 * 128:f0 + j1 * 128], pwrites=[RB[hb]])
                S.dma("pool", wgu_v[:, c, HF + j0 * 128:HF + j1 * 128],
                      wgu_d[l, c * 128:(c + 1) * 128, 2816 + f0 + j0 * 128:2816 + f0 + j1 * 128], pwrites=[RB[hb]])
        elif kind in ("FA", "FB") and region == "A":
            f0 = (0 if kind == "FA" else 1) * HF
            for r_ in RA:
                r_.new_epoch()
            wdv = wd_d[l, f0:f0 + HF, :].rearrange("(j p) n -> p j n", p=128)
            for (j0, j1) in ((0, 4), (4, 8), (8, 11)):
                S.dma("pool", wd_v[:, j0:j1, :], wdv[:, j0:j1, :], pwrites=RA)

    R_xs = [Res(f"xs{t}") for t in range(NT)]
    R_hs = [Res(f"hs{t}") for t in range(NT)]

    def load_x(t, bi):
        S.dma("sp", xbuf[bi][:, :, :], xs_v[:, :, t * T:(t + 1) * T], reads=[R_xs[t]], writes=R_x[bi])

    def store_x(t, bi):
        S.dma("sp", xs_v[:, :, t * T:(t + 1) * T], xbuf[bi][:, :, :], reads=R_x[bi], writes=[R_xs[t]])

    def store_h(t):
        S.dma("sp", hs_v[:, :, t * T:(t + 1) * T], hT[:, :, :], reads=R_h, writes=[R_hs[t]])

    def load_h(t):
        S.dma("sp", hT[:, :, :], hs_v[:, :, t * T:(t + 1) * T], reads=[R_hs[t]], writes=R_h)

    class NextPrep:
        def __init__(self, kind, l):
            self.kind = kind
            self.l = l
            self.gbase = {"M": V_NMIX, "X": V_NX, "FA": V_NFFN, "FB": None}[kind]

        def load(self):
            load_x(0, 0)

        def squares(self, sq, R_sq, eng="act"):
            if self.gbase is not None:
                norm_squares(0, sq, R_sq, eng)

        def finish(self, sq, R_sq, tmpf):
            if self.gbase is not None:
                norm_finish(self.l, 0, self.gbase, sq, R_sq, tmpf)
                if self.kind == "FA":
                    store_h(0)
            else:
                load_h(0)

        def stats(self, sq, R_sq, tmpf, rstd, R_rstd):
            if self.gbase is not None:
                norm_stats(self.l, sq, R_sq, tmpf, rstd, R_rstd)

        def apply(self, rstd, R_rstd):
            if self.gbase is not None:
                norm_apply(self.l, 0, self.gbase, rstd, R_rstd)
                if self.kind == "FA":
                    store_h(0)
            else:
                load_h(0)

    scnt = {"i": 0}

    def tokmajor_dma(t, k):
        si = k % 2
        r0 = t * T + k * 128
        S.dma("sp", stage[si][:, :], x_d[r0:r0 + 128, :], writes=[R_stage[si]])

    def tokmajor_tr(t, bi, k):
        si = k % 2
        for i in range(2):
            p, Rp = ps()
            for mm_ in range(4):
                c = 4 * i + mm_
                TR(p[:, mm_ * 128:(mm_ + 1) * 128], stage[si][:, c * 128:(c + 1) * 128], R=[R_stage[si]], W=[Rp], signal=(mm_ == 3))
            pview = p[:, :].rearrange("p (c n) -> p c n", c=4)
            if k == 0:
                for mm_ in range(4):
                    R_x[bi][4 * i + mm_].new_epoch()
            CP("act" if i else "dve", xbuf[bi][:, 4 * i:4 * i + 4, k * 128:(k + 1) * 128], pview, R=[Rp], W=[],
               PW=[R_x[bi][4 * i + mm_] for mm_ in range(4)])

    def load_x_tokmajor(t, bi):
        for k in range(4):
            tokmajor_dma(t, k)
            tokmajor_tr(t, bi, k)

    def transpose_out(t, bi):
        for k in range(4):
            si = scnt["i"] % 2
            scnt["i"] += 1
            R_stage[si].new_epoch()
            for i in range(2):
                p, Rp = ps()
                for mm_ in range(4):
                    m = 4 * i + mm_
                    TR(p[:, mm_ * 128:(mm_ + 1) * 128], xbuf[bi][:, m, k * 128:(k + 1) * 128], R=[R_x[bi][m]], W=[Rp], signal=(mm_ == 3))
                CP("act" if i else "dve", stage[si][:, i * 512:(i + 1) * 512], p[:, :], R=[Rp], W=[], PW=[R_stage[si]])
            r0 = t * T + k * 128
            S.dma("sp", y_d[r0:r0 + 128, :], stage[si][:, :], reads=[R_stage[si]])

    def norm_squares(bi, sq, R_sq, eng="act"):
        for c in range(8):
            if eng == "act":
                ACT(sq[:, c, :], xbuf[bi][:, c, :], AF.Square, R=[R_x[bi][c]], W=[R_sq[c]])
            else:
                TT("dve", sq[:, c, :], xbuf[bi][:, c, :], xbuf[bi][:, c, :], ALU.mult, R=[R_x[bi][c]], W=[R_sq[c]])

    def norm_finish(l, bi, gbase, sq, R_sq, tmpf):
        rstd, R_rstd = tmpf()
        lnt, R_lnt = tmpf()
        p, Rp = ps()
        for c in range(8):
            MM(p[:, :], cmat(C_1024), sq[:, c, :], c == 0, c == 7, R=[R_sq[c], R_c], W=[Rp], signal=(c == 7))
        rsqrt_from(rstd[:, :], lnt[:, :], p[:, :], l, R=[Rp], Rtmp=R_lnt, W=[R_rstd])
        for c in range(8):
            STT(hT[:, c, :], xbuf[bi][:, c, :], vcol(l, gbase + c), rstd[:, :], ALU.mult, ALU.mult,
                R=[R_x[bi][c], R_rstd, R_v], W=[R_h[c]])

    def norm_stats(l, sq, R_sq, tmpf, rstd, R_rstd):
        lnt, R_lnt = tmpf()
        p, Rp = ps()
        for c in range(8):
            MM(p[:, :], cmat(C_1024), sq[:, c, :], c == 0, c == 7, R=[R_sq[c], R_c], W=[Rp], signal=(c == 7))
        rsqrt_from(rstd[:, :], lnt[:, :], p[:, :], l, R=[Rp], Rtmp=R_lnt, W=[R_rstd])

    def norm_apply(l, bi, gbase, rstd, R_rstd):
        for c in range(8):
            STT(hT[:, c, :], xbuf[bi][:, c, :], vcol(l, gbase + c), rstd[:, :], ALU.mult, ALU.mult,
                R=[R_x[bi][c], R_rstd, R_v], W=[R_h[c]])

    def residual_chunk(bi, m, wmat, RW, nk, rhsT, R_rhs):
        p, Rp = ps()
        for k in range(nk):
            MM(p[:, :], wmat[:, k, m * 128:(m + 1) * 128], rhsT[:, k, :], k == 0, k == nk - 1,
               R=list(RW) + [R_rhs[k]], W=[Rp], signal=(k == nk - 1))
        TT("dve", xbuf[bi][:, m, :], xbuf[bi][:, m, :], p[:, :], ALU.add, R=[Rp, R_x[bi][m]], W=[R_x[bi][m]])

    sqpool = {"tiles": None, "i": 0}

    def ptile_big():
        tl = sqpool["tiles"]
        i = sqpool["i"]
        sqpool["i"] = (i + 1) % len(tl)
        return tl[i]

    def alloc_sqpool(n):
        sqpool["tiles"] = [(tb([T], BF16), Res(f"sqq{i}")) for i in range(n)]
        sqpool["i"] = 0

    def make_rot(tiles, ress):
        c = {"i": 0}

        def f():
            i = c["i"]
            c["i"] = (i + 1) % len(tiles)
            return tiles[i], ress[i]
        return f

    def phase_M(l, next_load, prepped, nextp):
        tb_reset()
        set_psum_split(8)
        w_in, w_out, diagw = W["M"]
        sq = tb([8, T], BF16)
        R_sq = [Res(f"sq{c}") for c in range(8)]
        qhat = tb([4, T], BF16)
        R_q = [Res(f"q{i}") for i in range(4)]
        khat = tb([2, 128 + T], BF16)
        R_k = Res("khat")
        vpad = tb([5, 2, 192], BF16)
        R_vp = Res("vpad")
        gbuf = tb([4, 30 + T], BF16)
        R_g = [Res(f"g{i}") for i in range(4)]
        mixT = tb([8, T], BF16)
        R_mix = [Res(f"mix{i}") for i in range(8)]
        ycv = tb([4, T], F32)
        R_y = [Res(f"y{i}") for i in range(4)]
        NTMP = 5
        tmp = make_rot([tb([T], F32) for _ in range(NTMP)], [Res(f"tmp{i}") for i in range(NTMP)])
        nrstd = tb([T], F32)
        R_nrstd = Res("nrstd")
        NP = 6
        ptile = make_rot([tb([T], BF16) for _ in range(NP)], [Res(f"P{i}") for i in range(NP)])
        alloc_sqpool(3)

        S.op("dve", lambda h: h.memset(vpad.rearrange("p a b c -> p (a b c)"), 0.0), writes=[R_vp])
        for cc in range(4):
            S.op("dve", lambda h, cc=cc: h.memset(gbuf[:, cc, 0:30], 0.0), writes=[R_g[cc]])

        def sigmoid_of(src, Rsrc):
            e, Re = tmp()
            ACT(e[:, :], src, AF.Exp, R=Rsrc, W=[Re], scale=-1.0)
            l_, Rl = tmp()
            ACT(l_[:, :], e[:, :], AF.Ln, R=[Re, R_v], W=[Rl], bias=dcol(l, 10))
            s_, Rs = tmp()
            ACT(s_[:, :], l_[:, :], AF.Exp, R=[Rl], W=[Rs], scale=-1.0)
            return s_, Rs

        def qk_mm(spec):
            col, kind, idx = spec
            p, Rp = ps()
            for c in range(8):
                MM(p[:, :], w_in[:, c, col:col + 128], hT[:, c, :], c == 0, c == 7, R=[RA[0 if kind == "q" else 1], R_h[c]], W=[Rp], signal=(c == 7))
            sqq, Rsqq = ptile_big()
            ACT(sqq[:, :], p[:, :], AF.Square, R=[Rp], W=[Rsqq])
            return (spec, p, Rp, sqq, Rsqq)

        def qk_fin(state):
            (col, kind, idx), p, Rp, sqq, Rsqq = state
            p2, Rp2 = ps()
            MM(p2[:, :], cmat(C_BD64), sqq[:, :], True, True, R=[Rsqq, R_c], W=[Rp2], signal=True)
            lt, Rlt = tmp()
            rs, Rrs = tmp()
            rsqrt_from(rs[:, :], lt[:, :], p2[:, :], l, R=[Rp2], Rtmp=Rlt, W=[Rrs])
            if kind == "q":
                STT(qhat[:, idx, :], p[:, :], dcol(l, 0), rs[:, :], ALU.mult, ALU.mult, R=[Rp, Rrs, R_v], W=[R_q[idx]])
            else:
                STT(khat[:, idx, 128:128 + T], p[:, :], vcol(l, V_GK), rs[:, :], ALU.mult, ALU.mult, R=[Rp, Rrs, R_v], W=[R_k])

        def scores(t, b, g):
            n = 4 * t + b
            kbs = ([0] if n > 0 else []) + [1]
            c0 = kbs[0] * 256
            banks = [ps(), ps()]
            for kb in kbs:
                for par in range(2):
                    p, Rp = banks[par]
                    pview = p[:, kb * 256:(kb + 1) * 256].rearrange("p (a b) -> p a b", a=2)
                    kblk = khat[64 * par:64 * par + 64, g, (b + kb) * 128:(b + kb + 1) * 128]
                    qrhs = qhat[64 * par:64 * par + 64, 2 * g:2 * g + 2, b * 128:(b + 1) * 128]
                    MM(pview, kblk, qrhs, True, False, R=[R_k, R_q[2 * g], R_q[2 * g + 1]], W=[Rp], signal=False)
                    ti = (g * 2 + par) * 2 + kb
                    MM(p[:, kb * 256:(kb + 1) * 256], cmat(C_ID), cbf[:, C_BIAS + ti * 256:C_BIAS + (ti + 1) * 256], False, True,
                       R=[R_c], W=[Rp], signal=(kb == 1))
            plist = []
            for par in range(2):
                p, Rp = banks[par]
                P_, RP_ = ptile()
                ACT(P_[:, c0:512], p[:, c0:512], AF.Exp, R=[Rp, R_v], W=[RP_], bias=dcol(l, 1))
                plist.append((par, P_, RP_))
            return (kbs, plist)

        def pv_den(t, b, g, sc):
            kbs, plist = sc
            pnd, Rpnd = ps()
            nk = len(kbs)
            for ki, kb in enumerate(kbs):
                for (par, P_, RP_) in plist:
                    S.op("pe", lambda h, par=par, kb=kb, P_=P_, ki=ki: h.matmul(
                        pnd[64 * par:64 * par + 64, 0:256], vpad[:, b + kb, g, 64:128], P_[:, kb * 256:(kb + 1) * 256],
                        start=(ki == 0), stop=(ki == nk - 1), tile_position=(0, 64 * par)),
                        reads=[R_vp, RP_], writes=[Rpnd], signal=False)
            for ki, kb in enumerate(kbs):
                for (par, P_, RP_) in plist:
                    last = (ki == nk - 1 and par == 1)
                    S.op("pe", lambda h, par=par, kb=kb, P_=P_, ki=ki: h.matmul(
                        pnd[64 * par:64 * par + 64, 256:512], cbf[:, C_ONES:C_ONES + 64], P_[:, kb * 256:(kb + 1) * 256],
                        start=(ki == 0), stop=(ki == nk - 1), tile_position=(0, 64 * par)),
                        reads=[R_c, RP_], writes=[Rpnd], signal=last)
            pn = pnd[:, 0:256]
            pd = pnd[:, 256:512]
            ld, Rld = tmp()
            for j in range(2):
                S.op("act", lambda h, j=j: h.activation(ld[:, j * 128:(j + 1) * 128], pd[:, j * 128:(j + 1) * 128], AF.Ln,
                                                        bias=dcol(l, 2 + 2 * g + j)),
                     reads=[Rpnd, R_v], writes=[Rld] if j == 0 else [], pwrites=[] if j == 0 else [Rld])
            rd, Rrd = tmp()
            ACT(rd[:, 0:256], ld[:, 0:256], AF.Exp, R=[Rld], W=[Rrd], scale=-1.0)
            TT("dve", mixT[:, 2 * g:2 * g + 2, b * 128:(b + 1) * 128], pn.rearrange("p (a b) -> p a b", a=2),
               rd[:, 0:256].rearrange("p (a b) -> p a b", a=2), ALU.mult, R=[Rpnd, Rrd], W=[], PW=[R_mix[2 * g], R_mix[2 * g + 1]])

        def conv_taps_dve(cc):
            TS(ycv[:, cc, :], gbuf[:, cc, 0:T], vcol(l, V_CW + cc * 31), ALU.mult, R=[R_g[cc], R_v], W=[R_y[cc]])
            for j in range(1, TD):
                STT(ycv[:, cc, :], gbuf[:, cc, j:j + T], vcol(l, V_CW + cc * 31 + j), ycv[:, cc, :], ALU.mult, ALU.add,
                    R=[R_g[cc], R_v, R_y[cc]], W=[R_y[cc]])

        def conv_chunk(cc):
            p, Rp = ps()
            for j in range(TD, 31):
                MM(p[:, :], diagw[:, cc, j, :], gbuf[:, cc, j:j + T], j == TD, j == 30, R=RB + [R_g[cc]], W=[Rp], signal=(j == 30))
            STT(ycv[:, cc, :], p[:, :], vcol(l, V_CB + cc), ycv[:, cc, :], ALU.add, ALU.add, R=[Rp, R_v, R_y[cc]], W=[R_y[cc]])
            S.op("dve", lambda h, cc=cc: h.tensor_copy(gbuf[:, cc, 0:30], gbuf[:, cc, T:T + 30]), reads=[], writes=[R_g[cc]])

        ln_state = {}

        def ln_part1():
            for cc in range(4):
                CP("dve", sq[:, cc, :], ycv[:, cc, :], R=[R_y[cc]], W=[R_sq[cc]])
            pm, Rpm = ps()
            for cc in range(4):
                MM(pm[:, :], cmat(C_512), sq[:, cc, :], cc == 0, cc == 3, R=[R_sq[cc], R_c], W=[Rpm], signal=(cc == 3))
            for cc in range(4):
                TT("dve", ycv[:, cc, :], ycv[:, cc, :], pm[:, :], ALU.subtract, R=[Rpm, R_y[cc]], W=[R_y[cc]])
                ACT(sq[:, 4 + cc, :], ycv[:, cc, :], AF.Square, R=[R_y[cc]], W=[R_sq[4 + cc]])

        def ln_part2():
            pvv, Rpvv = ps()
            for cc in range(4):
                MM(pvv[:, :], cmat(C_512), sq[:, 4 + cc, :], cc == 0, cc == 3, R=[R_sq[4 + cc], R_c], W=[Rpvv], signal=(cc == 3))
            lt, Rlt = tmp()
            rs, Rrs = tmp()
            rsqrt_from(rs[:, :], lt[:, :], pvv[:, :], l, R=[Rpvv], Rtmp=Rlt, W=[Rrs])
            for cc in range(4):
                STT(ycv[:, cc, :], ycv[:, cc, :], vcol(l, V_LNG + cc), rs[:, :], ALU.mult, ALU.mult, R=[R_y[cc], Rrs, R_v], W=[R_y[cc]])

        def ln_part3(cc):
            wp, Rwp = tmp()
            ACT(wp[:, :], ycv[:, cc, :], AF.Identity, R=[R_y[cc], R_v], W=[Rwp], bias=vcol(l, V_LNB + cc))
            s_, Rs = sigmoid_of(wp[:, :], [Rwp])
            TT("dve", mixT[:, 4 + cc, :], wp[:, :], s_[:, :], ALU.mult, R=[Rwp, Rs], W=[R_mix[4 + cc]])

        def pre(t):
            bi = t % 2
            if l == 0:
                load_x_tokmajor(t, bi)
            else:
                load_x(t, bi)

        if not prepped:
            pre(0)
            norm_squares(0, sq, R_sq)
            norm_finish(l, 0, V_NMIX, sq, R_sq, tmp)
        for t in range(NT):
            bi = t % 2
            last_tile = (t == NT - 1)
            if not last_tile and l > 0:
                pre(t + 1)
            elif last_tile and nextp is not None:
                nextp.load()
            if not last_tile and l == 0:
                tokmajor_dma(t + 1, 0)
                tokmajor_dma(t + 1, 1)
            specs = [(i * 128, "q", i) for i in range(4)] + [(1792, "k", 0), (1920, "k", 1)]
            prev = None
            for sp_ in specs:
                cur = qk_mm(sp_)
                if prev is not None:
                    qk_fin(prev)
                prev = cur
            p, Rp = ps()
            for k in range(4):
                for c in range(8):
                    MM(p[:, k * 128:(k + 1) * 128], hT[:, c, k * 128:(k + 1) * 128], w_in[:, c, 640:768], c == 0, c == 7,
                       R=[RA[1], R_h[c]], W=[Rp], signal=(k == 3 and c == 7))
            qk_fin(prev)
            S.op("act", lambda h, p=p: h.copy(vpad[:, 1:5, :, 64:128], p[:, :].rearrange("p (k g d) -> p k g d", k=4, g=2)),
                 reads=[Rp], writes=[R_vp])
            if not last_tile and l == 0:
                tokmajor_tr(t + 1, (t + 1) % 2, 0)
                tokmajor_tr(t + 1, (t + 1) % 2, 1)
                tokmajor_dma(t + 1, 2)
                tokmajor_dma(t + 1, 3)
            for cc in range(4):
                pv_, Rpv = ps()
                pg, Rpg = ps()
                for c in range(8):
                    MM(pg[:, :], w_in[:, c, 1280 + cc * 128:1280 + (cc + 1) * 128], hT[:, c, :], c == 0, c == 7, R=[RA[3], R_h[c]], W=[Rpg], signal=(c == 7))
                for c in range(8):
                    MM(pv_[:, :], w_in[:, c, 768 + cc * 128:768 + (cc + 1) * 128], hT[:, c, :], c == 0, c == 7, R=[RA[2], R_h[c]], W=[Rpv], signal=(c == 7))
                s_, Rs = sigmoid_of(pg[:, :], [Rpg])
                TT("dve", gbuf[:, cc, 30:30 + T], pv_[:, :], s_[:, :], ALU.mult, R=[Rpv, Rs], W=[R_g[cc]])
                conv_taps_dve(cc)
            if last_tile and next_load is not None:
                next_load("A")
            if not last_tile and l == 0:
                tokmajor_tr(t + 1, (t + 1) % 2, 2)
                tokmajor_tr(t + 1, (t + 1) % 2, 3)
            for cc in range(4):
                conv_chunk(cc)
            for i_ in range(4):
                R_mix[i_].new_epoch()
            units = [(b, g) for b in range(4) for g in range(2)]
            LOOK = 2
            pend = []
            for ui, (b, g) in enumerate(units):
                pl = scores(t, b, g)
                pend.append((b, g, pl))
                if len(pend) > LOOK:
                    pb, pg_, ppl = pend.pop(0)
                    pv_den(t, pb, pg_, ppl)
                if ui == 0:
                    ln_part1()
                if ui == 2:
                    ln_part2()
                if ui in (2, 3, 4, 5):
                    ln_part3(ui - 2)
                if ui == 5:
                    if not last_tile:
                        norm_squares((t + 1) % 2, sq, R_sq, "dve")
                    elif nextp is not None:
                        nextp.squares(sq, R_sq, "dve")
            while pend:
                pb, pg_, ppl = pend.pop(0)
                pv_den(t, pb, pg_, ppl)
            if not last_tile:
                norm_stats(l, sq, R_sq, tmp, nrstd, R_nrstd)
            elif nextp is not None:
                nextp.stats(sq, R_sq, tmp, nrstd, R_nrstd)
            S.op("dve", lambda h: h.tensor_copy(khat[:, :, 0:128], khat[:, :, T:T + 128]), reads=[], writes=[R_k])
            S.op("dve", lambda h: h.tensor_copy(vpad[:, 0, :, :], vpad[:, 4, :, :]), reads=[], writes=[R_vp])
            if dump_tile is not None and t == dump_tile and l == 0:
                dump("mixT", mixT, R_mix)
            if not last_tile:
                norm_apply(l, (t + 1) % 2, V_NMIX, nrstd, R_nrstd)
            elif nextp is not None:
                nextp.apply(nrstd, R_nrstd)
            for m in range(8):
                residual_chunk(bi, m, w_out, RB, 8, mixT, R_mix)
            store_x(t, bi)
        if next_load is not None:
            next_load("B0")
            next_load("B1")

    def phase_X(l, next_load, prepped, nextp):
        tb_reset()
        set_psum_split(8)
        wq, wo = W["X"]
        sq = tb([8, T], BF16)
        R_sq = [Res(f"xsq{c}") for c in range(8)]
        qx = tb([8, T], BF16)
        R_qx = [Res(f"qx{i}") for i in range(8)]
        oT = tb([8, T], BF16)
        R_o = [Res(f"o{i}") for i in range(8)]
        NTMP = 4
        tmp = make_rot([tb([T], F32) for _ in range(NTMP)], [Res(f"xtmp{i}") for i in range(NTMP)])
        NP = 6
        ptile = make_rot([tb([T], BF16) for _ in range(NP)], [Res(f"xP{i}") for i in range(NP)])
        alloc_sqpool(4)
        if l == 1:
            S.dma("sp", kvsb[:, :], kvs_d, writes=[R_kv])
        if next_load is not None:
            next_load("B0")
            next_load("B1")

        def q_mm(h_):
            pab = []
            sqs = []
            for a in range(2):
                m = 2 * h_ + a
                p, Rp = ps()
                for c in range(8):
                    MM(p[:, :], wq[:, c, m * 128:(m + 1) * 128], hT[:, c, :], c == 0, c == 7, R=RA + [R_h[c]], W=[Rp], signal=(c == 7))
                sqq, Rsqq = ptile_big()
                ACT(sqq[:, :], p[:, :], AF.Square, R=[Rp], W=[Rsqq])
                pab.append((p, Rp))
                sqs.append((sqq, Rsqq))
            return (h_, pab, sqs)

        def q_fin(state):
            h_, pab, sqs = state
            pm, Rpm = ps()
            for a in range(2):
                MM(pm[:, :], cmat(C_256), sqs[a][0][:, :], a == 0, a == 1, R=[sqs[a][1], R_c], W=[Rpm], signal=(a == 1))
            lt, Rlt = tmp()
            rs, Rrs = tmp()
            rsqrt_from(rs[:, :], lt[:, :], pm[:, :], l, R=[Rpm], Rtmp=Rlt, W=[Rrs])
            for a in range(2):
                p, Rp = pab[a]
                STT(qx[:, 2 * h_ + a, :], p[:, :], dcol(l, 7 + a), rs[:, :], ALU.mult, ALU.mult, R=[Rp, Rrs, R_v], W=[R_qx[2 * h_ + a]])

        def xscores(h_):
            Ps = []
            for mb in range(2):
                p, Rp = ps()
                for a in range(2):
                    m = 2 * h_ + a
                    MM(p[:, :], khx[:, m, mb * 128:(mb + 1) * 128], qx[:, m, :], a == 0, a == 1, R=[R_kv, R_qx[m]], W=[Rp], signal=(a == 1))
                P_, RP_ = ptile()
                ACT(P_[:, :], p[:, :], AF.Exp, R=[Rp, R_v], W=[RP_], bias=dcol(l, 6))
                Ps.append((P_, RP_))
            return (h_, Ps)

        def xpv(state):
            h_, Ps = state
            pd, Rpd = ps()
            for mb in range(2):
                MM(pd[:, :], cmat(C_ONES), Ps[mb][0][:, :], mb == 0, mb == 1, R=[R_c, Ps[mb][1]], W=[Rpd], signal=(mb == 1))
            pns = []
            for a in range(2):
                m = 2 * h_ + a
                pn, Rpn = ps()
                for mb in range(2):
                    MM(pn[:, :], vmem[:, mb, m * 128:(m + 1) * 128], Ps[mb][0][:, :], mb == 0, mb == 1, R=[R_kv, Ps[mb][1]], W=[Rpn], signal=(mb == 1))
                pns.append((pn, Rpn))
            ld, Rld = tmp()
            ACT(ld[:, :], pd[:, :], AF.Ln, R=[Rpd], W=[Rld])
            rd, Rrd = tmp()
            ACT(rd[:, :], ld[:, :], AF.Exp, R=[Rld], W=[Rrd], scale=-1.0)
            for a in range(2):
                m = 2 * h_ + a
                TT("dve", oT[:, m, :], pns[a][0][:, :], rd[:, :], ALU.mult, R=[pns[a][1], Rrd], W=[R_o[m]])

        if not prepped:
            load_x(0, 0)
            norm_squares(0, sq, R_sq)
            norm_finish(l, 0, V_NX, sq, R_sq, tmp)
        for t in range(NT):
            bi = t % 2
            last_tile = (t == NT - 1)
            if not last_tile:
                load_x(t + 1, (t + 1) % 2)
            elif nextp is not None:
                nextp.load()
            prev = None
            for h_ in range(4):
                cur = q_mm(h_)
                if prev is not None:
                    q_fin(prev)
                prev = cur
            q_fin(prev)
            if not last_tile:
                norm_squares((t + 1) % 2, sq, R_sq, "dve")
            elif nextp is not None:
                nextp.squares(sq, R_sq, "dve")
            prevs = None
            for h_ in range(4):
                cur = xscores(h_)
                if prevs is not None:
                    xpv(prevs)
                prevs = cur
            xpv(prevs)
            for m in range(8):
                residual_chunk(bi, m, wo, RA, 8, oT, R_o)
                if m == 1 and not last_tile:
                    norm_finish(l, (t + 1) % 2, V_NX, sq, R_sq, tmp)
                elif m == 1 and nextp is not None:
                    nextp.finish(sq, R_sq, tmp)
            store_x(t, bi)
        if next_load is not None:
            next_load("A")

    def phase_F(l, half, last, next_load, prepped, nextp):
        tb_reset()
        set_psum_split(8)
        wgu, wd = W["FA" if half == 0 else "FB"]
        sq = tb([8, T], BF16)
        R_sq = [Res(f"fsq{c}") for c in range(8)]
        actT = tb([NJ, T], BF16)
        R_a = [Res(f"a{i}") for i in range(NJ)]
        NTMP = 6
        tmp = make_rot([tb([T], F32) for _ in range(NTMP)], [Res(f"ftmp{i}") for i in range(NTMP)])

        def gu_mm(j):
            pg, Rpg = ps()
            pu, Rpu = ps()
            for c in range(8):
                MM(pg[:, :], wgu[:, c, j * 128:(j + 1) * 128], hT[:, c, :], c == 0, c == 7, R=[RB[0 if j < JS else 1], R_h[c]], W=[Rpg], signal=(c == 7))
            for c in range(8):
                MM(pu[:, :], wgu[:, c, HF + j * 128:HF + (j + 1) * 128], hT[:, c, :], c == 0, c == 7, R=[RB[0 if j < JS else 1], R_h[c]], W=[Rpu], signal=(c == 7))
            return (j, pg, Rpg, pu, Rpu)

        def gu_fin(state):
            j, pg, Rpg, pu, Rpu = state
            s_, Rs = tmp()
            ACT(s_[:, :], pg[:, :], AF.Silu, R=[Rpg], W=[Rs])
            TT("dve", actT[:, j, :], s_[:, :], pu[:, :], ALU.mult, R=[Rs, Rpu], W=[R_a[j]])

        def h_ready(t):
            bi = t % 2
            if half == 0:
                norm_finish(l, bi, V_NFFN, sq, R_sq, tmp)
                store_h(t)
            else:
                load_h(t)

        if not prepped:
            load_x(0, 0)
            if half == 0:
                norm_squares(0, sq, R_sq)
            h_ready(0)
        for t in range(NT):
            bi = t % 2
            last_tile = (t == NT - 1)
            if not last_tile:
                load_x(t + 1, (t + 1) % 2)
            elif nextp is not None:
                nextp.load()
            prev = None
            for j in range(NJ):
                cur = gu_mm(j)
                if prev is not None:
                    gu_fin(prev)
                prev = cur
                if j == JS - 1 and last_tile and next_load is not None:
                    next_load("B0")
            gu_fin(prev)
            if last_tile and next_load is not None:
                next_load("B1")
            if not last_tile and half == 0:
                norm_squares((t + 1) % 2, sq, R_sq)
            if not last_tile and half == 1:
                h_ready(t + 1)
            if last_tile and nextp is not None:
                nextp.squares(sq, R_sq)
            for m in range(8):
                residual_chunk(bi, m, wd, RA, NJ, actT, R_a)
                if m == 0 and not last_tile and half == 0:
                    h_ready(t + 1)
                elif m == 0 and last_tile and nextp is not None:
                    nextp.finish(sq, R_sq, tmp)
            if last:
                transpose_out(t, bi)
            else:
                store_x(t, bi)
        if next_load is not None:
            next_load("A")

    phases = []
    for l in range(DEPTH):
        phases += [("M", l), ("X", l), ("FA", l), ("FB", l)]
    if stop_after is not None:
        phases = phases[:stop_after]
    load_region(phases[0][0], phases[0][1], "A")
    load_region(phases[0][0], phases[0][1], "B0")
    load_region(phases[0][0], phases[0][1], "B1")
    for i, (kind, l) in enumerate(phases):
        nxt = None
        if i + 1 < len(phases):
            nk, nl = phases[i + 1]
            nxt = (lambda region, nk=nk, nl=nl: load_region(nk, nl, region))
        nextp = NextPrep(phases[i + 1][0], phases[i + 1][1]) if i + 1 < len(phases) else None
        prepped = (i > 0)
        if kind == "M":
            phase_M(l, nxt, prepped, nextp)
        elif kind == "X":
            phase_X(l, nxt, prepped, nextp)
        elif kind == "FA":
            phase_F(l, 0, False, nxt, prepped, nextp)
        else:
            phase_F(l, 1, (l == DEPTH - 1 and stop_after is None), nxt, prepped, nextp)
        compute_barrier()
    if stop_after is not None:
        dbg = nc.dram_tensor("dbg", [D, SEQ], F32, kind="ExternalOutput").ap()
        for t in range(NT):
            S.dma("sp", xbuf[0][:, :, :], xs_v[:, :, t * T:(t + 1) * T], reads=[R_xs[t]], writes=R_x[0])
            S.dma("sp", dbg.rearrange("(c p) s -> p c s", p=128)[:, :, t * T:(t + 1) * T], xbuf[0][:, :, :], reads=R_x[0])
    S.barrier()
    S.replay()
    nc._dump_names = dumps
    return nc


_CACHE = {}


def _get_nc(stop_after=None, dump_tile=None):
    key = (stop_after, dump_tile)
    if key not in _CACHE:
        _CACHE[key] = build(stop_after=stop_after, dump_tile=dump_tile)
    return _CACHE[key]


def kernel(x, mem, norm_mix_g, w_in, q_norm_g, k_norm_g, sinks, conv_w, conv_b, conv_ln_g, conv_ln_b, w_out,
           norm_x_g, norm_mem_g, wq_x, wkv_x, xq_norm_g, xk_norm_g, wo_x, norm_ffn_g, w_gate_up, w_down,
           _stop_after=None, _dump_tile=None):
    inp = dict(norm_mix_g=norm_mix_g, q_norm_g=q_norm_g, k_norm_g=k_norm_g, sinks=sinks, conv_w=conv_w, conv_b=conv_b,
               conv_ln_g=conv_ln_g, conv_ln_b=conv_ln_b, norm_x_g=norm_x_g, norm_mem_g=norm_mem_g,
               xq_norm_g=xq_norm_g, xk_norm_g=xk_norm_g, norm_ffn_g=norm_ffn_g)
    inp = {k: np.asarray(v, np.float32) for k, v in inp.items()}
    vecs = _host_vecs(inp)
    cb, cf = _host_consts()
    f = lambda a: np.ascontiguousarray(np.asarray(a, np.float32))
    shared = {"w_in": f(w_in), "w_out": f(w_out), "wq_x": f(wq_x), "wkv_x": f(wkv_x), "wo_x": f(wo_x),
              "w_gate_up": f(w_gate_up), "w_down": f(w_down), "vecs": vecs, "cbf": cb, "cf32": cf}
    x = np.asarray(x, np.float32)
    mem = np.asarray(mem, np.float32)
    B = x.shape[0]
    in_maps = []
    for b in range(B):
        m = dict(shared)
        m["x"] = np.ascontiguousarray(x[b])
        m["mem"] = np.ascontiguousarray(mem[b])
        in_maps.append(m)
    nc = _get_nc(_stop_after, _dump_tile)
    res = run_bass_kernel_spmd(nc, in_maps, core_ids=list(range(B)))
    if _dump_tile is not None:
        kernel.last_dumps = {n: np.asarray(res.results[0]["dump_" + n]) for n in nc._dump_names}
    if _stop_after is not None:
        return np.stack([np.asarray(r["dbg"]).T for r in res.results], axis=0)
    return np.stack([np.asarray(r["y"]) for r in res.results], axis=0).astype(np.float32)
```

```python
import numpy as np
import ml_dtypes
import concourse.bass as bass
import concourse.mybir as mybir
from concourse.bass_utils import run_bass_kernel_spmd

F32 = mybir.dt.float32
BF16 = mybir.dt.bfloat16
AF = mybir.ActivationFunctionType
ALU = mybir.AluOpType
AX = mybir.AxisListType

SEQ = 4096
D = 1024
T = 512
NT = SEQ // T
DEPTH = 2
EPS = 1e-6
NEGM = -30000.0
HF = 1408
NJ = 11
JS = 6
TD = 8


class Tok:
    __slots__ = ("sem", "key", "val", "clock")

    def __init__(self, sem, val, clock):
        self.sem = sem
        self.key = sem.num
        self.val = val
        self.clock = clock


class Res:
    __slots__ = ("name", "w", "r", "pw", "touch")

    def __init__(self, name=""):
        self.name = name
        self.w = []
        self.r = {}
        self.pw = []
        self.touch = 0

    def new_epoch(self):
        self.pw = list(self.r.values()) + list(self.w) + list(self.pw)
        self.w = []
        self.r = {}


class Eng:
    def __init__(self, name, sem):
        self.name = name
        self.sem = sem
        self.cnt = 0
        self.seen = {}
        self.ops = []


class DmaRing:
    def __init__(self, sems):
        self.sems = sems
        self.tot = [0] * len(sems)
        self.last = [None] * len(sems)
        self.i = 0


class Sched:
    def __init__(self, nc, n_dma_sems=12):
        self.nc = nc
        self.E = {}
        for n in ("pe", "act", "dve", "pool"):
            self.E[n] = Eng(n, nc.alloc_semaphore(name=f"sem_{n}"))
        self.E["sp"] = Eng("sp", None)
        self.rings = {}
        for q in ("sp", "pool"):
            self.rings[q] = DmaRing([nc.alloc_semaphore(name=f"dq_{q}_{i}") for i in range(n_dma_sems)])

    def _wait(self, e, tok):
        if e.seen.get(tok.key, 0) >= tok.val:
            return
        e.ops.append(("wait", tok.sem, tok.val))
        seen = e.seen
        for k, v in tok.clock.items():
            if seen.get(k, 0) < v:
                seen[k] = v
        seen[tok.key] = tok.val

    def _dep1(self, en, e, d):
        if en == "pe" and d.sem is e.sem:
            return
        self._wait(e, d)

    def _deps(self, en, e, reads, writes, pwrites):
        for r in reads:
            for d in r.w:
                self._dep1(en, e, d)
        for w in writes:
            for d in w.w:
                self._dep1(en, e, d)
            for d in w.r.values():
                self._dep1(en, e, d)
            for d in w.pw:
                self._dep1(en, e, d)
        for w in pwrites:
            for d in w.pw:
                self._dep1(en, e, d)
            for d in w.r.values():
                self._dep1(en, e, d)

    def _post(self, tok, reads, writes, pwrites):
        self.opidx = getattr(self, "opidx", 0) + 1
        for r in reads:
            r.touch = self.opidx
        for w in writes:
            w.touch = self.opidx
        for w in pwrites:
            w.touch = self.opidx
        for r in reads:
            old = r.r.get(tok.key)
            if old is None or old.val < tok.val:
                r.r[tok.key] = tok
        for w in writes:
            w.w = [tok]
            w.r = {}
            w.pw = []
        for w in pwrites:
            w.w.append(tok)

    def op(self, en, fn, reads=(), writes=(), pwrites=(), signal=True):
        e = self.E[en]
        self._deps(en, e, reads, writes, pwrites)
        if signal:
            e.cnt += 1
            val = e.cnt
        else:
            val = e.cnt + 1
        e.ops.append(("op", fn, signal))
        tok = Tok(e.sem, val, dict(e.seen))
        self._post(tok, reads, writes, pwrites)
        return tok

    def dma(self, q, out_ap, in_ap, reads=(), writes=(), pwrites=()):
        e = self.E[q]
        self._deps(q, e, reads, writes, pwrites)
        ring = self.rings[q]
        i = ring.i
        ring.i = (i + 1) % len(ring.sems)
        if ring.last[i] is not None:
            self._wait(e, ring.last[i])
        ring.tot[i] += 16
        e.ops.append(("dma", out_ap, in_ap, ring.sems[i]))
        tok = Tok(ring.sems[i], ring.tot[i], dict(e.seen))
        ring.last[i] = tok
        self._post(tok, reads, writes, pwrites)
        return tok

    def barrier(self):
        toks = []
        for n, e in self.E.items():
            if e.sem is not None and e.cnt > 0:
                toks.append(Tok(e.sem, e.cnt, {}))
        for q, ring in self.rings.items():
            for t in ring.last:
                if t is not None:
                    toks.append(t)
        for n, e in self.E.items():
            for t in toks:
                if t.sem is e.sem and n == "pe":
                    continue
                self._wait(e, t)

    def replay(self):
        nc = self.nc
        E = self.E

        def run(e, h):
            for o in e.ops:
                k = o[0]
                if k == "wait":
                    h.wait_ge(o[1], o[2])
                elif k == "op":
                    ins = o[1](h)
                    if o[2]:
                        ins.then_inc(e.sem, 1)
                else:
                    h.dma_start(out=o[1], in_=o[2]).then_inc(o[3], 16)

        with nc.Block() as block:
            @block.tensor
            def _(h):
                run(E["pe"], h)

            @block.scalar
            def _(h):
                run(E["act"], h)

            @block.vector
            def _(h):
                run(E["dve"], h)

            @block.gpsimd
            def _(h):
                run(E["pool"], h)

            @block.sync
            def _(h):
                run(E["sp"], h)


V_NMIX, V_NX, V_NFFN, V_NMEM = 0, 8, 16, 24
V_GQ, V_GK = 32, 33
V_CB, V_LNG, V_LNB = 34, 38, 42
V_GXQ, V_GXK = 46, 48
V_SINKP = 50
V_SINKR = 54
V_CW = 62
V_KEEP = 186
V_GQR, V_GKR = 186, 250
V_GXQR, V_GXKR = 314, 570
NV = 826

C_ID = 0
C_1024 = 128
C_BD64 = 256
C_256 = 384
C_512 = 512
C_ONES = 640
C_OPAD = 768
C_BIAS = 960
NCB = C_BIAS + 8 * 256


def _host_consts():
    cb = np.zeros((128, NCB), np.float32)
    cb[:, C_ID:C_ID + 128] = np.eye(128)
    cb[:, C_1024:C_1024 + 128] = 1.0 / 1024
    bd = np.zeros((128, 128), np.float32)
    bd[:64, :64] = 1.0 / 64
    bd[64:, 64:] = 1.0 / 64
    cb[:, C_BD64:C_BD64 + 128] = bd
    cb[:, C_256:C_256 + 128] = 1.0 / 256
    cb[:, C_512:C_512 + 128] = 1.0 / 512
    cb[:, C_ONES:C_ONES + 128] = 1.0
    cb[:, C_OPAD + 64:C_OPAD + 128] = 1.0
    s = np.arange(128)[:, None].astype(np.float32)
    q = np.arange(128)[None, :].astype(np.float32)
    for kb in range(2):
        for g in range(2):
            for par in range(2):
                ti = (g * 2 + par) * 2 + kb
                tab = np.zeros((128, 2, 128), np.float32)
                for j in range(2):
                    h = 4 * g + 2 * j + par
                    slope = 2.0 ** (-(h + 1))
                    if kb == 0:
                        dist = q + 128 - s
                        valid = s > q
                    else:
                        dist = q - s
                        valid = s <= q
                    tab[:, j, :] = np.where(valid, -slope * dist, NEGM)
                cb[:, C_BIAS + ti * 256:C_BIAS + (ti + 1) * 256] = tab.reshape(128, 256)
    return cb.astype(ml_dtypes.bfloat16), np.eye(128, dtype=np.float32)


def _host_vecs(inp):
    out = np.zeros((DEPTH, 128, NV), np.float32)
    p = np.arange(128)
    for l in range(DEPTH):
        v = out[l]

        def cols(vec, n):
            return np.asarray(vec, np.float32).reshape(n, 128).T
        v[:, V_NMIX:V_NMIX + 8] = cols(inp["norm_mix_g"][l], 8)
        v[:, V_NX:V_NX + 8] = cols(inp["norm_x_g"][l], 8)
        v[:, V_NFFN:V_NFFN + 8] = cols(inp["norm_ffn_g"][l], 8)
        v[:, V_NMEM:V_NMEM + 8] = cols(inp["norm_mem_g"][l], 8)
        v[:, V_GQ] = np.asarray(inp["q_norm_g"][l])[p % 64]
        v[:, V_GK] = np.asarray(inp["k_norm_g"][l])[p % 64]
        v[:, V_CB:V_CB + 4] = cols(inp["conv_b"][l], 4)
        v[:, V_LNG:V_LNG + 4] = cols(inp["conv_ln_g"][l], 4)
        v[:, V_LNB:V_LNB + 4] = cols(inp["conv_ln_b"][l], 4)
        v[:, V_GXQ:V_GXQ + 2] = cols(inp["xq_norm_g"][l], 2)
        v[:, V_GXK:V_GXK + 2] = cols(inp["xk_norm_g"][l], 2)
        sk = np.asarray(inp["sinks"][l], np.float32)
        for g in range(2):
            for j in range(2):
                v[:64, V_SINKP + 2 * g + j] = sk[4 * g + 2 * j]
                v[64:, V_SINKP + 2 * g + j] = sk[4 * g + 2 * j + 1]
        v[:, V_SINKR:V_SINKR + 8] = sk[None, :]
        cw = np.asarray(inp["conv_w"][l], np.float32)
        for cc in range(4):
            v[:, V_CW + cc * 31:V_CW + (cc + 1) * 31] = cw[:, cc * 128:(cc + 1) * 128].T
        v[:, V_GQR:V_GQR + 64] = np.asarray(inp["q_norm_g"][l])[None, :]
        v[:, V_GKR:V_GKR + 64] = np.asarray(inp["k_norm_g"][l])[None, :]
        v[:, V_GXQR:V_GXQR + 256] = np.asarray(inp["xq_norm_g"][l])[None, :]
        v[:, V_GXKR:V_GXKR + 256] = np.asarray(inp["xk_norm_g"][l])[None, :]
    return out


def build(stop_after=None, native_silu=True, dump_tile=None):
    nc = bass.Bass("TRN2", target_bir_lowering=False)
    S = Sched(nc)
    dumps = []

    def dump(name, ap, reads):
        shape = list(ap.shape)
        d = nc.dram_tensor("dump_" + name, shape, ap.dtype, kind="ExternalOutput").ap()
        S.dma("sp", d, ap, reads=reads)
        dumps.append(name)

    def din(name, shape, dt=F32):
        return nc.dram_tensor(name, list(shape), dt, kind="ExternalInput").ap()

    x_d = din("x", [SEQ, D])
    mem_d = din("mem", [256, D])
    w_in_d = din("w_in", [DEPTH, D, 1792])
    w_out_d = din("w_out", [DEPTH, D, D])
    wq_d = din("wq_x", [DEPTH, D, D])
    wkv_d = din("wkv_x", [DEPTH, D, 2 * D])
    wo_d = din("wo_x", [DEPTH, D, D])
    wgu_d = din("w_gate_up", [DEPTH, D, 2 * 2816])
    wd_d = din("w_down", [DEPTH, 2816, D])
    vecs_d = din("vecs", [DEPTH, 128, NV])
    cbf_d = din("cbf", [128, NCB], BF16)
    cf_d = din("cf32", [128, 128])
    y_d = nc.dram_tensor("y", [SEQ, D], F32, kind="ExternalOutput").ap()
    xs_d = nc.dram_tensor("xs", [D, SEQ], F32, kind="Internal").ap()
    hs_d = nc.dram_tensor("hs", [D, SEQ], BF16, kind="Internal").ap()
    kvs_d = nc.dram_tensor("kvs", [128, 4096], BF16, kind="Internal").ap()
    xs_v = xs_d.rearrange("(c p) s -> p c s", p=128)
    hs_v = hs_d.rearrange("(c p) s -> p c s", p=128)

    NA = 16 * 1024
    NB = 8 * 1024 + 4 * 31 * 128 + 512
    NB = max(NB, 8 * 2 * HF)
    WA = nc.alloc_sbuf_tensor("WA", [128, NA], BF16)
    WBt = nc.alloc_sbuf_tensor("WBt", [128, NB], BF16)
    RA = [Res(f"WA{i}") for i in range(4)]
    RB = [Res("WB0"), Res("WB1")]
    xbuf = [nc.alloc_sbuf_tensor(f"xb{i}", [128, 8, T], F32) for i in range(2)]
    R_x = [[Res(f"x{i}_{c}") for c in range(8)] for i in range(2)]
    hT = nc.alloc_sbuf_tensor("hT", [128, 8, T], BF16)
    R_h = [Res(f"h{c}") for c in range(8)]
    cbf = nc.alloc_sbuf_tensor("cbf_sb", [128, NCB], BF16)
    identf = nc.alloc_sbuf_tensor("identf", [128, 128], F32)
    R_c = Res("consts")
    pv = nc.alloc_sbuf_tensor("pv", [128, DEPTH, V_KEEP], F32)
    NDV = 16
    dv = nc.alloc_sbuf_tensor("dv", [128, DEPTH, NDV], F32)
    R_v = Res("vecs")
    kvsb = nc.alloc_sbuf_tensor("kvsb", [128, 4096], BF16)
    khx = kvsb[:, 0:2048].rearrange("p (m k) -> p m k", m=8)
    vmem = kvsb[:, 2048:4096].rearrange("p (m d) -> p m d", m=2)
    R_kv = Res("kv")
    stage = [nc.alloc_sbuf_tensor(f"stage{i}", [128, 1024], F32) for i in range(2)]
    R_stage = [Res("stage0"), Res("stage1")]
    TBN = 60 * 1024
    TB = nc.alloc_sbuf_tensor("TB", [128, TBN // 2], BF16)
    PS = [nc.alloc_psum_tensor(f"ps{i}", [128, 512], F32) for i in range(8)]
    R_ps = [Res(f"ps{i}") for i in range(8)]
    R_psh = [[Res(f"psh{i}_{k}") for k in range(2)] for i in range(8)]
    st = {"ps": 0, "tb": 0, "nfull": 8, "ph": 0}

    def set_psum_split(nfull):
        st["nfull"] = nfull
        st["ps"] = 0
        st["ph"] = 0

    def ps():
        i = min(range(8), key=lambda k: R_ps[k].touch)
        R_ps[i].touch = getattr(S, "opidx", 0) + 0.5
        return PS[i], R_ps[i]

    def psh():
        nh = (8 - st["nfull"]) * 2
        k = st["ph"]
        st["ph"] = (k + 1) % nh
        b = st["nfull"] + k // 2
        return PS[b][:, (k % 2) * 256:(k % 2) * 256 + 256], R_psh[b][k % 2]

    def tb_reset():
        st["tb"] = 0

    def tb(shape, dt):
        n = int(np.prod(shape))
        nb = n * (4 if dt == F32 else 2)
        nb = (nb + 63) // 64 * 64
        off = st["tb"]
        assert off + nb <= TBN, (off, nb, TBN)
        st["tb"] = off + nb
        v = TB[:, off // 2:(off + nb) // 2]
        if dt == F32:
            v = v.bitcast(F32)
        v = v[:, 0:n]
        if len(shape) == 2:
            return v.rearrange("p (a b) -> p a b", a=shape[0])
        if len(shape) == 3:
            return v.rearrange("p (a b c) -> p a b c", a=shape[0], b=shape[1])
        return v

    def ACT(out, in_, func, R, W, bias=None, scale=None, PW=()):
        kw = {}
        if bias is not None:
            kw["bias"] = bias
        if scale is not None:
            kw["scale"] = scale
        return S.op("act", lambda h: h.activation(out, in_, func, **kw), reads=R, writes=W, pwrites=PW)

    def TT(en, out, a, b, op, R, W, PW=()):
        return S.op(en, lambda h: h.tensor_tensor(out, a, b, op), reads=R, writes=W, pwrites=PW)

    def STT(out, in0, scalar, in1, op0, op1, R, W, PW=()):
        return S.op("dve", lambda h: h.scalar_tensor_tensor(out, in0, scalar, in1, op0, op1), reads=R, writes=W, pwrites=PW)

    def TS(out, in0, s1, op0, R, W, s2=None, op1=None, PW=()):
        if op1 is None:
            return S.op("dve", lambda h: h.tensor_scalar(out, in0, s1, None, op0), reads=R, writes=W, pwrites=PW)
        return S.op("dve", lambda h: h.tensor_scalar(out, in0, s1, s2, op0, op1), reads=R, writes=W, pwrites=PW)

    def CP(en, out, in_, R, W, PW=()):
        if en == "act":
            return S.op("act", lambda h: h.copy(out, in_), reads=R, writes=W, pwrites=PW)
        return S.op(en, lambda h: h.tensor_copy(out, in_), reads=R, writes=W, pwrites=PW)

    def MM(out, lhsT, rhs, start, stop, R, W, signal):
        return S.op("pe", lambda h: h.matmul(out, lhsT, rhs, start=start, stop=stop), reads=R, writes=W, signal=signal)

    def TR(out, in_, R, W, signal):
        return S.op("pe", lambda h: h.transpose(out, in_, identf[:, :]), reads=list(R) + [R_c], writes=W, signal=signal)

    def cmat(off, n=128):
        return cbf[:, off:off + n]

    def vcol(l, c, n=1):
        return pv[:, l, c:c + n]

    def dcol(l, c, n=1):
        return dv[:, l, c:c + n]

    def rsqrt_from(out, tmp, src, l, R, Rtmp, W):
        ACT(tmp, src, AF.Ln, R=list(R) + [R_v], W=[Rtmp], bias=dcol(l, 9))
        ACT(out, tmp, AF.Exp, R=[Rtmp], W=W, scale=-0.5)

    def compute_barrier():
        toks = []
        for n in ("pe", "act", "dve"):
            e = S.E[n]
            if e.cnt > 0:
                toks.append(Tok(e.sem, e.cnt, {}))
        for n in ("pe", "act", "dve"):
            e = S.E[n]
            for t_ in toks:
                if t_.sem is e.sem and n == "pe":
                    continue
                S._wait(e, t_)

    S.dma("sp", cbf[:, :], cbf_d, writes=[R_c])
    S.dma("sp", identf[:, :], cf_d, pwrites=[R_c])
    tb_reset()
    vfull = tb([DEPTH, NV], F32)
    R_vf = Res("vfull")
    S.dma("sp", vfull, vecs_d.rearrange("l p n -> p l n"), writes=[R_vf])
    CP("dve", pv[:, :, :], vfull[:, :, 0:V_KEEP], R=[R_vf], W=[R_v])
    S.op("dve", lambda h: h.memset(dv[:, :, :], 0.0), writes=[R_v])
    S.op("dve", lambda h: h.memset(dv[:, :, 9:10], EPS), writes=[R_v])
    S.op("dve", lambda h: h.memset(dv[:, :, 10:11], 1.0), writes=[R_v])
    sc1 = tb([256], F32)
    sc2 = tb([8], F32)
    R_s1 = Res("sc1")
    R_s2 = Res("sc2")
    for l in range(DEPTH):
        TS(dcol(l, 0), vcol(l, V_GQ), 0.125, ALU.mult, R=[R_v], W=[R_v])
        TS(dcol(l, 7, 2), vcol(l, V_GXQ, 2), 1.0 / 16, ALU.mult, R=[R_v], W=[R_v])
        TT("dve", sc1[:, 0:64], vfull[:, l, V_GQR:V_GQR + 64], vfull[:, l, V_GKR:V_GKR + 64], ALU.mult, R=[R_vf], W=[R_s1])
        TT("dve", sc1[:, 0:64], sc1[:, 0:64], sc1[:, 0:64], ALU.mult, R=[R_s1], W=[R_s1])
        S.op("dve", lambda h: h.reduce_max(sc2[:, 0:1], sc1[:, 0:64], AX.X), reads=[R_s1], writes=[R_s2])
        ACT(sc2[:, 1:2], sc2[:, 0:1], AF.Ln, R=[R_s2], W=[R_s2])
        ACT(sc2[:, 2:3], sc2[:, 1:2], AF.Exp, R=[R_s2], W=[R_s2], scale=0.5)
        S.op("dve", lambda h, l=l: h.reduce_max(sc2[:, 3:4], vfull[:, l, V_SINKR:V_SINKR + 8], AX.X), reads=[R_vf, R_s2], writes=[R_s2])
        STT(sc2[:, 4:5], sc2[:, 2:3], 8.0, sc2[:, 3:4], ALU.mult, ALU.max, R=[R_s2], W=[R_s2])
        TS(dcol(l, 1), sc2[:, 4:5], -1.0, ALU.mult, R=[R_s2, R_v], W=[R_v])
        ACT(dcol(l, 2, 4), vcol(l, V_SINKP, 4), AF.Exp, R=[R_v], W=[R_v], bias=dcol(l, 1))
        TT("dve", sc1[:, :], vfull[:, l, V_GXQR:V_GXQR + 256], vfull[:, l, V_GXKR:V_GXKR + 256], ALU.mult, R=[R_vf, R_s1], W=[R_s1])
        TT("dve", sc1[:, :], sc1[:, :], sc1[:, :], ALU.mult, R=[R_s1], W=[R_s1])
        S.op("dve", lambda h: h.reduce_max(sc2[:, 5:6], sc1[:, :], AX.X), reads=[R_s1, R_s2], writes=[R_s2])
        ACT(sc2[:, 6:7], sc2[:, 5:6], AF.Ln, R=[R_s2], W=[R_s2])
        ACT(sc2[:, 7:8], sc2[:, 6:7], AF.Exp, R=[R_s2], W=[R_s2], scale=0.5)
        TS(dcol(l, 6), sc2[:, 7:8], -16.0, ALU.mult, R=[R_s2, R_v], W=[R_v])

    wkv_by_layer = {1: (WA[:, 0:8 * 2048].rearrange("p (c n) -> p c n", c=8), RA),
                    0: (WBt[:, 0:8 * 2048].rearrange("p (c n) -> p c n", c=8), RB)}
    for l_ in (1, 0):
        wkv_, Rw_ = wkv_by_layer[l_]
        for c in range(8):
            S.dma("pool", wkv_[:, c, :], wkv_d[l_, c * 128:(c + 1) * 128, :], pwrites=Rw_)
    memsb = xbuf[0][:, 0:4, :].rearrange("p a b -> p (a b)").rearrange("p (m d) -> p m d", m=2)
    R_mem = Res("mem")
    S.dma("sp", memsb, mem_d.rearrange("(m p) d -> p m d", p=128), writes=[R_mem])
    msq = tb([1024], F32)
    R_msq = Res("msq")
    mst = tb([8], F32)
    R_mst = Res("mst")
    mn = xbuf[1][:, 0:4, :].rearrange("p a b -> p (a b)").rearrange("p (m d) -> p m d", m=2)
    R_mn = Res("mn")
    for mb in range(2):
        ACT(msq[:, :], memsb[:, mb, :], AF.Square, R=[R_mem], W=[R_msq])
        S.op("dve", lambda h, mb=mb: h.reduce_sum(mst[:, mb:mb + 1], msq[:, :], AX.X), reads=[R_msq, R_mst], writes=[R_mst])
    TS(mst[:, 2:4], mst[:, 0:2], 1.0 / 1024, ALU.mult, R=[R_mst], W=[R_mst])
    ACT(mst[:, 4:6], mst[:, 2:4], AF.Ln, R=[R_mst, R_v], W=[R_mst], bias=dcol(0, 9))
    ACT(mst[:, 6:8], mst[:, 4:6], AF.Exp, R=[R_mst], W=[R_mst], scale=-0.5)
    for mb in range(2):
        TS(mn[:, mb, :], memsb[:, mb, :], mst[:, 6 + mb:7 + mb], ALU.mult, R=[R_mem, R_mst], W=[R_mn])
    memT = tb([8, 256], BF16)
    R_mT = Res("memT")
    ksq = tb([2, 256], BF16)
    R_ksq = Res("ksq")
    krs = tb([256], F32)
    R_krs = Res("krs")
    klt = tb([256], F32)
    R_klt = Res("klt")
    for l in (1, 0):
        wkv, Rwkv = wkv_by_layer[l]
        R_mT.new_epoch()
        R_kv.new_epoch()
        for c in range(8):
            p, Rp = ps()
            for mb in range(2):
                TR(p[:, mb * 128:(mb + 1) * 128], mn[:, mb, c * 128:(c + 1) * 128], R=[R_mn], W=[Rp], signal=(mb == 1))
            TS(memT[:, c, :], p[:, 0:256], vcol(l, V_NMEM + c), ALU.mult, R=[Rp, R_v], W=[], PW=[R_mT])
        for h_ in range(4):
            pab = []
            R_ksq.new_epoch()
            for a in range(2):
                p, Rp = ps()
                m = 2 * h_ + a
                for c in range(8):
                    MM(p[:, 0:256], wkv[:, c, m * 128:(m + 1) * 128], memT[:, c, :], c == 0, c == 7, R=list(Rwkv) + [R_mT], W=[Rp], signal=(c == 7))
                ACT(ksq[:, a, :], p[:, 0:256], AF.Square, R=[Rp], W=[], PW=[R_ksq])
                pab.append((p, Rp))
            pm, Rpm = ps()
            for a in range(2):
                MM(pm[:, 0:256], cmat(C_256), ksq[:, a, :], a == 0, a == 1, R=[R_ksq, R_c], W=[Rpm], signal=(a == 1))
            rsqrt_from(krs[:, :], klt[:, :], pm[:, 0:256], l, R=[Rpm], Rtmp=R_klt, W=[R_krs])
            for a in range(2):
                p, Rp = pab[a]
                STT(khx[:, 2 * h_ + a, :], p[:, 0:256], vcol(l, V_GXK + a), krs[:, :], ALU.mult, ALU.mult, R=[Rp, R_krs, R_v], W=[], PW=[R_kv])
        for mb in range(2):
            for n2 in range(2):
                p, Rp = ps()
                for c in range(8):
                    MM(p[:, :], memT[:, c, mb * 128:(mb + 1) * 128], wkv[:, c, 1024 + n2 * 512:1024 + (n2 + 1) * 512], c == 0, c == 7, R=list(Rwkv) + [R_mT], W=[Rp], signal=(c == 7))
                CP("act", vmem[:, mb, n2 * 512:(n2 + 1) * 512], p[:, :], R=[Rp], W=[], PW=[R_kv])
        if l == 1:
            S.dma("sp", kvs_d, kvsb[:, :], reads=[R_kv])
    compute_barrier()
    S.barrier()
    for r in (R_kv, R_v, R_c, R_mT):
        r.w = []
        r.r = {}
        r.pw = []

    def wview(region, off, c, n):
        t_ = WA if region == "A" else WBt
        return t_[:, off:off + c * n].rearrange("p (c n) -> p c n", c=c)

    w_in_v = wview("A", 0, 8, 2048)
    w_out_v = wview("B", 0, 8, 1024)
    _dgo = 8 * 1024
    diagw_v = WBt[:, _dgo:_dgo + 4 * 31 * 128].rearrange("p (a j m) -> p a j m", a=4, j=31)
    diagw_f = WBt[:, _dgo:_dgo + 4 * 31 * 128].rearrange("p (a m) -> p a m", a=124)
    wq_v = wview("A", 0, 8, 1024)
    wo_v = wview("A", 8 * 1024, 8, 1024)
    wgu_v = wview("B", 0, 8, 2 * HF)
    wd_v = WA[:, 0:NJ * 1024].rearrange("p (j n) -> p j n", j=NJ)
    W = {"M": (w_in_v, w_out_v, diagw_v), "X": (wq_v, wo_v), "FA": (wgu_v, wd_v), "FB": (wgu_v, wd_v)}

    def load_region(kind, l, region):
        if kind == "M" and region == "A":
            src = w_in_d[l].rearrange("(c p) n -> p c n", p=128)
            for (bi_, c0, c1) in ((0, 0, 512), (1, 512, 768), (3, 1280, 1792), (2, 768, 1280)):
                RA[bi_].new_epoch()
                S.dma("pool", w_in_v[:, :, c0:c1], src[:, :, c0:c1], pwrites=[RA[bi_]])
                if bi_ == 1:
                    for (dst, sc_) in ((1792, 512), (1856, 512), (1920, 576), (1984, 576)):
                        S.op("pool", lambda h, dst=dst, sc_=sc_: h.tensor_copy(w_in_v[:, :, dst:dst + 64], w_in_v[:, :, sc_:sc_ + 64]),
                             reads=[RA[1]], pwrites=[RA[1]])
        elif kind == "M" and region == "B1":
            for r_ in RB:
                r_.new_epoch()
            for c in range(8):
                S.dma("pool", w_out_v[:, c, :], w_out_d[l, c * 128:(c + 1) * 128, :], pwrites=RB)
            S.op("dve", lambda h: h.tensor_tensor(
                diagw_f, cmat(C_ID).unsqueeze(1).to_broadcast([128, 124, 128]),
                pv[:, l, V_CW:V_CW + 124].unsqueeze(2).to_broadcast([128, 124, 128]), ALU.mult),
                reads=[R_c, R_v], pwrites=RB)
        elif kind == "X" and region == "A":
            for r_ in RA:
                r_.new_epoch()
            for c in range(8):
                S.dma("pool", wq_v[:, c, :], wq_d[l, c * 128:(c + 1) * 128, :], pwrites=RA)
            for c in range(8):
                S.dma("pool", wo_v[:, c, :], wo_d[l, c * 128:(c + 1) * 128, :], pwrites=RA)
        elif kind in ("FA", "FB") and region in ("B0", "B1"):
            f0 = (0 if kind == "FA" else 1) * HF
            hb = 0 if region == "B0" else 1
            j0, j1 = (0, JS) if hb == 0 else (JS, NJ)
            RB[hb].new_epoch()
            for c in range(8):
                S.dma("pool", wgu_v[:, c, j0 * 128:j1 * 128], wgu_d[l, c * 128:(c + 1) * 128, f0 + j0 * 128:f0 + j1 * 128], pwrites=[RB[hb]])
                S.dma("pool", wgu_v[:, c, HF + j0 * 128:HF + j1 * 128],
                      wgu_d[l, c * 128:(c + 1) * 128, 2816 + f0 + j0 * 128:2816 + f0 + j1 * 128], pwrites=[RB[hb]])
        elif kind in ("FA", "FB") and region == "A":
            f0 = (0 if kind == "FA" else 1) * HF
            for r_ in RA:
                r_.new_epoch()
            wdv = wd_d[l, f0:f0 + HF, :].rearrange("(j p) n -> p j n", p=128)
            for (j0, j1) in ((0, 4), (4, 8), (8, 11)):
                S.dma("pool", wd_v[:, j0:j1, :], wdv[:, j0:j1, :], pwrites=RA)

    R_xs = [Res(f"xs{t}") for t in range(NT)]
    R_hs = [Res(f"hs{t}") for t in range(NT)]

    def load_x(t, bi):
        S.dma("sp", xbuf[bi][:, :, :], xs_v[:, :, t * T:(t + 1) * T], reads=[R_xs[t]], writes=R_x[bi])

    def store_x(t, bi):
        S.dma("sp", xs_v[:, :, t * T:(t + 1) * T], xbuf[bi][:, :, :], reads=R_x[bi], writes=[R_xs[t]])

    def store_h(t):
        S.dma("sp", hs_v[:, :, t * T:(t + 1) * T], hT[:, :, :], reads=R_h, writes=[R_hs[t]])

    def load_h(t):
        S.dma("sp", hT[:, :, :], hs_v[:, :, t * T:(t + 1) * T], reads=[R_hs[t]], writes=R_h)

    class NextPrep:
        def __init__(self, kind, l):
            self.kind = kind
            self.l = l
            self.gbase = {"M": V_NMIX, "X": V_NX, "FA": V_NFFN, "FB": None}[kind]

        def load(self):
            load_x(0, 0)

        def squares(self, sq, R_sq, eng="act"):
            if self.gbase is not None:
                norm_squares(0, sq, R_sq, eng)

        def finish(self, sq, R_sq, tmpf):
            if self.gbase is not None:
                norm_finish(self.l, 0, self.gbase, sq, R_sq, tmpf)
                if self.kind == "FA":
                    store_h(0)
            else:
                load_h(0)

        def stats(self, sq, R_sq, tmpf, rstd, R_rstd):
            if self.gbase is not None:
                norm_stats(self.l, sq, R_sq, tmpf, rstd, R_rstd)

        def apply(self, rstd, R_rstd):
            if self.gbase is not None:
                norm_apply(self.l, 0, self.gbase, rstd, R_rstd)
                if self.kind == "FA":
                    store_h(0)
            else:
                load_h(0)

    scnt = {"i": 0}

    def tokmajor_dma(t, k):
        si = k % 2
        r0 = t * T + k * 128
        S.dma("sp", stage[si][:, :], x_d[r0:r0 + 128, :], writes=[R_stage[si]])

    def tokmajor_tr(t, bi, k):
        si = k % 2
        for i in range(2):
            p, Rp = ps()
            for mm_ in range(4):
                c = 4 * i + mm_
                TR(p[:, mm_ * 128:(mm_ + 1) * 128], stage[si][:, c * 128:(c + 1) * 128], R=[R_stage[si]], W=[Rp], signal=(mm_ == 3))
            pview = p[:, :].rearrange("p (c n) -> p c n", c=4)
            if k == 0:
                for mm_ in range(4):
                    R_x[bi][4 * i + mm_].new_epoch()
            CP("act" if i else "dve", xbuf[bi][:, 4 * i:4 * i + 4, k * 128:(k + 1) * 128], pview, R=[Rp], W=[],
               PW=[R_x[bi][4 * i + mm_] for mm_ in range(4)])

    def load_x_tokmajor(t, bi):
        for k in range(4):
            tokmajor_dma(t, k)
            tokmajor_tr(t, bi, k)

    def transpose_out(t, bi):
        for k in range(4):
            si = scnt["i"] % 2
            scnt["i"] += 1
            R_stage[si].new_epoch()
            for i in range(2):
                p, Rp = ps()
                for mm_ in range(4):
                    m = 4 * i + mm_
                    TR(p[:, mm_ * 128:(mm_ + 1) * 128], xbuf[bi][:, m, k * 128:(k + 1) * 128], R=[R_x[bi][m]], W=[Rp], signal=(mm_ == 3))
                CP("act" if i else "dve", stage[si][:, i * 512:(i + 1) * 512], p[:, :], R=[Rp], W=[], PW=[R_stage[si]])
            r0 = t * T + k * 128
            S.dma("sp", y_d[r0:r0 + 128, :], stage[si][:, :], reads=[R_stage[si]])

    def norm_squares(bi, sq, R_sq, eng="act"):
        for c in range(8):
            if eng == "act":
                ACT(sq[:, c, :], xbuf[bi][:, c, :], AF.Square, R=[R_x[bi][c]], W=[R_sq[c]])
            else:
                TT("dve", sq[:, c, :], xbuf[bi][:, c, :], xbuf[bi][:, c, :], ALU.mult, R=[R_x[bi][c]], W=[R_sq[c]])

    def norm_finish(l, bi, gbase, sq, R_sq, tmpf):
        rstd, R_rstd = tmpf()
        lnt, R_lnt = tmpf()
        p, Rp = ps()
        for c in range(8):
            MM(p[:, :], cmat(C_1024), sq[:, c, :], c == 0, c == 7, R=[R_sq[c], R_c], W=[Rp], signal=(c == 7))
        rsqrt_from(rstd[:, :], lnt[:, :], p[:, :], l, R=[Rp], Rtmp=R_lnt, W=[R_rstd])
        for c in range(8):
            STT(hT[:, c, :], xbuf[bi][:, c, :], vcol(l, gbase + c), rstd[:, :], ALU.mult, ALU.mult,
                R=[R_x[bi][c], R_rstd, R_v], W=[R_h[c]])

    def norm_stats(l, sq, R_sq, tmpf, rstd, R_rstd):
        lnt, R_lnt = tmpf()
        p, Rp = ps()
        for c in range(8):
            MM(p[:, :], cmat(C_1024), sq[:, c, :], c == 0, c == 7, R=[R_sq[c], R_c], W=[Rp], signal=(c == 7))
        rsqrt_from(rstd[:, :], lnt[:, :], p[:, :], l, R=[Rp], Rtmp=R_lnt, W=[R_rstd])

    def norm_apply(l, bi, gbase, rstd, R_rstd):
        for c in range(8):
            STT(hT[:, c, :], xbuf[bi][:, c, :], vcol(l, gbase + c), rstd[:, :], ALU.mult, ALU.mult,
                R=[R_x[bi][c], R_rstd, R_v], W=[R_h[c]])

    def residual_chunk(bi, m, wmat, RW, nk, rhsT, R_rhs):
        p, Rp = ps()
        for k in range(nk):
            MM(p[:, :], wmat[:, k, m * 128:(m + 1) * 128], rhsT[:, k, :], k == 0, k == nk - 1,
               R=list(RW) + [R_rhs[k]], W=[Rp], signal=(k == nk - 1))
        TT("dve", xbuf[bi][:, m, :], xbuf[bi][:, m, :], p[:, :], ALU.add, R=[Rp, R_x[bi][m]], W=[R_x[bi][m]])

    sqpool = {"tiles": None, "i": 0}

    def ptile_big():
        tl = sqpool["tiles"]
        i = sqpool["i"]
        sqpool["i"] = (i + 1) % len(tl)
        return tl[i]

    def alloc_sqpool(n):
        sqpool["tiles"] = [(tb([T], BF16), Res(f"sqq{i}")) for i in range(n)]
        sqpool["i"] = 0

    def make_rot(tiles, ress):
        c = {"i": 0}

        def f():
            i = c["i"]
            c["i"] = (i + 1) % len(tiles)
            return tiles[i], ress[i]
        return f

    def phase_M(l, next_load, prepped, nextp):
        tb_reset()
        set_psum_split(8)
        w_in, w_out, diagw = W["M"]
        sq = tb([8, T], BF16)
        R_sq = [Res(f"sq{c}") for c in range(8)]
        qhat = tb([4, T], BF16)
        R_q = [Res(f"q{i}") for i in range(4)]
        khat = tb([2, 128 + T], BF16)
        R_k = Res("khat")
        vpad = tb([5, 2, 192], BF16)
        R_vp = Res("vpad")
        gbuf = tb([4, 30 + T], BF16)
        R_g = [Res(f"g{i}") for i in range(4)]
        mixT = tb([8, T], BF16)
        R_mix = [Res(f"mix{i}") for i in range(8)]
        ycv = tb([4, T], F32)
        R_y = [Res(f"y{i}") for i in range(4)]
        NTMP = 5
        tmp = make_rot([tb([T], F32) for _ in range(NTMP)], [Res(f"tmp{i}") for i in range(NTMP)])
        nrstd = tb([T], F32)
        R_nrstd = Res("nrstd")
        NP = 6
        ptile = make_rot([tb([T], BF16) for _ in range(NP)], [Res(f"P{i}") for i in range(NP)])
        alloc_sqpool(3)

        S.op("dve", lambda h: h.memset(vpad.rearrange("p a b c -> p (a b c)"), 0.0), writes=[R_vp])
        for cc in range(4):
            S.op("dve", lambda h, cc=cc: h.memset(gbuf[:, cc, 0:30], 0.0), writes=[R_g[cc]])

        def sigmoid_of(src, Rsrc):
            e, Re = tmp()
            ACT(e[:, :], src, AF.Exp, R=Rsrc, W=[Re], scale=-1.0)
            l_, Rl = tmp()
            ACT(l_[:, :], e[:, :], AF.Ln, R=[Re, R_v], W=[Rl], bias=dcol(l, 10))
            s_, Rs = tmp()
            ACT(s_[:, :], l_[:, :], AF.Exp, R=[Rl], W=[Rs], scale=-1.0)
            return s_, Rs

        def qk_mm(spec):
            col, kind, idx = spec
            p, Rp = ps()
            for c in range(8):
                MM(p[:, :], w_in[:, c, col:col + 128], hT[:, c, :], c == 0, c == 7, R=[RA[0 if kind == "q" else 1], R_h[c]], W=[Rp], signal=(c == 7))
            sqq, Rsqq = ptile_big()
            ACT(sqq[:, :], p[:, :], AF.Square, R=[Rp], W=[Rsqq])
            return (spec, p, Rp, sqq, Rsqq)

        def qk_fin(state):
            (col, kind, idx), p, Rp, sqq, Rsqq = state
            p2, Rp2 = ps()
            MM(p2[:, :], cmat(C_BD64), sqq[:, :], True, True, R=[Rsqq, R_c], W=[Rp2], signal=True)
            lt, Rlt = tmp()
            rs, Rrs = tmp()
            rsqrt_from(rs[:, :], lt[:, :], p2[:, :], l, R=[Rp2], Rtmp=Rlt, W=[Rrs])
            if kind == "q":
                STT(qhat[:, idx, :], p[:, :], dcol(l, 0), rs[:, :], ALU.mult, ALU.mult, R=[Rp, Rrs, R_v], W=[R_q[idx]])
            else:
                STT(khat[:, idx, 128:128 + T], p[:, :], vcol(l, V_GK), rs[:, :], ALU.mult, ALU.mult, R=[Rp, Rrs, R_v], W=[R_k])

        def scores(t, b, g):
            n = 4 * t + b
            kbs = ([0] if n > 0 else []) + [1]
            c0 = kbs[0] * 256
            banks = [ps(), ps()]
            for kb in kbs:
                for par in range(2):
                    p, Rp = banks[par]
                    pview = p[:, kb * 256:(kb + 1) * 256].rearrange("p (a b) -> p a b", a=2)
                    kblk = khat[64 * par:64 * par + 64, g, (b + kb) * 128:(b + kb + 1) * 128]
                    qrhs = qhat[64 * par:64 * par + 64, 2 * g:2 * g + 2, b * 128:(b + 1) * 128]
                    MM(pview, kblk, qrhs, True, False, R=[R_k, R_q[2 * g], R_q[2 * g + 1]], W=[Rp], signal=False)
                    ti = (g * 2 + par) * 2 + kb
                    MM(p[:, kb * 256:(kb + 1) * 256], cmat(C_ID), cbf[:, C_BIAS + ti * 256:C_BIAS + (ti + 1) * 256], False, True,
                       R=[R_c], W=[Rp], signal=(kb == 1))
            plist = []
            for par in range(2):
                p, Rp = banks[par]
                P_, RP_ = ptile()
                ACT(P_[:, c0:512], p[:, c0:512], AF.Exp, R=[Rp, R_v], W=[RP_], bias=dcol(l, 1))
                plist.append((par, P_, RP_))
            return (kbs, plist)

        def pv_den(t, b, g, sc):
            kbs, plist = sc
            pnd, Rpnd = ps()
            pn = pnd[:, 0:256]
            pd = pnd[:, 256:512]
            items = [(kb, par, P_, RP_) for (par, P_, RP_) in plist for kb in kbs]
            nmm = len(items)
            for i, (kb, par, P_, RP_) in enumerate(items):
                lo = 64 if par == 0 else 0
                MM(pn, vpad[:, b + kb, g, lo:lo + 128], P_[:, kb * 256:(kb + 1) * 256], i == 0, i == nmm - 1, R=[R_vp, RP_], W=[Rpnd], signal=False)
            for i, (kb, par, P_, RP_) in enumerate(items):
                lo = 64 if par == 0 else 0
                MM(pd, cbf[:, C_OPAD + lo:C_OPAD + lo + 128], P_[:, kb * 256:(kb + 1) * 256], i == 0, i == nmm - 1, R=[R_c, RP_], W=[Rpnd], signal=(i == nmm - 1))
            ld, Rld = tmp()
            R_first = True
            for j in range(2):
                S.op("act", lambda h, j=j: h.activation(ld[:, j * 128:(j + 1) * 128], pd[:, j * 128:(j + 1) * 128], AF.Ln,
                                                        bias=dcol(l, 2 + 2 * g + j)),
                     reads=[Rpnd, R_v], writes=[Rld] if j == 0 else [], pwrites=[] if j == 0 else [Rld])
            rd, Rrd = tmp()
            ACT(rd[:, 0:256], ld[:, 0:256], AF.Exp, R=[Rld], W=[Rrd], scale=-1.0)
            TT("dve", mixT[:, 2 * g:2 * g + 2, b * 128:(b + 1) * 128], pn.rearrange("p (a b) -> p a b", a=2),
               rd[:, 0:256].rearrange("p (a b) -> p a b", a=2), ALU.mult, R=[Rpnd, Rrd], W=[], PW=[R_mix[2 * g], R_mix[2 * g + 1]])

        def conv_taps_dve(cc):
            TS(ycv[:, cc, :], gbuf[:, cc, 0:T], vcol(l, V_CW + cc * 31), ALU.mult, R=[R_g[cc], R_v], W=[R_y[cc]])
            for j in range(1, TD):
                STT(ycv[:, cc, :], gbuf[:, cc, j:j + T], vcol(l, V_CW + cc * 31 + j), ycv[:, cc, :], ALU.mult, ALU.add,
                    R=[R_g[cc], R_v, R_y[cc]], W=[R_y[cc]])

        def conv_chunk(cc):
            p, Rp = ps()
            for j in range(TD, 31):
                MM(p[:, :], diagw[:, cc, j, :], gbuf[:, cc, j:j + T], j == TD, j == 30, R=RB + [R_g[cc]], W=[Rp], signal=(j == 30))
            STT(ycv[:, cc, :], p[:, :], vcol(l, V_CB + cc), ycv[:, cc, :], ALU.add, ALU.add, R=[Rp, R_v, R_y[cc]], W=[R_y[cc]])
            S.op("dve", lambda h, cc=cc: h.tensor_copy(gbuf[:, cc, 0:30], gbuf[:, cc, T:T + 30]), reads=[], writes=[R_g[cc]])

        ln_state = {}

        def ln_part1():
            for cc in range(4):
                CP("dve", sq[:, cc, :], ycv[:, cc, :], R=[R_y[cc]], W=[R_sq[cc]])
            pm, Rpm = ps()
            for cc in range(4):
                MM(pm[:, :], cmat(C_512), sq[:, cc, :], cc == 0, cc == 3, R=[R_sq[cc], R_c], W=[Rpm], signal=(cc == 3))
            for cc in range(4):
                TT("dve", ycv[:, cc, :], ycv[:, cc, :], pm[:, :], ALU.subtract, R=[Rpm, R_y[cc]], W=[R_y[cc]])
                ACT(sq[:, 4 + cc, :], ycv[:, cc, :], AF.Square, R=[R_y[cc]], W=[R_sq[4 + cc]])

        def ln_part2():
            pvv, Rpvv = ps()
            for cc in range(4):
                MM(pvv[:, :], cmat(C_512), sq[:, 4 + cc, :], cc == 0, cc == 3, R=[R_sq[4 + cc], R_c], W=[Rpvv], signal=(cc == 3))
            lt, Rlt = tmp()
            rs, Rrs = tmp()
            rsqrt_from(rs[:, :], lt[:, :], pvv[:, :], l, R=[Rpvv], Rtmp=Rlt, W=[Rrs])
            for cc in range(4):
                STT(ycv[:, cc, :], ycv[:, cc, :], vcol(l, V_LNG + cc), rs[:, :], ALU.mult, ALU.mult, R=[R_y[cc], Rrs, R_v], W=[R_y[cc]])

        def ln_part3(cc):
            wp, Rwp = tmp()
            ACT(wp[:, :], ycv[:, cc, :], AF.Identity, R=[R_y[cc], R_v], W=[Rwp], bias=vcol(l, V_LNB + cc))
            s_, Rs = sigmoid_of(wp[:, :], [Rwp])
            TT("dve", mixT[:, 4 + cc, :], wp[:, :], s_[:, :], ALU.mult, R=[Rwp, Rs], W=[R_mix[4 + cc]])

        def pre(t):
            bi = t % 2
            if l == 0:
                load_x_tokmajor(t, bi)
            else:
                load_x(t, bi)

        if not prepped:
            pre(0)
            norm_squares(0, sq, R_sq)
            norm_finish(l, 0, V_NMIX, sq, R_sq, tmp)
        for t in range(NT):
            bi = t % 2
            last_tile = (t == NT - 1)
            if not last_tile and l > 0:
                pre(t + 1)
            elif last_tile and nextp is not None:
                nextp.load()
            if not last_tile and l == 0:
                tokmajor_dma(t + 1, 0)
                tokmajor_dma(t + 1, 1)
            specs = [(i * 128, "q", i) for i in range(4)] + [(1792, "k", 0), (1920, "k", 1)]
            prev = None
            for sp_ in specs:
                cur = qk_mm(sp_)
                if prev is not None:
                    qk_fin(prev)
                prev = cur
            p, Rp = ps()
            for k in range(4):
                for c in range(8):
                    MM(p[:, k * 128:(k + 1) * 128], hT[:, c, k * 128:(k + 1) * 128], w_in[:, c, 640:768], c == 0, c == 7,
                       R=[RA[1], R_h[c]], W=[Rp], signal=(k == 3 and c == 7))
            qk_fin(prev)
            S.op("act", lambda h, p=p: h.copy(vpad[:, 1:5, :, 64:128], p[:, :].rearrange("p (k g d) -> p k g d", k=4, g=2)),
                 reads=[Rp], writes=[R_vp])
            if not last_tile and l == 0:
                tokmajor_tr(t + 1, (t + 1) % 2, 0)
                tokmajor_tr(t + 1, (t + 1) % 2, 1)
                tokmajor_dma(t + 1, 2)
                tokmajor_dma(t + 1, 3)
            for cc in range(4):
                pv_, Rpv = ps()
                pg, Rpg = ps()
                for c in range(8):
                    MM(pg[:, :], w_in[:, c, 1280 + cc * 128:1280 + (cc + 1) * 128], hT[:, c, :], c == 0, c == 7, R=[RA[3], R_h[c]], W=[Rpg], signal=(c == 7))
                for c in range(8):
                    MM(pv_[:, :], w_in[:, c, 768 + cc * 128:768 + (cc + 1) * 128], hT[:, c, :], c == 0, c == 7, R=[RA[2], R_h[c]], W=[Rpv], signal=(c == 7))
                s_, Rs = sigmoid_of(pg[:, :], [Rpg])
                TT("dve", gbuf[:, cc, 30:30 + T], pv_[:, :], s_[:, :], ALU.mult, R=[Rpv, Rs], W=[R_g[cc]])
                conv_taps_dve(cc)
            if last_tile and next_load is not None:
                next_load("A")
            if not last_tile and l == 0:
                tokmajor_tr(t + 1, (t + 1) % 2, 2)
                tokmajor_tr(t + 1, (t + 1) % 2, 3)
            for cc in range(4):
                conv_chunk(cc)
            for i_ in range(4):
                R_mix[i_].new_epoch()
            units = [(b, g) for b in range(4) for g in range(2)]
            LOOK = 2
            pend = []
            for ui, (b, g) in enumerate(units):
                pl = scores(t, b, g)
                pend.append((b, g, pl))
                if len(pend) > LOOK:
                    pb, pg_, ppl = pend.pop(0)
                    pv_den(t, pb, pg_, ppl)
                if ui == 0:
                    ln_part1()
                if ui == 2:
                    ln_part2()
                if ui in (2, 3, 4, 5):
                    ln_part3(ui - 2)
                if ui == 5:
                    if not last_tile:
                        norm_squares((t + 1) % 2, sq, R_sq, "dve")
                    elif nextp is not None:
                        nextp.squares(sq, R_sq, "dve")
            while pend:
                pb, pg_, ppl = pend.pop(0)
                pv_den(t, pb, pg_, ppl)
            if not last_tile:
                norm_stats(l, sq, R_sq, tmp, nrstd, R_nrstd)
            elif nextp is not None:
                nextp.stats(sq, R_sq, tmp, nrstd, R_nrstd)
            S.op("dve", lambda h: h.tensor_copy(khat[:, :, 0:128], khat[:, :, T:T + 128]), reads=[], writes=[R_k])
            S.op("dve", lambda h: h.tensor_copy(vpad[:, 0, :, :], vpad[:, 4, :, :]), reads=[], writes=[R_vp])
            if dump_tile is not None and t == dump_tile and l == 0:
                dump("mixT", mixT, R_mix)
            if not last_tile:
                norm_apply(l, (t + 1) % 2, V_NMIX, nrstd, R_nrstd)
            elif nextp is not None:
                nextp.apply(nrstd, R_nrstd)
            for m in range(8):
                residual_chunk(bi, m, w_out, RB, 8, mixT, R_mix)
            store_x(t, bi)
        if next_load is not None:
            next_load("B0")
            next_load("B1")

    def phase_X(l, next_load, prepped, nextp):
        tb_reset()
        set_psum_split(8)
        wq, wo = W["X"]
        sq = tb([8, T], BF16)
        R_sq = [Res(f"xsq{c}") for c in range(8)]
        qx = tb([8, T], BF16)
        R_qx = [Res(f"qx{i}") for i in range(8)]
        oT = tb([8, T], BF16)
        R_o = [Res(f"o{i}") for i in range(8)]
        NTMP = 4
        tmp = make_rot([tb([T], F32) for _ in range(NTMP)], [Res(f"xtmp{i}") for i in range(NTMP)])
        NP = 6
        ptile = make_rot([tb([T], BF16) for _ in range(NP)], [Res(f"xP{i}") for i in range(NP)])
        alloc_sqpool(4)
        if l == 1:
            S.dma("sp", kvsb[:, :], kvs_d, writes=[R_kv])
        if next_load is not None:
            next_load("B0")
            next_load("B1")

        def q_mm(h_):
            pab = []
            sqs = []
            for a in range(2):
                m = 2 * h_ + a
                p, Rp = ps()
                for c in range(8):
                    MM(p[:, :], wq[:, c, m * 128:(m + 1) * 128], hT[:, c, :], c == 0, c == 7, R=RA + [R_h[c]], W=[Rp], signal=(c == 7))
                sqq, Rsqq = ptile_big()
                ACT(sqq[:, :], p[:, :], AF.Square, R=[Rp], W=[Rsqq])
                pab.append((p, Rp))
                sqs.append((sqq, Rsqq))
            return (h_, pab, sqs)

        def q_fin(state):
            h_, pab, sqs = state
            pm, Rpm = ps()
            for a in range(2):
                MM(pm[:, :], cmat(C_256), sqs[a][0][:, :], a == 0, a == 1, R=[sqs[a][1], R_c], W=[Rpm], signal=(a == 1))
            lt, Rlt = tmp()
            rs, Rrs = tmp()
            rsqrt_from(rs[:, :], lt[:, :], pm[:, :], l, R=[Rpm], Rtmp=Rlt, W=[Rrs])
            for a in range(2):
                p, Rp = pab[a]
                STT(qx[:, 2 * h_ + a, :], p[:, :], dcol(l, 7 + a), rs[:, :], ALU.mult, ALU.mult, R=[Rp, Rrs, R_v], W=[R_qx[2 * h_ + a]])

        def xscores(h_):
            Ps = []
            for mb in range(2):
                p, Rp = ps()
                for a in range(2):
                    m = 2 * h_ + a
                    MM(p[:, :], khx[:, m, mb * 128:(mb + 1) * 128], qx[:, m, :], a == 0, a == 1, R=[R_kv, R_qx[m]], W=[Rp], signal=(a == 1))
                P_, RP_ = ptile()
                ACT(P_[:, :], p[:, :], AF.Exp, R=[Rp, R_v], W=[RP_], bias=dcol(l, 6))
                Ps.append((P_, RP_))
            return (h_, Ps)

        def xpv(state):
            h_, Ps = state
            pd, Rpd = ps()
            for mb in range(2):
                MM(pd[:, :], cmat(C_ONES), Ps[mb][0][:, :], mb == 0, mb == 1, R=[R_c, Ps[mb][1]], W=[Rpd], signal=(mb == 1))
            pns = []
            for a in range(2):
                m = 2 * h_ + a
                pn, Rpn = ps()
                for mb in range(2):
                    MM(pn[:, :], vmem[:, mb, m * 128:(m + 1) * 128], Ps[mb][0][:, :], mb == 0, mb == 1, R=[R_kv, Ps[mb][1]], W=[Rpn], signal=(mb == 1))
                pns.append((pn, Rpn))
            ld, Rld = tmp()
            ACT(ld[:, :], pd[:, :], AF.Ln, R=[Rpd], W=[Rld])
            rd, Rrd = tmp()
            ACT(rd[:, :], ld[:, :], AF.Exp, R=[Rld], W=[Rrd], scale=-1.0)
            for a in range(2):
                m = 2 * h_ + a
                TT("dve", oT[:, m, :], pns[a][0][:, :], rd[:, :], ALU.mult, R=[pns[a][1], Rrd], W=[R_o[m]])

        if not prepped:
            load_x(0, 0)
            norm_squares(0, sq, R_sq)
            norm_finish(l, 0, V_NX, sq, R_sq, tmp)
        for t in range(NT):
            bi = t % 2
            last_tile = (t == NT - 1)
            if not last_tile:
                load_x(t + 1, (t + 1) % 2)
            elif nextp is not None:
                nextp.load()
            prev = None
            for h_ in range(4):
                cur = q_mm(h_)
                if prev is not None:
                    q_fin(prev)
                prev = cur
            q_fin(prev)
            if not last_tile:
                norm_squares((t + 1) % 2, sq, R_sq, "dve")
            elif nextp is not None:
                nextp.squares(sq, R_sq, "dve")
            prevs = None
            for h_ in range(4):
                cur = xscores(h_)
                if prevs is not None:
                    xpv(prevs)
                prevs = cur
            xpv(prevs)
            for m in range(8):
                residual_chunk(bi, m, wo, RA, 8, oT, R_o)
                if m == 1 and not last_tile:
                    norm_finish(l, (t + 1) % 2, V_NX, sq, R_sq, tmp)
                elif m == 1 and nextp is not None:
                    nextp.finish(sq, R_sq, tmp)
            store_x(t, bi)
        if next_load is not None:
            next_load("A")

    def phase_F(l, half, last, next_load, prepped, nextp):
        tb_reset()
        set_psum_split(8)
        wgu, wd = W["FA" if half == 0 else "FB"]
        sq = tb([8, T], BF16)
        R_sq = [Res(f"fsq{c}") for c in range(8)]
        actT = tb([NJ, T], BF16)
        R_a = [Res(f"a{i}") for i in range(NJ)]
        NTMP = 6
        tmp = make_rot([tb([T], F32) for _ in range(NTMP)], [Res(f"ftmp{i}") for i in range(NTMP)])

        def gu_mm(j):
            pg, Rpg = ps()
            pu, Rpu = ps()
            for c in range(8):
                MM(pg[:, :], wgu[:, c, j * 128:(j + 1) * 128], hT[:, c, :], c == 0, c == 7, R=[RB[0 if j < JS else 1], R_h[c]], W=[Rpg], signal=(c == 7))
            for c in range(8):
                MM(pu[:, :], wgu[:, c, HF + j * 128:HF + (j + 1) * 128], hT[:, c, :], c == 0, c == 7, R=[RB[0 if j < JS else 1], R_h[c]], W=[Rpu], signal=(c == 7))
            return (j, pg, Rpg, pu, Rpu)

        def gu_fin(state):
            j, pg, Rpg, pu, Rpu = state
            s_, Rs = tmp()
            ACT(s_[:, :], pg[:, :], AF.Silu, R=[Rpg], W=[Rs])
            TT("dve", actT[:, j, :], s_[:, :], pu[:, :], ALU.mult, R=[Rs, Rpu], W=[R_a[j]])

        def h_ready(t):
            bi = t % 2
            if half == 0:
                norm_finish(l, bi, V_NFFN, sq, R_sq, tmp)
                store_h(t)
            else:
                load_h(t)

        if not prepped:
            load_x(0, 0)
            if half == 0:
                norm_squares(0, sq, R_sq)
            h_ready(0)
        for t in range(NT):
            bi = t % 2
            last_tile = (t == NT - 1)
            if not last_tile:
                load_x(t + 1, (t + 1) % 2)
            elif nextp is not None:
                nextp.load()
            prev = None
            for j in range(NJ):
                cur = gu_mm(j)
                if prev is not None:
                    gu_fin(prev)
                prev = cur
                if j == JS - 1 and last_tile and next_load is not None:
                    next_load("B0")
            gu_fin(prev)
            if last_tile and next_load is not None:
                next_load("B1")
            if not last_tile and half == 0:
                norm_squares((t + 1) % 2, sq, R_sq)
            if not last_tile and half == 1:
                h_ready(t + 1)
            if last_tile and nextp is not None:
                nextp.squares(sq, R_sq)
            for m in range(8):
                residual_chunk(bi, m, wd, RA, NJ, actT, R_a)
                if m == 0 and not last_tile and half == 0:
                    h_ready(t + 1)
                elif m == 0 and last_tile and nextp is not None:
                    nextp.finish(sq, R_sq, tmp)
            if last:
                transpose_out(t, bi)
            else:
                store_x(t, bi)
        if next_load is not None:
            next_load("A")

    phases = []
    for l in range(DEPTH):
        phases += [("M", l), ("X", l), ("FA", l), ("FB", l)]
    if stop_after is not None:
        phases = phases[:stop_after]
    load_region(phases[0][0], phases[0][1], "A")
    load_region(phases[0][0], phases[0][1], "B0")
    load_region(phases[0][0], phases[0][1], "B1")
    for i, (kind, l) in enumerate(phases):
        nxt = None
        if i + 1 < len(phases):
            nk, nl = phases[i + 1]
            nxt = (lambda region, nk=nk, nl=nl: load_region(nk, nl, region))
        nextp = NextPrep(phases[i + 1][0], phases[i + 1][1]) if i + 1 < len(phases) else None
        prepped = (i > 0)
        if kind == "M":
            phase_M(l, nxt, prepped, nextp)
        elif kind == "X":
            phase_X(l, nxt, prepped, nextp)
        elif kind == "FA":
            phase_F(l, 0, False, nxt, prepped, nextp)
        else:
            phase_F(l, 1, (l == DEPTH - 1 and stop_after is None), nxt, prepped, nextp)
        compute_barrier()
    if stop_after is not None:
        dbg = nc.dram_tensor("dbg", [D, SEQ], F32, kind="ExternalOutput").ap()
        for t in range(NT):
            S.dma("sp", xbuf[0][:, :, :], xs_v[:, :, t * T:(t + 1) * T], reads=[R_xs[t]], writes=R_x[0])
            S.dma("sp", dbg.rearrange("(c p) s -> p c s", p=128)[:, :, t * T:(t + 1) * T], xbuf[0][:, :, :], reads=R_x[0])
    S.barrier()
    S.replay()
    nc._dump_names = dumps
    return nc


_CACHE = {}


def _get_nc(stop_after=None, dump_tile=None):
    key = (stop_after, dump_tile)
    if key not in _CACHE:
        _CACHE[key] = build(stop_after=stop_after, dump_tile=dump_tile)
    return _CACHE[key]


def kernel(x, mem, norm_mix_g, w_in, q_norm_g, k_norm_g, sinks, conv_w, conv_b, conv_ln_g, conv_ln_b, w_out,
           norm_x_g, norm_mem_g, wq_x, wkv_x, xq_norm_g, xk_norm_g, wo_x, norm_ffn_g, w_gate_up, w_down,
           _stop_after=None, _dump_tile=None):
    inp = dict(norm_mix_g=norm_mix_g, q_norm_g=q_norm_g, k_norm_g=k_norm_g, sinks=sinks, conv_w=conv_w, conv_b=conv_b,
               conv_ln_g=conv_ln_g, conv_ln_b=conv_ln_b, norm_x_g=norm_x_g, norm_mem_g=norm_mem_g,
               xq_norm_g=xq_norm_g, xk_norm_g=xk_norm_g, norm_ffn_g=norm_ffn_g)
    inp = {k: np.asarray(v, np.float32) for k, v in inp.items()}
    vecs = _host_vecs(inp)
    cb, cf = _host_consts()
    f = lambda a: np.ascontiguousarray(np.asarray(a, np.float32))
    shared = {"w_in": f(w_in), "w_out": f(w_out), "wq_x": f(wq_x), "wkv_x": f(wkv_x), "wo_x": f(wo_x),
              "w_gate_up": f(w_gate_up), "w_down": f(w_down), "vecs": vecs, "cbf": cb, "cf32": cf}
    x = np.asarray(x, np.float32)
    mem = np.asarray(mem, np.float32)
    B = x.shape[0]
    in_maps = []
    for b in range(B):
        m = dict(shared)
        m["x"] = np.ascontiguousarray(x[b])
        m["mem"] = np.ascontiguousarray(mem[b])
        in_maps.append(m)
    nc = _get_nc(_stop_after, _dump_tile)
    res = run_bass_kernel_spmd(nc, in_maps, core_ids=list(range(B)))
    if _dump_tile is not None:
        kernel.last_dumps = {n: np.asarray(res.results[0]["dump_" + n]) for n in nc._dump_names}
    if _stop_after is not None:
        return np.stack([np.asarray(r["dbg"]).T for r in res.results], axis=0)
    return np.stack([np.asarray(r["y"]) for r in res.results], axis=0).astype(np.float32)
```
